# Optimizing a Trainium2 kernel written in Bass

```python
import math
import jax, jax.numpy as jnp
from jax import lax
import numpy as np

D_MODEL = 1024
BATCH = 1
SEQ = 16384
DEPTH = 2

HEAD_DIM = 64
ROT_DIM = HEAD_DIM // 4
ROPE_THETA = 500000.0
BLK = 128
DSW_HEADS = 8
DSW_WIDTH = DSW_HEADS * HEAD_DIM
DSW_PATTERNS = ((128, 1), (512, 4), (2048, 16))
DIFF_HEADS = 4
DIFF_QK_WIDTH = DIFF_HEADS * 2 * HEAD_DIM
DIFF_V_DIM = 2 * HEAD_DIM
DIFF_WIDTH = DIFF_HEADS * DIFF_V_DIM
ATT_SPLITS = (DSW_WIDTH, DSW_WIDTH, DSW_WIDTH, DSW_WIDTH,
              DIFF_QK_WIDTH, DIFF_QK_WIDTH, DIFF_WIDTH, DIFF_WIDTH)
ATT_IN = sum(ATT_SPLITS)
ATT_OUT_IN = DSW_WIDTH + DIFF_WIDTH
SGU_WIDTH = 2 * D_MODEL
SGU_GROUPS = 8
SGU_CHUNK = 128
SGU_IN = 3 * SGU_WIDTH
N_EVEN = (DEPTH + 1) // 2
N_ODD = DEPTH // 2
RMS_EPS = 1e-6
LN_EPS = 1e-5
NEG_INF = -1e30

kernel_name = "hybrid_dilated_diff_sgu_trunk"


def rms_norm(x, g, eps=RMS_EPS):
    xf = x.astype(jnp.float32)
    y = xf * lax.rsqrt(jnp.mean(xf * xf, axis=-1, keepdims=True) + eps)
    return (y * g.astype(jnp.float32)).astype(x.dtype)


def layer_norm(x, g, b, eps=LN_EPS):
    xf = x.astype(jnp.float32)
    mu = jnp.mean(xf, axis=-1, keepdims=True)
    xc = xf - mu
    y = xc * lax.rsqrt(jnp.mean(xc * xc, axis=-1, keepdims=True) + eps)
    return (y * g.astype(jnp.float32) + b.astype(jnp.float32)).astype(x.dtype)


def rope_tables(seq_len):
    pos = jnp.arange(seq_len, dtype=jnp.float32)
    inv = 1.0 / (ROPE_THETA ** (jnp.arange(0, ROT_DIM, 2, dtype=jnp.float32) / ROT_DIM))
    ang = pos[:, None] * inv[None, :]
    return jnp.cos(ang), jnp.sin(ang)


def partial_rope(x, cos, sin):
    half = ROT_DIM // 2
    shp = (cos.shape[0],) + (1,) * (x.ndim - 3) + (half,)
    c, s = cos.reshape(shp), sin.reshape(shp)
    xf = x.astype(jnp.float32)
    x1, x2 = xf[..., :half], xf[..., half:ROT_DIM]
    out = jnp.concatenate([x1 * c - x2 * s, x2 * c + x1 * s, xf[..., ROT_DIM:]], axis=-1)
    return out.astype(x.dtype)


def dilated_branch(q, k, v, dil, wsub):
    b, s, h, d = q.shape
    unit = dil * BLK
    s_pad = -(-s // unit) * unit
    n_sub = s_pad // dil
    nb = n_sub // BLK

    def to_sub(t):
        t = jnp.pad(t, ((0, 0), (0, s_pad - s), (0, 0), (0, 0)))
        return t.reshape(b, n_sub, dil, h, d).transpose(0, 2, 3, 1, 4).reshape(b, dil, h, nb, BLK, d)

    def with_prev(t):
        prev = jnp.pad(t[:, :, :, :-1], ((0, 0), (0, 0), (0, 0), (1, 0), (0, 0), (0, 0)))
        return jnp.concatenate([prev, t], axis=4)

    qs = to_sub(q)
    kk = with_prev(to_sub(k))
    vv = with_prev(to_sub(v))
    sc = jnp.einsum('brhnqd,brhnkd->brhnqk', qs, kk,
                    preferred_element_type=jnp.float32) * (d ** -0.5)
    qi = jnp.arange(BLK)[:, None] + BLK
    kj = jnp.arange(2 * BLK)[None, :]
    dist = qi - kj
    band = (dist >= 0) & (dist <= wsub)
    has_prev = (jnp.arange(nb)[:, None, None] > 0) | (kj[None] >= BLK)
    mask = band[None] & has_prev
    sc = jnp.where(mask, sc, NEG_INF)
    m = jnp.max(sc, axis=-1, keepdims=True)
    p = jnp.exp(sc - m)
    l = jnp.sum(p, axis=-1, keepdims=True)
    o = jnp.einsum('brhnqk,brhnkd->brhnqd', p.astype(v.dtype), vv,
                   preferred_element_type=jnp.float32) / l
    lse = (m + jnp.log(l))[..., 0]
    o = o.reshape(b, dil, h, n_sub, d).transpose(0, 3, 1, 2, 4).reshape(b, s_pad, h, d)[:, :s]
    lse = lse.reshape(b, dil, h, n_sub).transpose(0, 3, 1, 2).reshape(b, s_pad, h)[:, :s]
    return o, lse


def dsw_attention(q, k, v):
    outs, lses = [], []
    for window, dil in DSW_PATTERNS:
        o, lse = dilated_branch(q, k, v, dil, window // dil)
        outs.append(o)
        lses.append(lse)
    wts = jax.nn.softmax(jnp.stack(lses, axis=0), axis=0)
    return jnp.sum(wts[..., None] * jnp.stack(outs, axis=0), axis=0)


def diff_attention(q, k, v, lam):
    b, s, h, _, d = q.shape
    nblk = s // BLK
    qb = q.reshape(b, nblk, BLK, h, 2, d).transpose(1, 0, 4, 3, 2, 5)
    kt = k.transpose(0, 3, 2, 1, 4)
    vt = v.transpose(0, 2, 1, 3)
    kpos = jnp.arange(s)
    scale = d ** -0.5

    def one_block(args):
        qblk, i = args
        sc = jnp.einsum('bchqd,bchkd->bchqk', qblk, kt,
                        preferred_element_type=jnp.float32) * scale
        qpos = i * BLK + jnp.arange(BLK)
        sc = jnp.where(kpos[None, :] <= qpos[:, None], sc, NEG_INF)
        p = jax.nn.softmax(sc, axis=-1)
        a = p[:, 0] - lam * p[:, 1]
        return jnp.einsum('bhqk,bhkd->bhqd', a.astype(vt.dtype), vt,
                          preferred_element_type=jnp.float32)

    o = lax.map(one_block, (qb, jnp.arange(nblk)))
    return o.transpose(1, 0, 3, 2, 4).reshape(b, s, h, 2 * d)


def hybrid_attention_layer(x, norm_g, w_in, dsw_qg, dsw_kg, diff_qg, diff_kg,
                           lq1, lk1, lq2, lk2, subln_g, w_out, cos, sin, lam_init):
    b, s, _ = x.shape
    h = rms_norm(x, norm_g)
    proj = jnp.einsum('bsd,de->bse', h, w_in)
    offs = [int(o) for o in np.cumsum(ATT_SPLITS)[:-1]]
    qa, ka, va, ga, qd, kd, vd, gd = jnp.split(proj, offs, axis=-1)
    qa = partial_rope(rms_norm(qa.reshape(b, s, DSW_HEADS, HEAD_DIM), dsw_qg), cos, sin)
    ka = partial_rope(rms_norm(ka.reshape(b, s, DSW_HEADS, HEAD_DIM), dsw_kg), cos, sin)
    va = va.reshape(b, s, DSW_HEADS, HEAD_DIM)
    oa = dsw_attention(qa, ka, va).reshape(b, s, DSW_WIDTH)
    qd = partial_rope(rms_norm(qd.reshape(b, s, DIFF_HEADS, 2, HEAD_DIM), diff_qg), cos, sin)
    kd = partial_rope(rms_norm(kd.reshape(b, s, DIFF_HEADS, 2, HEAD_DIM), diff_kg), cos, sin)
    vd = vd.reshape(b, s, DIFF_HEADS, DIFF_V_DIM)
    f32 = jnp.float32
    lam = (jnp.exp(jnp.sum(lq1.astype(f32) * lk1.astype(f32)))
           - jnp.exp(jnp.sum(lq2.astype(f32) * lk2.astype(f32))) + lam_init)
    od = diff_attention(qd, kd, vd, lam)
    od = (rms_norm(od, subln_g) * (1.0 - lam_init)).reshape(b, s, DIFF_WIDTH)
    y = jnp.concatenate([oa.astype(x.dtype) * jax.nn.silu(ga),
                         od.astype(x.dtype) * jax.nn.silu(gd)], axis=-1)
    return x + jnp.einsum('bse,ed->bsd', y, w_out)


def sgu_layer(x, norm_g, w_in, ln_g, ln_b, w_s, b_s, w_out):
    b, s, _ = x.shape
    h = rms_norm(x, norm_g)
    proj = jnp.einsum('bsd,de->bse', h, w_in)
    u, v, g = jnp.split(proj, [SGU_WIDTH, 2 * SGU_WIDTH], axis=-1)
    u = jax.nn.gelu(u, approximate=False)
    v = layer_norm(jax.nn.gelu(v, approximate=False), ln_g, ln_b)
    nc = s // SGU_CHUNK
    vg = v.reshape(b, nc, SGU_CHUNK, SGU_GROUPS, SGU_WIDTH // SGU_GROUPS)
    causal = jnp.tril(jnp.ones((SGU_CHUNK, SGU_CHUNK), dtype=bool))
    ws = jnp.where(causal[None], w_s, 0)
    sp = jnp.einsum('gts,bnsgc->bntgc', ws, vg) + b_s.T[None, None, :, :, None]
    sp = sp.reshape(b, s, SGU_WIDTH)
    y = u * sp * jax.nn.silu(g)
    return x + jnp.einsum('bse,ed->bsd', y, w_out)


def setup_inputs(seed: int = 0) -> dict:
    key = jax.random.key(seed)
    ks = jax.random.split(key, 20)
    nrm = jax.random.normal
    f32 = jnp.float32
    D = D_MODEL
    return {
        "x": nrm(ks[0], (BATCH, SEQ, D), f32),
        "att_norm": 1.0 + 0.02 * nrm(ks[1], (N_EVEN, D), f32),
        "att_w_in": nrm(ks[2], (N_EVEN, D, ATT_IN), f32) * D ** -0.5,
        "dsw_q_norm": 1.0 + 0.02 * nrm(ks[3], (N_EVEN, HEAD_DIM), f32),
        "dsw_k_norm": 1.0 + 0.02 * nrm(ks[4], (N_EVEN, HEAD_DIM), f32),
        "diff_q_norm": 1.0 + 0.02 * nrm(ks[5], (N_EVEN, HEAD_DIM), f32),
        "diff_k_norm": 1.0 + 0.02 * nrm(ks[6], (N_EVEN, HEAD_DIM), f32),
        "diff_lam_q1": 0.1 * nrm(ks[7], (N_EVEN, HEAD_DIM), f32),
        "diff_lam_k1": 0.1 * nrm(ks[8], (N_EVEN, HEAD_DIM), f32),
        "diff_lam_q2": 0.1 * nrm(ks[9], (N_EVEN, HEAD_DIM), f32),
        "diff_lam_k2": 0.1 * nrm(ks[10], (N_EVEN, HEAD_DIM), f32),
        "diff_subln": 1.0 + 0.02 * nrm(ks[11], (N_EVEN, DIFF_V_DIM), f32),
        "att_w_out": nrm(ks[12], (N_EVEN, ATT_OUT_IN, D), f32) * ATT_OUT_IN ** -0.5,
        "sgu_norm": 1.0 + 0.02 * nrm(ks[13], (N_ODD, D), f32),
        "sgu_w_in": nrm(ks[14], (N_ODD, D, SGU_IN), f32) * D ** -0.5,
        "sgu_ln_g": 1.0 + 0.02 * nrm(ks[15], (N_ODD, SGU_WIDTH), f32),
        "sgu_ln_b": 0.02 * nrm(ks[16], (N_ODD, SGU_WIDTH), f32),
        "sgu_w_s": nrm(ks[17], (N_ODD, SGU_GROUPS, SGU_CHUNK, SGU_CHUNK), f32) * SGU_CHUNK ** -0.5,
        "sgu_b_s": 1.0 + 0.1 * nrm(ks[18], (N_ODD, SGU_GROUPS, SGU_CHUNK), f32),
        "sgu_w_out": nrm(ks[19], (N_ODD, SGU_WIDTH, D), f32) * SGU_WIDTH ** -0.5,
    }


def reference(x, att_norm, att_w_in, dsw_q_norm, dsw_k_norm, diff_q_norm, diff_k_norm,
              diff_lam_q1, diff_lam_k1, diff_lam_q2, diff_lam_k2, diff_subln, att_w_out,
              sgu_norm, sgu_w_in, sgu_ln_g, sgu_ln_b, sgu_w_s, sgu_b_s, sgu_w_out):
    cos, sin = rope_tables(x.shape[1])
    for i in range(DEPTH):
        j = i // 2
        if i % 2 == 0:
            lam_init = 0.8 - 0.6 * math.exp(-0.3 * i)
            x = hybrid_attention_layer(
                x, att_norm[j], att_w_in[j], dsw_q_norm[j], dsw_k_norm[j],
                diff_q_norm[j], diff_k_norm[j], diff_lam_q1[j], diff_lam_k1[j],
                diff_lam_q2[j], diff_lam_k2[j], diff_subln[j], att_w_out[j],
                cos, sin, lam_init)
        else:
            x = sgu_layer(x, sgu_norm[j], sgu_w_in[j], sgu_ln_g[j], sgu_ln_b[j],
                          sgu_w_s[j], sgu_b_s[j], sgu_w_out[j])
    return x
```

```python
import math
from contextlib import ExitStack

import ml_dtypes
import numpy as np

import concourse.bass as bass
import concourse.mybir as mybir
from concourse.bass_utils import run_bass_kernel_spmd

F32 = mybir.dt.float32
BF16 = mybir.dt.bfloat16
AF = mybir.ActivationFunctionType
ALU = mybir.AluOpType

S = 16384
D = 1024
TT = 512
NT = S // TT
NCORE = 8
NSLOT = NT // NCORE
NHALO = 4
NEG = -30000.0
RMS_EPS = 1e-6
LN_EPS = 1e-5
LAM_INIT0 = 0.8 - 0.6 * math.exp(-0.3 * 0)

C_ATTN = 0
C_SGUN = 8
C_DSWQ = 16
C_DSWK = 17
C_DIFQ = 18
C_DIFK = 19
C_SUBLN = 20
C_CB = 21
C_HB = 29
C_ZERO = 45
C_LQ1 = 46
C_LK1 = 110
C_LQ2 = 174
C_LK2 = 238
C_EPSR = 302
C_EPSL = 303
C_ONE = 304
NCST = 305
M_ONES = 0
M_BONES = 128
M_ROT = 256
M_TRI = 384
M_DSW = 512
NCM = 512 + 23 * 128


class Sched:
    def __init__(self, nc, es):
        self.nc = nc
        self.es = es
        self.E = {"pe": nc.tensor, "act": nc.scalar, "dve": nc.vector, "pool": nc.gpsimd, "sp": nc.sync}
        self.semh = {}
        self.cnt = {}
        for e in self.E:
            self.semh[e] = es.enter_context(nc.semaphore("s_" + e))
            self.cnt[e] = 0
        self.known = {e: {} for e in self.E}
        self.lastw = {}
        self.readers = {}
        self.nwait = 0

    def dsem(self, name):
        k = "d_" + name
        if k not in self.semh:
            self.semh[k] = self.es.enter_context(self.nc.semaphore(k))
            self.cnt[k] = 0
        return k

    def _wait(self, eng, deps):
        need = {}
        for (sk, val) in deps:
            if sk == eng and eng in ("pe", "sp"):
                continue
            if val > need.get(sk, 0):
                need[sk] = val
        for sk, val in need.items():
            if self.known[eng].get(sk, 0) >= val:
                continue
            self.E[eng].wait_ge(self.semh[sk], val)
            self.known[eng][sk] = val
            self.nwait += 1

    def _deps(self, reads, writes):
        deps = []
        for k in reads:
            if k in self.lastw:
                deps.append(self.lastw[k])
        for k in writes:
            if k in self.lastw:
                deps.append(self.lastw[k])
            deps.extend(self.readers.get(k, {}).items())
        return deps

    def _record(self, ev, reads, writes):
        for k in reads:
            r = self.readers.setdefault(k, {})
            if ev[1] > r.get(ev[0], 0):
                r[ev[0]] = ev[1]
        for k in writes:
            self.lastw[k] = ev
            self.readers[k] = {}

    def op(self, eng, fn, reads=(), writes=()):
        self._wait(eng, self._deps(reads, writes))
        ins = fn()
        self.cnt[eng] += 1
        ins.then_inc(self.semh[eng], 1)
        self._record((eng, self.cnt[eng]), reads, writes)

    def dma(self, q, sem, out, in_, reads=(), writes=()):
        sk = self.dsem(sem)
        self._wait(q, self._deps(reads, writes))
        ins = self.E[q].dma_start(out=out, in_=in_)
        self.cnt[sk] += 16
        ins.then_inc(self.semh[sk], 16)
        self._record((sk, self.cnt[sk]), reads, writes)

    def barrier(self):
        for e in self.E:
            for sk, c in self.cnt.items():
                if sk == e and e in ("pe", "sp"):
                    continue
                if c > 0 and self.known[e].get(sk, 0) < c:
                    self.E[e].wait_ge(self.semh[sk], c)
                    self.known[e][sk] = c
        self.lastw = {}
        self.readers = {}


def build_program(stop=None):
    nc = bass.Bass("TRN2", target_bir_lowering=False)
    x_all = nc.dram_tensor("x_all", [D, S], F32, kind="ExternalInput").ap()
    x_halo = nc.dram_tensor("x_halo", [D, NSLOT * NHALO * TT], F32, kind="ExternalInput").ap()
    rope_all = nc.dram_tensor("rope_all", [2, 128, S], F32, kind="ExternalInput").ap()
    rope_halo = nc.dram_tensor("rope_halo", [2, 128, NSLOT * NHALO * TT], F32, kind="ExternalInput").ap()
    w_in0 = nc.dram_tensor("w_in0", [D, 4096], F32, kind="ExternalInput").ap()
    w_out0 = nc.dram_tensor("w_out0", [D, D], F32, kind="ExternalInput").ap()
    w_in1 = nc.dram_tensor("w_in1", [D, 6144], F32, kind="ExternalInput").ap()
    w_out1 = nc.dram_tensor("w_out1", [2048, D], F32, kind="ExternalInput").ap()
    cst_d = nc.dram_tensor("cst", [128, NCST], F32, kind="ExternalInput").ap()
    cmat_d = nc.dram_tensor("cmat", [128, NCM], BF16, kind="ExternalInput").ap()
    lngb_d = nc.dram_tensor("lngb", [2, 2048], F32, kind="ExternalInput").ap()
    wsT_d = nc.dram_tensor("wsT", [128, 8 * 128], F32, kind="ExternalInput").ap()
    bsb_d = nc.dram_tensor("bsb", [1, 8 * 128], F32, kind="ExternalInput").ap()
    out_d = nc.dram_tensor("outT", [D, NSLOT * TT], F32, kind="ExternalOutput").ap()
    W0b = nc.dram_tensor("W0b", [D, 4096], BF16).ap()
    Wo0b = nc.dram_tensor("Wo0b", [D, D], BF16).ap()
    W1b = nc.dram_tensor("W1b", [D, 6144], BF16).ap()
    Wo1b = nc.dram_tensor("Wo1b", [2048, D], BF16).ap()
    KD = nc.dram_tensor("KD", [4, 128, S], BF16).ap()
    VD = nc.dram_tensor("VD", [S, 512], BF16).ap()
    NKA = NSLOT * (NHALO + 1) * TT
    KA = nc.dram_tensor("KA", [4, 128, NKA], BF16).ap()
    VA = nc.dram_tensor("VA", [NKA, 512], BF16).ap()
    QA = nc.dram_tensor("QA", [4, 128, NSLOT * TT], BF16).ap()
    QD = nc.dram_tensor("QD", [4, 128, NSLOT * TT], BF16).ap()
    GA = nc.dram_tensor("GA", [8, 64, NSLOT * TT], BF16).ap()
    GD = nc.dram_tensor("GD", [4, 128, NSLOT * TT], BF16).ap()

    with ExitStack() as es:
        es.enter_context(nc.allow_low_precision("bf16 matmul operands, fp32 accumulation"))
        es.enter_context(nc.allow_non_contiguous_dma("strided weight / scratch tiles"))
        sc = Sched(nc, es)

        uid = [0]

        def sb(name, shape, dt, stack=es):
            uid[0] += 1
            return stack.enter_context(nc.sbuf_tensor(f"{name}_{uid[0]}", shape, dt))

        cst = sb("cst_sb", [128, NCST], F32)
        cmat = sb("cmat_sb", [128, NCM], BF16)
        lamc = sb("lamc", [128, 8], F32)
        x1 = sb("x1", [128, 8, TT], F32)
        psum = [es.enter_context(nc.psum_tensor(f"ps{i}", [128, TT], F32)) for i in range(8)]
        ps_rr = [0]

        def ps_next(lo=0, hi=8):
            i = lo + (ps_rr[0] % (hi - lo))
            ps_rr[0] += 1
            return i

        ones = cmat[:, M_ONES:M_ONES + 128]
        bones = cmat[:, M_BONES:M_BONES + 128]
        rotm = cmat[:, M_ROT:M_ROT + 128]
        tri = cmat[:, M_TRI:M_TRI + 128]

        def ccol(c, n=128):
            return cst[0:n, c:c + 1]

        sc.dma("sp", "cst", cst[:], cst_d, writes=["cst"])
        sc.dma("sp", "cmat", cmat[:], cmat_d, writes=["cmat"])
        for (src, dst, rows, nm) in ((w_in0, W0b, D, "W0b"), (w_out0, Wo0b, D, "Wo0b"),
                                     (w_in1, W1b, D, "W1b"), (w_out1, Wo1b, 2048, "Wo1b")):
            step = 128
            for r0 in range(0, rows, step):
                sc.dma("pool", "wcast", dst[r0:r0 + step, :], src[r0:r0 + step, :], writes=[nm])
                done = sc.cnt["d_wcast"] - 32
                if done > 0:
                    nc.gpsimd.wait_ge(sc.semh["d_wcast"], done)
                    sc.known["pool"]["d_wcast"] = done
        with ExitStack() as ls:
            lt = sb("lam_t", [128, 128], F32, ls)
            for i, (ca, cb) in enumerate(((C_LQ1, C_LK1), (C_LQ2, C_LK2))):
                sc.op("dve", lambda ca=ca, cb=cb, i=i: nc.vector.tensor_tensor(
                    out=lt[:, i * 64:(i + 1) * 64], in0=cst[:, ca:ca + 64], in1=cst[:, cb:cb + 64], op=ALU.mult),
                    reads=["cst"], writes=["lt"])
            for i in range(2):
                sc.op("dve", lambda i=i: nc.vector.reduce_sum(
                    out=lamc[:, 2 + i:3 + i], in_=lt[:, i * 64:(i + 1) * 64], axis=mybir.AxisListType.X),
                    reads=["lt"], writes=["lamc"])
            sc.op("act", lambda: nc.scalar.activation(out=lamc[:, 4:6], in_=lamc[:, 2:4], func=AF.Exp),
                  reads=["lamc"], writes=["lamc"])
            sc.op("dve", lambda: nc.vector.scalar_tensor_tensor(
                out=lamc[:, 0:1], in0=lamc[:, 5:6], scalar=-LAM_INIT0, in1=lamc[:, 4:5],
                op0=ALU.add, op1=ALU.subtract), reads=["lamc"], writes=["lamc"])
            sc.op("dve", lambda: nc.vector.tensor_scalar(
                out=lamc[:, 1:2], in0=cst[:, C_SUBLN:C_SUBLN + 1], scalar1=1.0 - LAM_INIT0, scalar2=None,
                op0=ALU.mult), reads=["cst", "lamc"], writes=["lamc"])
            sc.barrier()

        def rmsnorm_tile(xin, xk, gcol0, sq, sqk, hT, hk, rs, rsk):
            sc.op("act", lambda: nc.scalar.activation(out=sq[:], in_=xin[:], func=AF.Square),
                  reads=[xk], writes=[sqk])
            pi = ps_next()
            for k in range(8):
                sc.op("pe", lambda k=k: nc.tensor.matmul(psum[pi][:], lhsT=ones, rhs=sq[:, k, :],
                                                       start=(k == 0), stop=(k == 7)),
                      reads=[sqk, "cmat"], writes=[f"ps{pi}"])
            sc.op("act", lambda: nc.scalar.activation(out=rs[:], in_=psum[pi][:], func=AF.Ln,
                                                      scale=1.0 / D, bias=ccol(C_EPSR)),
                  reads=[f"ps{pi}", "cst"], writes=[rsk])
            sc.op("act", lambda: nc.scalar.activation(out=rs[:], in_=rs[:], func=AF.Exp, scale=-0.5),
                  reads=[rsk], writes=[rsk])
            for k in range(8):
                sc.op("dve", lambda k=k: nc.vector.scalar_tensor_tensor(
                    out=hT[:, k, :], in0=xin[:, k, :], scalar=cst[:, gcol0 + k:gcol0 + k + 1], in1=rs[:],
                    op0=ALU.mult, op1=ALU.mult), reads=[xk, rsk, "cst"], writes=[hk])

        def proj_fm(pi, hT, hk, w, wk, c0, m=128, prow=128):
            for k in range(8):
                sc.op("pe", lambda k=k: nc.tensor.matmul(psum[pi][0:m, :], lhsT=w[:, k, c0:c0 + m], rhs=hT[:, k, :],
                                                       start=(k == 0), stop=(k == 7)),
                      reads=[hk, wk], writes=[f"ps{pi}"])

        if stop == "setup":
            return nc
        with ExitStack() as p1:
            xin = [sb(f"xin{i}", [128, 8, TT], F32, p1) for i in range(2)]
            sq = sb("sq", [128, 8, TT], BF16, p1)
            hT = sb("hT", [128, 8, TT], BF16, p1)
            rs = sb("rs", [128, TT], F32, p1)
            wkvd = sb("wkvd", [128, 8, 1024], BF16, p1)
            wkva = sb("wkva", [128, 8, 1024], BF16, p1)
            wst = [sb(f"wst{i}", [128, 8, 512], BF16, p1) for i in range(2)]
            rope = [sb(f"rope{i}", [128, 2, TT], F32, p1) for i in range(2)]
            NPB = 3
            sqc = [sb(f"sqc{i}", [128, TT], BF16, p1) for i in range(NPB)]
            qa_ = [sb(f"qA{i}", [128, TT], BF16, p1) for i in range(NPB)]
            rsc = [sb(f"rsc{i}", [128, TT], F32, p1) for i in range(NPB)]
            t1 = [sb(f"t1{i}", [128, TT], F32, p1) for i in range(NPB)]
            t2 = [sb(f"t2{i}", [128, TT], F32, p1) for i in range(NPB)]
            ko = [sb(f"ko{i}", [128, TT], BF16, p1) for i in range(NPB)]
            vst = [sb(f"vst{i}", [128, TT], BF16, p1) for i in range(2)]
            pb = [0]
            vb = [0]
            wsb = [0]

            W0v = W0b.rearrange("(k p) e -> p k e", p=128)
            sc.dma("sp", "wkvd", wkvd[:, :, 0:512], W0v[:, :, 2560:3072], reads=["W0b"], writes=["wkvd"])
            sc.dma("sp", "wkvd", wkvd[:, :, 512:1024], W0v[:, :, 3072:3584], reads=["W0b"], writes=["wkvd"])
            sc.dma("sp", "wkva", wkva[:, :, 0:512], W0v[:, :, 512:1024], reads=["W0b"], writes=["wkva"])
            sc.dma("sp", "wkva", wkva[:, :, 512:1024], W0v[:, :, 1024:1536], reads=["W0b"], writes=["wkva"])

            def qk_post(pi, gcol, ropet, ropek, dst_ap, dkey):
                b = pb[0] % NPB
                pb[0] += 1
                sc.op("act", lambda: nc.scalar.activation(out=sqc[b][:], in_=psum[pi][:], func=AF.Square),
                      reads=[f"ps{pi}"], writes=[f"sqc{b}"])
                sc.op("act", lambda: nc.scalar.activation(out=qa_[b][:], in_=psum[pi][:], func=AF.Identity,
                                                          scale=ccol(gcol)),
                      reads=[f"ps{pi}", "cst"], writes=[f"qA{b}"])
                p2 = ps_next()
                sc.op("pe", lambda: nc.tensor.matmul(psum[p2][:], lhsT=bones, rhs=sqc[b][:], start=True, stop=True),
                      reads=[f"sqc{b}", "cmat"], writes=[f"ps{p2}"])
                p3 = ps_next()
                sc.op("pe", lambda: nc.tensor.matmul(psum[p3][:], lhsT=rotm, rhs=qa_[b][:], start=True, stop=True),
                      reads=[f"qA{b}", "cmat"], writes=[f"ps{p3}"])
                sc.op("act", lambda: nc.scalar.activation(out=rsc[b][:], in_=psum[p2][:], func=AF.Ln,
                                                          scale=1.0 / 64, bias=ccol(C_EPSR)),
                      reads=[f"ps{p2}", "cst"], writes=[f"rsc{b}"])
                sc.op("act", lambda: nc.scalar.activation(out=rsc[b][:], in_=rsc[b][:], func=AF.Exp, scale=-0.5),
                      reads=[f"rsc{b}"], writes=[f"rsc{b}"])
                sc.op("pool", lambda: nc.gpsimd.tensor_tensor(out=t1[b][:], in0=qa_[b][:], in1=ropet[:, 0, :],
                                                              op=ALU.mult),
                      reads=[f"qA{b}", ropek], writes=[f"t1{b}"])
                sc.op("dve", lambda: nc.vector.tensor_tensor(out=t2[b][:], in0=psum[p3][:], in1=ropet[:, 1, :],
                                                             op=ALU.mult),
                      reads=[f"ps{p3}", ropek], writes=[f"t2{b}"])
                sc.op("pool", lambda: nc.gpsimd.tensor_tensor(out=t1[b][:], in0=t1[b][:], in1=t2[b][:], op=ALU.add),
                      reads=[f"t1{b}", f"t2{b}"], writes=[f"t1{b}"])
                sc.op("dve", lambda: nc.vector.tensor_tensor(out=ko[b][:], in0=t1[b][:], in1=rsc[b][:], op=ALU.mult),
                      reads=[f"t1{b}", f"rsc{b}"], writes=[f"ko{b}"])
                sc.dma("sp", f"ko{b}", dst_ap, ko[b][:], reads=[f"ko{b}"], writes=[dkey])

            def v_tm(w, wk, c0, dst_rows_fn, dkey):
                for sub in range(4):
                    pi = ps_next()
                    for k in range(8):
                        sc.op("pe", lambda k=k: nc.tensor.matmul(
                            psum[pi][:], lhsT=hT[:, k, sub * 128:(sub + 1) * 128], rhs=w[:, k, c0:c0 + 512],
                            start=(k == 0), stop=(k == 7)), reads=["hT", wk], writes=[f"ps{pi}"])
                    b = vb[0] % 2
                    vb[0] += 1
                    sc.op("act", lambda: nc.scalar.activation(out=vst[b][:], in_=psum[pi][:], func=AF.Copy),
                          reads=[f"ps{pi}"], writes=[f"vst{b}"])
                    sc.dma("sp", f"vst{b}", dst_rows_fn(sub), vst[b][:], reads=[f"vst{b}"], writes=[dkey])

            def silu_fm(pi, m, dst_ap, dkey):
                b = pb[0] % NPB
                pb[0] += 1
                sc.op("act", lambda: nc.scalar.activation(out=ko[b][0:m, :], in_=psum[pi][0:m, :], func=AF.Silu),
                      reads=[f"ps{pi}"], writes=[f"ko{b}"])
                sc.dma("sp", f"ko{b}", dst_ap, ko[b][0:m, :], reads=[f"ko{b}"], writes=[dkey])

            def load_wst(c0):
                b = wsb[0] % 2
                wsb[0] += 1
                sc.dma("sp", f"wst{b}", wst[b][:], W0v[:, :, c0:c0 + 512], reads=["W0b"], writes=[f"wst{b}"])
                return wst[b], f"wst{b}"

            tiles = [("all", pos) for pos in range(NT)] + [("halo", i) for i in range(NSLOT * NHALO)]
            if stop == "p1a":
                tiles = tiles[:8]
            xav = x_all.rearrange("(k p) t -> p k t", p=128)
            xhv = x_halo.rearrange("(k p) t -> p k t", p=128)

            def load_tile(ti):
                kind, idx = tiles[ti]
                xb = ti % 2
                srcx = xav if kind == "all" else xhv
                srcr = rope_all if kind == "all" else rope_halo
                sc.dma("sp", f"xin{xb}", xin[xb][:], srcx[:, :, idx * TT:(idx + 1) * TT], writes=[f"xin{xb}"])
                sc.dma("sp", f"rope{xb}", rope[xb][:],
                       srcr.rearrange("c p t -> p c t")[:, :, idx * TT:(idx + 1) * TT], writes=[f"rope{xb}"])

            load_tile(0)
            for ti, (kind, idx) in enumerate(tiles):
                xb = ti % 2
                xk = f"xin{xb}"
                rk = f"rope{xb}"
                if ti + 1 < len(tiles):
                    load_tile(ti + 1)
                rmsnorm_tile(xin[xb], xk, C_ATTN, sq, "sq", hT, "hT", rs, "rs")
                if kind == "all":
                    pos = idx
                    for ch in range(4):
                        pi = ps_next()
                        proj_fm(pi, hT, "hT", wkvd, "wkvd", ch * 128)
                        qk_post(pi, C_DIFK, rope[xb], rk, KD[ch, :, pos * TT:(pos + 1) * TT], "KD")
                    v_tm(wkvd, "wkvd", 512, lambda sub, pos=pos: VD[(pos * 4 + sub) * 128:(pos * 4 + sub + 1) * 128, :],
                         "VD")
                    if pos % 8 == 7:
                        j = pos // 8
                        kpos = j * (NHALO + 1) + NHALO
                        for ch in range(4):
                            pi = ps_next()
                            proj_fm(pi, hT, "hT", wkva, "wkva", ch * 128)
                            qk_post(pi, C_DSWK, rope[xb], rk, KA[ch, :, kpos * TT:(kpos + 1) * TT], "KA")
                        v_tm(wkva, "wkva", 512,
                             lambda sub, kpos=kpos: VA[(kpos * 4 + sub) * 128:(kpos * 4 + sub + 1) * 128, :], "VA")
                        w, wk = load_wst(0)
                        for ch in range(4):
                            pi = ps_next()
                            proj_fm(pi, hT, "hT", w, wk, ch * 128)
                            qk_post(pi, C_DSWQ, rope[xb], rk, QA[ch, :, j * TT:(j + 1) * TT], "QA")
                        w, wk = load_wst(1536)
                        for h in range(8):
                            pi = ps_next()
                            proj_fm(pi, hT, "hT", w, wk, h * 64, m=64)
                            silu_fm(pi, 64, GA[h, :, j * TT:(j + 1) * TT], "GA")
                        w, wk = load_wst(2048)
                        for ch in range(4):
                            pi = ps_next()
                            proj_fm(pi, hT, "hT", w, wk, ch * 128)
                            qk_post(pi, C_DIFQ, rope[xb], rk, QD[ch, :, j * TT:(j + 1) * TT], "QD")
                        w, wk = load_wst(3584)
                        for ch in range(4):
                            pi = ps_next()
                            proj_fm(pi, hT, "hT", w, wk, ch * 128)
                            silu_fm(pi, 128, GD[ch, :, j * TT:(j + 1) * TT], "GD")
                else:
                    j, m = idx // NHALO, idx % NHALO
                    kpos = j * (NHALO + 1) + m
                    for ch in range(4):
                        pi = ps_next()
                        proj_fm(pi, hT, "hT", wkva, "wkva", ch * 128)
                        qk_post(pi, C_DSWK, rope[xb], rk, KA[ch, :, kpos * TT:(kpos + 1) * TT], "KA")
                    v_tm(wkva, "wkva", 512,
                         lambda sub, kpos=kpos: VA[(kpos * 4 + sub) * 128:(kpos * 4 + sub + 1) * 128, :], "VA")
            sc.barrier()

        if stop in ("p1", "p1a"):
            return nc
        Wo0v_a = Wo0b[0:512, :].rearrange("(h d) e -> d h e", d=64)
        Wo0v_d = Wo0b[512:1024, :].rearrange("(h d) e -> d h e", d=128)
        VDv = VD.rearrange("(n p) e -> p n e", p=128)
        VAv = VA.rearrange("(n p) e -> p n e", p=128)
        W1v = W1b.rearrange("(k p) e -> p k e", p=128)
        Wo1v = Wo1b.rearrange("(c p) e -> p c e", p=128)
        xav = x_all.rearrange("(k p) t -> p k t", p=128)
        for j in range(NSLOT):
            own = 8 * j + 7
            with ExitStack() as p2:
                qaT = sb("qaT", [128, 4, TT], BF16, p2)
                qdT = sb("qdT", [128, 4, TT], BF16, p2)
                gaT = sb("gaT", [64, 8, TT], BF16, p2)
                gdT = sb("gdT", [128, 4, TT], BF16, p2)
                yaT = sb("yaT", [64, 8, TT], BF16, p2)
                ydT = sb("ydT", [128, 4, TT], BF16, p2)
                woa = sb("woa", [64, 8, D], BF16, p2)
                wod = sb("wod", [128, 4, D], BF16, p2)
                NKB = 3
                kbuf = [sb(f"kbuf{i}", [128, TT], BF16, p2) for i in range(NKB)]
                vbuf = [sb(f"vbuf{i}", [128, 4, 128], BF16, p2) for i in range(NKB)]
                kab = [sb(f"kab{i}", [128, (NHALO + 1) * TT], BF16, p2) for i in range(2)]
                vab = [sb(f"vab{i}", [128, (NHALO + 1) * 4, 64], BF16, p2) for i in range(2)]
                NPT = 4
                pT = [sb(f"pT{i}", [128, TT], BF16, p2) for i in range(NPT)]
                tf = [sb(f"tf{i}", [128, TT], F32, p2) for i in range(6)]
                sqo = sb("sqo", [128, TT], BF16, p2)
                gg = sb("gg", [128, TT], BF16, p2)
                ptc = [0]

                sc.dma("sp", "qaT", qaT[:], QA.rearrange("c p t -> p c t")[:, :, j * TT:(j + 1) * TT], writes=["qaT"])
                sc.dma("sp", "qdT", qdT[:], QD.rearrange("c p t -> p c t")[:, :, j * TT:(j + 1) * TT], writes=["qdT"])
                sc.dma("sp", "gaT", gaT[:], GA.rearrange("c p t -> p c t")[:, :, j * TT:(j + 1) * TT], writes=["gaT"])
                sc.dma("sp", "gdT", gdT[:], GD.rearrange("c p t -> p c t")[:, :, j * TT:(j + 1) * TT], writes=["gdT"])
                sc.dma("sp", "woa", woa[:], Wo0v_a, writes=["woa"])
                sc.dma("sp", "wod", wod[:], Wo0v_d, writes=["wod"])
                sc.dma("sp", "x1", x1[:], xav[:, :, own * TT:(own + 1) * TT], writes=["x1"])

                nkb = (NHALO + 1) * 4
                for h in range(8):
                    ci, base = h // 2, (h % 2) * 64
                    b2 = h % 2
                    kk, vk = f"kab{b2}", f"vab{b2}"
                    sc.dma("sp", kk, kab[b2][base:base + 64, :],
                           KA[ci, base:base + 64, j * nkb * 128:(j + 1) * nkb * 128], writes=[kk])
                    sc.dma("sp", vk, vab[b2][:], VAv[:, j * nkb:(j + 1) * nkb, h * 64:(h + 1) * 64], writes=[vk])
                    PO, PL = 0, 1
                    for kb in range(nkb):
                        pi = ps_next(4, 8)
                        sc.op("pe", lambda: nc.tensor.matmul(
                            psum[pi][:], lhsT=kab[b2][base:base + 64, kb * 128:(kb + 1) * 128],
                            rhs=qaT[base:base + 64, ci, :], start=True, stop=True),
                            reads=[kk, "qaT"], writes=[f"ps{pi}"])
                        pb_ = ptc[0] % NPT
                        ptc[0] += 1
                        bcol = (C_HB + 4 * j + kb // 4) if kb < NHALO * 4 else C_ZERO
                        sc.op("act", lambda: nc.scalar.activation(out=pT[pb_][:], in_=psum[pi][:], func=AF.Exp,
                                                                  scale=0.125, bias=ccol(bcol)),
                              reads=[f"ps{pi}", "cst"], writes=[f"pT{pb_}"])
                        m0 = M_DSW + (19 - kb) * 128
                        sc.op("pool", lambda: nc.gpsimd.tensor_tensor(out=pT[pb_][:], in0=pT[pb_][:],
                                                                      in1=cmat[:, m0:m0 + 512], op=ALU.mult),
                              reads=[f"pT{pb_}", "cmat"], writes=[f"pT{pb_}"])
                        sc.op("pe", lambda: nc.tensor.matmul(psum[PO][0:64, :], lhsT=vab[b2][:, kb, :], rhs=pT[pb_][:],
                                                             start=(kb == 0), stop=(kb == nkb - 1)),
                              reads=[vk, f"pT{pb_}"], writes=[f"ps{PO}"])
                        sc.op("pe", lambda: nc.tensor.matmul(psum[PL][0:64, :], lhsT=cmat[:, M_ONES:M_ONES + 64],
                                                             rhs=pT[pb_][:], start=(kb == 0), stop=(kb == nkb - 1)),
                              reads=["cmat", f"pT{pb_}"], writes=[f"ps{PL}"])
                    sc.op("act", lambda: nc.scalar.activation(out=tf[0][0:64, :], in_=psum[PL][0:64, :], func=AF.Ln),
                          reads=[f"ps{PL}"], writes=["tf0"])
                    sc.op("act", lambda: nc.scalar.activation(out=tf[0][0:64, :], in_=tf[0][0:64, :], func=AF.Exp,
                                                              scale=-1.0), reads=["tf0"], writes=["tf0"])
                    sc.op("pool", lambda: nc.gpsimd.tensor_tensor(out=tf[0][0:64, :], in0=tf[0][0:64, :],
                                                                  in1=gaT[:, h, :], op=ALU.mult),
                          reads=["tf0", "gaT"], writes=["tf0"])
                    sc.op("dve", lambda: nc.vector.tensor_tensor(out=yaT[:, h, :], in0=psum[PO][0:64, :],
                                                                 in1=tf[0][0:64, :], op=ALU.mult),
                          reads=[f"ps{PO}", "tf0"], writes=["yaT"])

                npos = 8 * j + 8
                for h in range(4):
                    first = True
                    for pos in range(npos):
                        kbi = (h * npos + pos) % NKB
                        kk, vk = f"kbuf{kbi}", f"vbuf{kbi}"
                        sc.dma("sp", kk, kbuf[kbi][:], KD[h, :, pos * TT:(pos + 1) * TT], writes=[kk])
                        sc.dma("sp", vk, vbuf[kbi][:], VDv[:, pos * 4:(pos + 1) * 4, h * 128:(h + 1) * 128], writes=[vk])
                        diag = (pos == own)
                        bcol = (C_CB + (pos - 8 * j)) if pos >= 8 * j else C_ZERO
                        for comp in range(2):
                            base = comp * 64
                            PO, PL = comp, 2 + comp
                            for kb in range(4):
                                q0 = kb * 128 if diag else 0
                                last = (pos == npos - 1 and kb == 3)
                                pi = ps_next(4, 8)
                                sc.op("pe", lambda: nc.tensor.matmul(
                                    psum[pi][:, q0:TT], lhsT=kbuf[kbi][base:base + 64, kb * 128:(kb + 1) * 128],
                                    rhs=qdT[base:base + 64, h, q0:TT], start=True, stop=True),
                                    reads=[kk, "qdT"], writes=[f"ps{pi}"])
                                pb_ = ptc[0] % NPT
                                ptc[0] += 1
                                sc.op("act", lambda: nc.scalar.activation(
                                    out=pT[pb_][:, q0:TT], in_=psum[pi][:, q0:TT], func=AF.Exp, scale=0.125,
                                    bias=ccol(bcol)), reads=[f"ps{pi}", "cst"], writes=[f"pT{pb_}"])
                                if diag:
                                    sc.op("pool", lambda: nc.gpsimd.tensor_tensor(
                                        out=pT[pb_][:, q0:q0 + 128], in0=pT[pb_][:, q0:q0 + 128], in1=tri,
                                        op=ALU.mult), reads=[f"pT{pb_}", "cmat"], writes=[f"pT{pb_}"])
                                st = first and kb == 0
                                sc.op("pe", lambda: nc.tensor.matmul(
                                    psum[PO][:, q0:TT], lhsT=vbuf[kbi][:, kb, :], rhs=pT[pb_][:, q0:TT],
                                    start=st, stop=last), reads=[vk, f"pT{pb_}"], writes=[f"ps{PO}"])
                                sc.op("pe", lambda: nc.tensor.matmul(
                                    psum[PL][:, q0:TT], lhsT=ones, rhs=pT[pb_][:, q0:TT],
                                    start=st, stop=last), reads=["cmat", f"pT{pb_}"], writes=[f"ps{PL}"])
                        first = False
                    for comp in range(2):
                        sc.op("act", lambda comp=comp: nc.scalar.activation(out=tf[comp][:], in_=psum[2 + comp][:],
                                                                            func=AF.Ln),
                              reads=[f"ps{2 + comp}"], writes=[f"tf{comp}"])
                        sc.op("act", lambda comp=comp: nc.scalar.activation(out=tf[comp][:], in_=tf[comp][:],
                                                                            func=AF.Exp, scale=-1.0),
                              reads=[f"tf{comp}"], writes=[f"tf{comp}"])
                        sc.op("dve", lambda comp=comp: nc.vector.tensor_tensor(
                            out=tf[2 + comp][:], in0=psum[comp][:], in1=tf[comp][:], op=ALU.mult),
                            reads=[f"ps{comp}", f"tf{comp}"], writes=[f"tf{2 + comp}"])
                    sc.op("dve", lambda: nc.vector.scalar_tensor_tensor(
                        out=tf[4][:], in0=tf[3][:], scalar=lamc[:, 0:1], in1=tf[2][:], op0=ALU.mult, op1=ALU.add),
                        reads=["tf3", "tf2", "lamc"], writes=["tf4"])
                    sc.op("act", lambda: nc.scalar.activation(out=sqo[:], in_=tf[4][:], func=AF.Square),
                          reads=["tf4"], writes=["sqo"])
                    pi = ps_next(4, 8)
                    sc.op("pe", lambda: nc.tensor.matmul(psum[pi][:], lhsT=ones, rhs=sqo[:], start=True, stop=True),
                          reads=["sqo", "cmat"], writes=[f"ps{pi}"])
                    sc.op("act", lambda: nc.scalar.activation(out=tf[5][:], in_=psum[pi][:], func=AF.Ln,
                                                              scale=1.0 / 128, bias=ccol(C_EPSR)),
                          reads=[f"ps{pi}", "cst"], writes=["tf5"])
                    sc.op("act", lambda: nc.scalar.activation(out=tf[5][:], in_=tf[5][:], func=AF.Exp, scale=-0.5),
                          reads=["tf5"], writes=["tf5"])
                    sc.op("pool", lambda: nc.gpsimd.tensor_tensor(out=tf[5][:], in0=tf[5][:], in1=gdT[:, h, :],
                                                                  op=ALU.mult),
                          reads=["tf5", "gdT"], writes=["tf5"])
                    sc.op("dve", lambda: nc.vector.scalar_tensor_tensor(
                        out=ydT[:, h, :], in0=tf[4][:], scalar=lamc[:, 1:2], in1=tf[5][:],
                        op0=ALU.mult, op1=ALU.mult), reads=["tf4", "tf5", "lamc"], writes=["ydT"])

                for dc in range(8):
                    pi = ps_next(4, 8)
                    for h in range(8):
                        sc.op("pe", lambda h=h: nc.tensor.matmul(
                            psum[pi][:], lhsT=woa[:, h, dc * 128:(dc + 1) * 128], rhs=yaT[:, h, :],
                            start=(h == 0), stop=False), reads=["woa", "yaT"], writes=[f"ps{pi}"])
                    for h in range(4):
                        sc.op("pe", lambda h=h: nc.tensor.matmul(
                            psum[pi][:], lhsT=wod[:, h, dc * 128:(dc + 1) * 128], rhs=ydT[:, h, :],
                            start=False, stop=(h == 3)), reads=["wod", "ydT"], writes=[f"ps{pi}"])
                    sc.op("dve", lambda: nc.vector.tensor_tensor(out=x1[:, dc, :], in0=psum[pi][:], in1=x1[:, dc, :],
                                                                 op=ALU.add),
                          reads=[f"ps{pi}", "x1"], writes=["x1"])
                sc.barrier()
            if stop == "p2":
                return nc

            with ExitStack() as p3:
                sq = sb("sq1", [128, 8, TT], BF16, p3)
                h1 = sb("h1", [128, 8, TT], BF16, p3)
                rs = sb("rs1", [128, TT], F32, p3)
                wst = [sb(f"w1st{i}", [128, 4096], BF16, p3) for i in range(2)]
                uT = sb("uT", [128, 16, TT], BF16, p3)
                gsT = sb("gsT", [128, 16, TT], BF16, p3)
                gv = sb("gv", [128, 4, 2048], BF16, p3)
                vtmp = sb("vtmp", [128, 2048], F32, p3)
                lng = sb("lng", [128, 2, 2048], F32, p3)
                wsf = sb("wsf", [128, 8, 128], BF16, p3)
                bsb = sb("bsb_sb", [128, 8, 128], F32, p3)
                stats = sb("stats", [128, 4, 6], F32, p3)
                mv = sb("mv", [128, 4], F32, p3)
                tf = [sb(f"tg{i}", [128, TT], F32, p3) for i in range(3)]
                wsb = [0]

                def load_w1(src_ap, shape3):
                    b = wsb[0] % 2
                    wsb[0] += 1
                    a, bb = shape3
                    view = wst[b][:, 0:a * bb].rearrange("p (a b) -> p a b", b=bb)
                    sc.dma("sp", f"w1st{b}", view, src_ap, writes=[f"w1st{b}"])
                    return view, f"w1st{b}"

                sc.dma("sp", "lng", lng[:, 0, :], lngb_d[0:1, :].partition_broadcast(128), writes=["lng"])
                sc.dma("sp", "lng", lng[:, 1, :], lngb_d[1:2, :].partition_broadcast(128), writes=["lng"])
                sc.dma("pool", "wsf", wsf[:].rearrange("p g t -> p (g t)"), wsT_d, writes=["wsf"])
                sc.dma("sp", "bsb", bsb[:].rearrange("p g t -> p (g t)"), bsb_d.partition_broadcast(128),
                       writes=["bsb"])
                for g in range(8):
                    sc.op("pool", lambda g=g: nc.gpsimd.tensor_tensor(out=wsf[:, g, :], in0=wsf[:, g, :], in1=tri,
                                                                      op=ALU.mult),
                          reads=["wsf", "cmat"], writes=["wsf"])

                rmsnorm_tile(x1, "x1", C_SGUN, sq, "sq1", h1, "h1", rs, "rs1")
                for wc in range(4):
                    w, wk = load_w1(W1v[:, :, wc * 512:(wc + 1) * 512], (8, 512))
                    for s4 in range(4):
                        pi = ps_next()
                        proj_fm(pi, h1, "h1", w, wk, s4 * 128)
                        c = wc * 4 + s4
                        sc.op("act", lambda c=c: nc.scalar.activation(out=uT[:, c, :], in_=psum[pi][:], func=AF.Gelu),
                              reads=[f"ps{pi}"], writes=["uT"])
                for wc in range(4):
                    w, wk = load_w1(W1v[:, :, 2048 + wc * 512:2048 + (wc + 1) * 512], (8, 512))
                    for sub in range(4):
                        pi = ps_next()
                        for k in range(8):
                            sc.op("pe", lambda k=k: nc.tensor.matmul(
                                psum[pi][:], lhsT=h1[:, k, sub * 128:(sub + 1) * 128], rhs=w[:, k, :],
                                start=(k == 0), stop=(k == 7)), reads=["h1", wk], writes=[f"ps{pi}"])
                        sc.op("act", lambda: nc.scalar.activation(out=gv[:, sub, wc * 512:(wc + 1) * 512],
                                                                  in_=psum[pi][:], func=AF.Gelu),
                              reads=[f"ps{pi}"], writes=[f"gv{sub}"])
                for wc in range(4):
                    w, wk = load_w1(W1v[:, :, 4096 + wc * 512:4096 + (wc + 1) * 512], (8, 512))
                    for s4 in range(4):
                        pi = ps_next()
                        proj_fm(pi, h1, "h1", w, wk, s4 * 128)
                        c = wc * 4 + s4
                        sc.op("act", lambda c=c: nc.scalar.activation(out=gsT[:, c, :], in_=psum[pi][:], func=AF.Silu),
                              reads=[f"ps{pi}"], writes=["gsT"])
                for sub in range(4):
                    for q in range(4):
                        sc.op("dve", lambda q=q: nc.vector.bn_stats(out=stats[:, q, :],
                                                                    in_=gv[:, sub, q * 512:(q + 1) * 512]),
                              reads=[f"gv{sub}"], writes=["stats"])
                    sc.op("dve", lambda: nc.vector.bn_aggr(out=mv[:, 0:2], in_=stats[:].rearrange("p a b -> p (a b)")),
                          reads=["stats"], writes=["mv"])
                    sc.op("act", lambda: nc.scalar.activation(out=mv[:, 2:3], in_=mv[:, 1:2], func=AF.Ln,
                                                              bias=ccol(C_EPSL)),
                          reads=["mv", "cst"], writes=["mv"])
                    sc.op("act", lambda: nc.scalar.activation(out=mv[:, 3:4], in_=mv[:, 2:3], func=AF.Exp, scale=-0.5),
                          reads=["mv"], writes=["mv"])
                    sc.op("dve", lambda: nc.vector.tensor_scalar(
                        out=vtmp[:], in0=gv[:, sub, :], scalar1=mv[:, 0:1], scalar2=mv[:, 3:4],
                        op0=ALU.subtract, op1=ALU.mult), reads=[f"gv{sub}", "mv"], writes=["vtmp"])
                    sc.op("pool", lambda: nc.gpsimd.tensor_tensor(out=vtmp[:], in0=vtmp[:], in1=lng[:, 0, :],
                                                                  op=ALU.mult),
                          reads=["vtmp", "lng"], writes=["vtmp"])
                    sc.op("pool", lambda: nc.gpsimd.tensor_tensor(out=gv[:, sub, :], in0=vtmp[:], in1=lng[:, 1, :],
                                                                  op=ALU.add),
                          reads=["vtmp", "lng"], writes=[f"gv{sub}"])
                for c in range(16):
                    g = c // 2
                    pi = ps_next()
                    for sub in range(4):
                        sc.op("pe", lambda sub=sub: nc.tensor.matmul(
                            psum[pi][:, sub * 128:(sub + 1) * 128], lhsT=gv[:, sub, c * 128:(c + 1) * 128],
                            rhs=wsf[:, g, :], start=True, stop=True),
                            reads=[f"gv{sub}", "wsf"], writes=[f"ps{pi}"])
                    tb = c % 3
                    for sub in range(4):
                        sc.op("dve", lambda sub=sub: nc.vector.tensor_tensor(
                            out=tf[tb][:, sub * 128:(sub + 1) * 128], in0=psum[pi][:, sub * 128:(sub + 1) * 128],
                            in1=bsb[:, g, :], op=ALU.add), reads=[f"ps{pi}", "bsb"], writes=[f"tg{tb}"])
                    sc.op("pool", lambda: nc.gpsimd.tensor_tensor(out=gsT[:, c, :], in0=gsT[:, c, :], in1=uT[:, c, :],
                                                                  op=ALU.mult),
                          reads=["gsT", "uT"], writes=["gsT"])
                    sc.op("dve", lambda: nc.vector.tensor_tensor(out=uT[:, c, :], in0=tf[tb][:], in1=gsT[:, c, :],
                                                                 op=ALU.mult),
                          reads=[f"tg{tb}", "gsT"], writes=["uT"])
                for q4 in range(4):
                    w, wk = load_w1(Wo1v[:, :, q4 * 256:(q4 + 1) * 256], (16, 256))
                    for d2 in range(2):
                        dc = q4 * 2 + d2
                        pi = ps_next()
                        for c in range(16):
                            sc.op("pe", lambda c=c: nc.tensor.matmul(
                                psum[pi][:], lhsT=w[:, c, d2 * 128:(d2 + 1) * 128], rhs=uT[:, c, :],
                                start=(c == 0), stop=(c == 15)), reads=[wk, "uT"], writes=[f"ps{pi}"])
                        tb = dc % 3
                        sc.op("dve", lambda: nc.vector.tensor_tensor(out=tf[tb][:], in0=psum[pi][:], in1=x1[:, dc, :],
                                                                     op=ALU.add),
                              reads=[f"ps{pi}", "x1"], writes=[f"tg{tb}"])
                        sc.dma("sp", f"tg{tb}", out_d[dc * 128:(dc + 1) * 128, j * TT:(j + 1) * TT], tf[tb][:],
                               reads=[f"tg{tb}"], writes=["out"])
                sc.barrier()
            if stop == "p3":
                return nc
    return nc


def _tile_order(c):
    order = []
    for j in range(NSLOT):
        order += [8 * j + i for i in range(c + 1, 8)] + [8 * j + i for i in range(0, c)] + [8 * j + c]
    return order


def _rope_tables(positions):
    rot = 16
    inv = 1.0 / (500000.0 ** (np.arange(0, rot, 2, dtype=np.float32) / rot))
    ang = positions.astype(np.float32)[:, None] * inv[None, :].astype(np.float32)
    cos, sin = np.cos(ang).astype(np.float32), np.sin(ang).astype(np.float32)
    T = positions.shape[0]
    cf = np.ones((128, T), np.float32)
    sf = np.zeros((128, T), np.float32)
    for hb in (0, 64):
        cf[hb:hb + 8] = cos.T
        cf[hb + 8:hb + 16] = cos.T
        sf[hb:hb + 8] = sin.T
        sf[hb + 8:hb + 16] = sin.T
    return np.stack([cf, sf], 0)


def _const_mats():
    cm = np.zeros((128, NCM), np.float32)
    cm[:, M_ONES:M_ONES + 128] = 1.0
    cm[0:64, M_BONES:M_BONES + 64] = 1.0
    cm[64:128, M_BONES + 64:M_BONES + 128] = 1.0
    for m in range(128):
        dl = m % 64
        if dl < 8:
            cm[m + 8, M_ROT + m] = -1.0
        elif dl < 16:
            cm[m - 8, M_ROT + m] = 1.0
    kk = np.arange(128)[:, None]
    qq = np.arange(128)[None, :]
    cm[:, M_TRI:M_TRI + 128] = (kk <= qq).astype(np.float32)
    for idx in range(23):
        delta = idx - 3
        d = 128 * delta + qq - kk
        m = ((d >= 0) & (d <= 128)).astype(np.float32)
        m += ((d % 4 == 0) & (d >= 0) & (d <= 512)).astype(np.float32)
        m += ((d % 16 == 0) & (d >= 0) & (d <= 2048)).astype(np.float32)
        cm[:, M_DSW + idx * 128:M_DSW + (idx + 1) * 128] = m
    return cm.astype(ml_dtypes.bfloat16)


_NC_CACHE = {}


def _prep(x, att_norm, att_w_in, dsw_q_norm, dsw_k_norm, diff_q_norm, diff_k_norm,
           diff_lam_q1, diff_lam_k1, diff_lam_q2, diff_lam_k2, diff_subln, att_w_out,
           sgu_norm, sgu_w_in, sgu_ln_g, sgu_ln_b, sgu_w_s, sgu_b_s, sgu_w_out):
    f = np.float32
    xT = np.ascontiguousarray(np.asarray(x, f)[0].T)
    xT_tiles = xT.reshape(D, NT, TT)
    cm = _const_mats()
    w_in0 = np.ascontiguousarray(np.asarray(att_w_in, f)[0])
    w_out0 = np.ascontiguousarray(np.asarray(att_w_out, f)[0])
    w_in1 = np.ascontiguousarray(np.asarray(sgu_w_in, f)[0])
    w_out1 = np.ascontiguousarray(np.asarray(sgu_w_out, f)[0])
    lngb = np.ascontiguousarray(np.stack([np.asarray(sgu_ln_g, f)[0], np.asarray(sgu_ln_b, f)[0]], 0))
    wsT = np.ascontiguousarray(np.asarray(sgu_w_s, f)[0].transpose(2, 0, 1).reshape(128, 8 * 128))
    bsb = np.ascontiguousarray(np.asarray(sgu_b_s, f)[0].reshape(1, 8 * 128))
    in_maps = []
    orders = []
    for c in range(NCORE):
        order = _tile_order(c)
        orders.append(order)
        x_all = np.ascontiguousarray(xT_tiles[:, order, :].reshape(D, S))
        pos_all = (np.asarray(order)[:, None] * TT + np.arange(TT)[None, :]).reshape(-1)
        halo_tiles = []
        for j in range(NSLOT):
            halo_tiles += [8 * j + c - NHALO + m for m in range(NHALO)]
        x_halo = np.zeros((D, NSLOT * NHALO, TT), f)
        pos_halo = np.zeros((NSLOT * NHALO, TT), np.int64)
        for i, t in enumerate(halo_tiles):
            if t >= 0:
                x_halo[:, i, :] = xT_tiles[:, t, :]
                pos_halo[i] = t * TT + np.arange(TT)
        cst = np.zeros((128, NCST), f)
        cst[:, C_ATTN:C_ATTN + 8] = np.asarray(att_norm, f)[0].reshape(8, 128).T
        cst[:, C_SGUN:C_SGUN + 8] = np.asarray(sgu_norm, f)[0].reshape(8, 128).T
        cst[:, C_DSWQ] = np.tile(np.asarray(dsw_q_norm, f)[0], 2)
        cst[:, C_DSWK] = np.tile(np.asarray(dsw_k_norm, f)[0], 2)
        cst[:, C_DIFQ] = np.tile(np.asarray(diff_q_norm, f)[0], 2)
        cst[:, C_DIFK] = np.tile(np.asarray(diff_k_norm, f)[0], 2)
        cst[:, C_SUBLN] = np.asarray(diff_subln, f)[0]
        for i in range(8):
            cst[:, C_CB + i] = 0.0 if (i >= 7 - c) else NEG
        for j in range(NSLOT):
            for m in range(NHALO):
                cst[:, C_HB + 4 * j + m] = 0.0 if (8 * j + c - NHALO + m) >= 0 else NEG
        cst[:, C_LQ1:C_LQ1 + 64] = np.asarray(diff_lam_q1, f)[0][None, :]
        cst[:, C_LK1:C_LK1 + 64] = np.asarray(diff_lam_k1, f)[0][None, :]
        cst[:, C_LQ2:C_LQ2 + 64] = np.asarray(diff_lam_q2, f)[0][None, :]
        cst[:, C_LK2:C_LK2 + 64] = np.asarray(diff_lam_k2, f)[0][None, :]
        cst[:, C_EPSR] = RMS_EPS
        cst[:, C_EPSL] = LN_EPS
        cst[:, C_ONE] = 1.0
        in_maps.append({
            "x_all": x_all,
            "x_halo": np.ascontiguousarray(x_halo.reshape(D, NSLOT * NHALO * TT)),
            "rope_all": _rope_tables(pos_all),
            "rope_halo": _rope_tables(pos_halo.reshape(-1)),
            "w_in0": w_in0, "w_out0": w_out0, "w_in1": w_in1, "w_out1": w_out1,
            "cst": cst, "cmat": cm, "lngb": lngb, "wsT": wsT, "bsb": bsb,
        })
    return in_maps


def kernel(**inputs):
    f = np.float32
    in_maps = _prep(**inputs)
    if "nc" not in _NC_CACHE:
        _NC_CACHE["nc"] = build_program()
    nc = _NC_CACHE["nc"]
    res = run_bass_kernel_spmd(nc, in_maps, core_ids=list(range(NCORE)))
    out = np.zeros((S, D), f)
    for c in range(NCORE):
        oT = np.asarray(res.results[c]["outT"], f)
        for j in range(NSLOT):
            t = 8 * j + c
            out[t * TT:(t + 1) * TT, :] = oT[:, j * TT:(j + 1) * TT].T
    return out[None]
```

```python
import math
from contextlib import ExitStack

import ml_dtypes
import numpy as np

import concourse.bass as bass
import concourse.mybir as mybir
from concourse.bass_utils import run_bass_kernel_spmd

F32 = mybir.dt.float32
BF16 = mybir.dt.bfloat16
AF = mybir.ActivationFunctionType
ALU = mybir.AluOpType

S = 16384
D = 1024
TT = 512
NT = S // TT
NCORE = 8
NSLOT = NT // NCORE
NHALO = 4
NEG = -30000.0
RMS_EPS = 1e-6
LN_EPS = 1e-5
LAM_INIT0 = 0.8 - 0.6 * math.exp(-0.3 * 0)

C_ATTN = 0
C_SGUN = 8
C_DSWQ = 16
C_DSWK = 17
C_DIFQ = 18
C_DIFK = 19
C_SUBLN = 20
C_CB = 21
C_HB = 29
C_ZERO = 45
C_LQ1 = 46
C_LK1 = 110
C_LQ2 = 174
C_LK2 = 238
C_EPSR = 302
C_EPSL = 303
C_ONE = 304
NCST = 305
M_ONES = 0
M_BONES = 128
M_ROT = 256
M_TRI = 384
M_DSW = 512
NCM = 512 + 23 * 128


class Sched:
    def __init__(self, nc, es):
        self.nc = nc
        self.es = es
        self.E = {"pe": nc.tensor, "act": nc.scalar, "dve": nc.vector, "pool": nc.gpsimd, "sp": nc.sync}
        self.semh = {}
        self.cnt = {}
        for e in self.E:
            self.semh[e] = es.enter_context(nc.semaphore("s_" + e))
            self.cnt[e] = 0
        self.known = {e: {} for e in self.E}
        self.lastw = {}
        self.readers = {}
        self.nwait = 0

    def dsem(self, name):
        k = "d_" + name
        if k not in self.semh:
            self.semh[k] = self.es.enter_context(self.nc.semaphore(k))
            self.cnt[k] = 0
        return k

    def _wait(self, eng, deps):
        need = {}
        for (sk, val) in deps:
            if sk == eng and eng in ("pe", "sp"):
                continue
            if val > need.get(sk, 0):
                need[sk] = val
        for sk, val in need.items():
            if self.known[eng].get(sk, 0) >= val:
                continue
            self.E[eng].wait_ge(self.semh[sk], val)
            self.known[eng][sk] = val
            self.nwait += 1

    def _deps(self, reads, writes):
        deps = []
        for k in reads:
            if k in self.lastw:
                deps.append(self.lastw[k])
        for k in writes:
            if k in self.lastw:
                deps.append(self.lastw[k])
            deps.extend(self.readers.get(k, {}).items())
        return deps

    def _record(self, ev, reads, writes):
        for k in reads:
            r = self.readers.setdefault(k, {})
            if ev[1] > r.get(ev[0], 0):
                r[ev[0]] = ev[1]
        for k in writes:
            self.lastw[k] = ev
            self.readers[k] = {}

    def op(self, eng, fn, reads=(), writes=()):
        self._wait(eng, self._deps(reads, writes))
        ins = fn()
        self.cnt[eng] += 1
        ins.then_inc(self.semh[eng], 1)
        self._record((eng, self.cnt[eng]), reads, writes)

    def dma(self, q, sem, out, in_, reads=(), writes=()):
        sk = self.dsem(sem)
        self._wait(q, self._deps(reads, writes))
        ins = self.E[q].dma_start(out=out, in_=in_)
        self.cnt[sk] += 16
        ins.then_inc(self.semh[sk], 16)
        self._record((sk, self.cnt[sk]), reads, writes)

    def barrier(self):
        for e in self.E:
            for sk, c in self.cnt.items():
                if sk == e and e in ("pe", "sp"):
                    continue
                if c > 0 and self.known[e].get(sk, 0) < c:
                    self.E[e].wait_ge(self.semh[sk], c)
                    self.known[e][sk] = c
        self.lastw = {}
        self.readers = {}


def build_program(stop=None):
    nc = bass.Bass("TRN2", target_bir_lowering=False)
    x_all = nc.dram_tensor("x_all", [D, S], F32, kind="ExternalInput").ap()
    x_halo = nc.dram_tensor("x_halo", [D, NSLOT * NHALO * TT], F32, kind="ExternalInput").ap()
    rope_all = nc.dram_tensor("rope_all", [2, 128, S], F32, kind="ExternalInput").ap()
    rope_halo = nc.dram_tensor("rope_halo", [2, 128, NSLOT * NHALO * TT], F32, kind="ExternalInput").ap()
    w_in0 = nc.dram_tensor("w_in0", [D, 4096], F32, kind="ExternalInput").ap()
    w_out0 = nc.dram_tensor("w_out0", [D, D], F32, kind="ExternalInput").ap()
    w_in1 = nc.dram_tensor("w_in1", [D, 6144], F32, kind="ExternalInput").ap()
    w_out1 = nc.dram_tensor("w_out1", [2048, D], F32, kind="ExternalInput").ap()
    cst_d = nc.dram_tensor("cst", [128, NCST], F32, kind="ExternalInput").ap()
    cmat_d = nc.dram_tensor("cmat", [128, NCM], BF16, kind="ExternalInput").ap()
    lngb_d = nc.dram_tensor("lngb", [2, 2048], F32, kind="ExternalInput").ap()
    wsT_d = nc.dram_tensor("wsT", [128, 8 * 128], F32, kind="ExternalInput").ap()
    bsb_d = nc.dram_tensor("bsb", [1, 8 * 128], F32, kind="ExternalInput").ap()
    out_d = nc.dram_tensor("outT", [D, NSLOT * TT], F32, kind="ExternalOutput").ap()
    W0b = nc.dram_tensor("W0b", [D, 4096], BF16).ap()
    Wo0b = nc.dram_tensor("Wo0b", [D, D], BF16).ap()
    W1b = nc.dram_tensor("W1b", [D, 6144], BF16).ap()
    Wo1b = nc.dram_tensor("Wo1b", [2048, D], BF16).ap()
    KD = nc.dram_tensor("KD", [4, 128, S], BF16).ap()
    VD = nc.dram_tensor("VD", [S, 512], BF16).ap()
    NKA = NSLOT * (NHALO + 1) * TT
    KA = nc.dram_tensor("KA", [4, 128, NKA], BF16).ap()
    VA = nc.dram_tensor("VA", [NKA, 512], BF16).ap()
    QA = nc.dram_tensor("QA", [4, 128, NSLOT * TT], BF16).ap()
    QD = nc.dram_tensor("QD", [4, 128, NSLOT * TT], BF16).ap()
    GA = nc.dram_tensor("GA", [8, 64, NSLOT * TT], BF16).ap()
    GD = nc.dram_tensor("GD", [4, 128, NSLOT * TT], BF16).ap()

    with ExitStack() as es:
        es.enter_context(nc.allow_low_precision("bf16 matmul operands, fp32 accumulation"))
        es.enter_context(nc.allow_non_contiguous_dma("strided weight / scratch tiles"))
        sc = Sched(nc, es)

        uid = [0]

        def sb(name, shape, dt, stack=es):
            uid[0] += 1
            return stack.enter_context(nc.sbuf_tensor(f"{name}_{uid[0]}", shape, dt))

        cst = sb("cst_sb", [128, NCST], F32)
        cmat = sb("cmat_sb", [128, NCM], BF16)
        lamc = sb("lamc", [128, 8], F32)
        x1 = sb("x1", [128, 8, TT], F32)
        onesf = sb("onesf", [128, 128], F32)
        psum = [es.enter_context(nc.psum_tensor(f"ps{i}", [128, TT], F32)) for i in range(8)]
        ps_rr = [0]

        def ps_next(lo=0, hi=8):
            i = lo + (ps_rr[0] % (hi - lo))
            ps_rr[0] += 1
            return i

        ones = cmat[:, M_ONES:M_ONES + 128]
        bones = cmat[:, M_BONES:M_BONES + 128]
        rotm = cmat[:, M_ROT:M_ROT + 128]
        tri = cmat[:, M_TRI:M_TRI + 128]

        def ccol(c, n=128):
            return cst[0:n, c:c + 1]

        sc.dma("sp", "cst", cst[:], cst_d, writes=["cst"])
        sc.dma("sp", "cmat", cmat[:], cmat_d, writes=["cmat"])
        for (src, dst, rows, nm) in ((w_in0, W0b, D, "W0b"), (w_out0, Wo0b, D, "Wo0b"),
                                     (w_in1, W1b, D, "W1b"), (w_out1, Wo1b, 2048, "Wo1b")):
            step = 128
            for r0 in range(0, rows, step):
                sc.dma("pool", "wcast", dst[r0:r0 + step, :], src[r0:r0 + step, :], writes=[nm])
                done = sc.cnt["d_wcast"] - 32
                if done > 0:
                    nc.gpsimd.wait_ge(sc.semh["d_wcast"], done)
                    sc.known["pool"]["d_wcast"] = done
        sc.op("dve", lambda: nc.vector.memset(onesf[:], 1.0), writes=["onesf"])
        with ExitStack() as ls:
            lt = sb("lam_t", [128, 128], F32, ls)
            for i, (ca, cb) in enumerate(((C_LQ1, C_LK1), (C_LQ2, C_LK2))):
                sc.op("dve", lambda ca=ca, cb=cb, i=i: nc.vector.tensor_tensor(
                    out=lt[:, i * 64:(i + 1) * 64], in0=cst[:, ca:ca + 64], in1=cst[:, cb:cb + 64], op=ALU.mult),
                    reads=["cst"], writes=["lt"])
            for i in range(2):
                sc.op("dve", lambda i=i: nc.vector.reduce_sum(
                    out=lamc[:, 2 + i:3 + i], in_=lt[:, i * 64:(i + 1) * 64], axis=mybir.AxisListType.X),
                    reads=["lt"], writes=["lamc"])
            sc.op("act", lambda: nc.scalar.activation(out=lamc[:, 4:6], in_=lamc[:, 2:4], func=AF.Exp),
                  reads=["lamc"], writes=["lamc"])
            sc.op("dve", lambda: nc.vector.scalar_tensor_tensor(
                out=lamc[:, 0:1], in0=lamc[:, 5:6], scalar=-LAM_INIT0, in1=lamc[:, 4:5],
                op0=ALU.add, op1=ALU.subtract), reads=["lamc"], writes=["lamc"])
            sc.op("dve", lambda: nc.vector.tensor_scalar(
                out=lamc[:, 1:2], in0=cst[:, C_SUBLN:C_SUBLN + 1], scalar1=1.0 - LAM_INIT0, scalar2=None,
                op0=ALU.mult), reads=["cst", "lamc"], writes=["lamc"])
            sc.barrier()

        def rmsnorm_tile(xin, xk, gcol0, sq, sqk, hT, hk, rs, rsk):
            sc.op("act", lambda: nc.scalar.activation(out=sq[:], in_=xin[:], func=AF.Square),
                  reads=[xk], writes=[sqk])
            pi = ps_next()
            for k in range(8):
                sc.op("pe", lambda k=k: nc.tensor.matmul(psum[pi][:], lhsT=ones, rhs=sq[:, k, :],
                                                       start=(k == 0), stop=(k == 7)),
                      reads=[sqk, "cmat"], writes=[f"ps{pi}"])
            sc.op("act", lambda: nc.scalar.activation(out=rs[:], in_=psum[pi][:], func=AF.Ln,
                                                      scale=1.0 / D, bias=ccol(C_EPSR)),
                  reads=[f"ps{pi}", "cst"], writes=[rsk])
            sc.op("act", lambda: nc.scalar.activation(out=rs[:], in_=rs[:], func=AF.Exp, scale=-0.5),
                  reads=[rsk], writes=[rsk])
            for k in range(8):
                sc.op("dve", lambda k=k: nc.vector.scalar_tensor_tensor(
                    out=hT[:, k, :], in0=xin[:, k, :], scalar=cst[:, gcol0 + k:gcol0 + k + 1], in1=rs[:],
                    op0=ALU.mult, op1=ALU.mult), reads=[xk, rsk, "cst"], writes=[hk])

        def proj_fm(pi, hT, hk, w, wk, c0, m=128, prow=128):
            for k in range(8):
                sc.op("pe", lambda k=k: nc.tensor.matmul(psum[pi][0:m, :], lhsT=w[:, k, c0:c0 + m], rhs=hT[:, k, :],
                                                       start=(k == 0), stop=(k == 7)),
                      reads=[hk, wk], writes=[f"ps{pi}"])

        if stop == "setup":
            return nc
        with ExitStack() as p1:
            xin = [sb(f"xin{i}", [128, 8, TT], F32, p1) for i in range(2)]
            sq = sb("sq", [128, 8, TT], BF16, p1)
            hT = sb("hT", [128, 8, TT], BF16, p1)
            rs = sb("rs", [128, TT], F32, p1)
            wkvd = sb("wkvd", [128, 8, 1024], BF16, p1)
            wkva = sb("wkva", [128, 8, 1024], BF16, p1)
            wst = [sb(f"wst{i}", [128, 8, 512], BF16, p1) for i in range(2)]
            rope = [sb(f"rope{i}", [128, 2, TT], F32, p1) for i in range(2)]
            NPB = 3
            sqc = [sb(f"sqc{i}", [128, TT], BF16, p1) for i in range(NPB)]
            qa_ = [sb(f"qA{i}", [128, TT], BF16, p1) for i in range(NPB)]
            rsc = [sb(f"rsc{i}", [128, TT], F32, p1) for i in range(NPB)]
            t1 = [sb(f"t1{i}", [128, TT], F32, p1) for i in range(NPB)]
            t2 = [sb(f"t2{i}", [128, TT], F32, p1) for i in range(NPB)]
            ko = [sb(f"ko{i}", [128, TT], BF16, p1) for i in range(NPB)]
            vst = [sb(f"vst{i}", [128, TT], BF16, p1) for i in range(2)]
            pb = [0]
            vb = [0]
            wsb = [0]

            W0v = W0b.rearrange("(k p) e -> p k e", p=128)
            sc.dma("sp", "wkvd", wkvd[:, :, 0:512], W0v[:, :, 2560:3072], reads=["W0b"], writes=["wkvd"])
            sc.dma("sp", "wkvd", wkvd[:, :, 512:1024], W0v[:, :, 3072:3584], reads=["W0b"], writes=["wkvd"])
            sc.dma("sp", "wkva", wkva[:, :, 0:512], W0v[:, :, 512:1024], reads=["W0b"], writes=["wkva"])
            sc.dma("sp", "wkva", wkva[:, :, 512:1024], W0v[:, :, 1024:1536], reads=["W0b"], writes=["wkva"])

            def qk_post(pi, gcol, ropet, ropek, dst_ap, dkey):
                b = pb[0] % NPB
                pb[0] += 1
                sc.op("act", lambda: nc.scalar.activation(out=sqc[b][:], in_=psum[pi][:], func=AF.Square),
                      reads=[f"ps{pi}"], writes=[f"sqc{b}"])
                sc.op("act", lambda: nc.scalar.activation(out=qa_[b][:], in_=psum[pi][:], func=AF.Identity,
                                                          scale=ccol(gcol)),
                      reads=[f"ps{pi}", "cst"], writes=[f"qA{b}"])
                p2 = ps_next()
                sc.op("pe", lambda: nc.tensor.matmul(psum[p2][:], lhsT=bones, rhs=sqc[b][:], start=True, stop=True),
                      reads=[f"sqc{b}", "cmat"], writes=[f"ps{p2}"])
                p3 = ps_next()
                sc.op("pe", lambda: nc.tensor.matmul(psum[p3][:], lhsT=rotm, rhs=qa_[b][:], start=True, stop=True),
                      reads=[f"qA{b}", "cmat"], writes=[f"ps{p3}"])
                sc.op("act", lambda: nc.scalar.activation(out=rsc[b][:], in_=psum[p2][:], func=AF.Ln,
                                                          scale=1.0 / 64, bias=ccol(C_EPSR)),
                      reads=[f"ps{p2}", "cst"], writes=[f"rsc{b}"])
                sc.op("act", lambda: nc.scalar.activation(out=rsc[b][:], in_=rsc[b][:], func=AF.Exp, scale=-0.5),
                      reads=[f"rsc{b}"], writes=[f"rsc{b}"])
                sc.op("pool", lambda: nc.gpsimd.tensor_tensor(out=t1[b][:], in0=qa_[b][:], in1=ropet[:, 0, :],
                                                              op=ALU.mult),
                      reads=[f"qA{b}", ropek], writes=[f"t1{b}"])
                sc.op("dve", lambda: nc.vector.tensor_tensor(out=t2[b][:], in0=psum[p3][:], in1=ropet[:, 1, :],
                                                             op=ALU.mult),
                      reads=[f"ps{p3}", ropek], writes=[f"t2{b}"])
                sc.op("pool", lambda: nc.gpsimd.tensor_tensor(out=t1[b][:], in0=t1[b][:], in1=t2[b][:], op=ALU.add),
                      reads=[f"t1{b}", f"t2{b}"], writes=[f"t1{b}"])
                sc.op("dve", lambda: nc.vector.tensor_tensor(out=ko[b][:], in0=t1[b][:], in1=rsc[b][:], op=ALU.mult),
                      reads=[f"t1{b}", f"rsc{b}"], writes=[f"ko{b}"])
                sc.dma("sp", f"ko{b}", dst_ap, ko[b][:], reads=[f"ko{b}"], writes=[dkey])

            def v_tm(w, wk, c0, dst_rows_fn, dkey):
                for sub in range(4):
                    pi = ps_next()
                    for k in range(8):
                        sc.op("pe", lambda k=k: nc.tensor.matmul(
                            psum[pi][:], lhsT=hT[:, k, sub * 128:(sub + 1) * 128], rhs=w[:, k, c0:c0 + 512],
                            start=(k == 0), stop=(k == 7)), reads=["hT", wk], writes=[f"ps{pi}"])
                    b = vb[0] % 2
                    vb[0] += 1
                    sc.op("act", lambda: nc.scalar.activation(out=vst[b][:], in_=psum[pi][:], func=AF.Copy),
                          reads=[f"ps{pi}"], writes=[f"vst{b}"])
                    sc.dma("sp", f"vst{b}", dst_rows_fn(sub), vst[b][:], reads=[f"vst{b}"], writes=[dkey])

            def silu_fm(pi, m, dst_ap, dkey):
                b = pb[0] % NPB
                pb[0] += 1
                sc.op("act", lambda: nc.scalar.activation(out=ko[b][0:m, :], in_=psum[pi][0:m, :], func=AF.Silu),
                      reads=[f"ps{pi}"], writes=[f"ko{b}"])
                sc.dma("sp", f"ko{b}", dst_ap, ko[b][0:m, :], reads=[f"ko{b}"], writes=[dkey])

            def load_wst(c0):
                b = wsb[0] % 2
                wsb[0] += 1
                sc.dma("sp", f"wst{b}", wst[b][:], W0v[:, :, c0:c0 + 512], reads=["W0b"], writes=[f"wst{b}"])
                return wst[b], f"wst{b}"

            tiles = [("all", pos) for pos in range(NT)] + [("halo", i) for i in range(NSLOT * NHALO)]
            if stop == "p1a":
                tiles = tiles[:8]
            xav = x_all.rearrange("(k p) t -> p k t", p=128)
            xhv = x_halo.rearrange("(k p) t -> p k t", p=128)

            def load_tile(ti):
                kind, idx = tiles[ti]
                xb = ti % 2
                srcx = xav if kind == "all" else xhv
                srcr = rope_all if kind == "all" else rope_halo
                sc.dma("sp", f"xin{xb}", xin[xb][:], srcx[:, :, idx * TT:(idx + 1) * TT], writes=[f"xin{xb}"])
                sc.dma("sp", f"rope{xb}", rope[xb][:],
                       srcr.rearrange("c p t -> p c t")[:, :, idx * TT:(idx + 1) * TT], writes=[f"rope{xb}"])

            load_tile(0)
            for ti, (kind, idx) in enumerate(tiles):
                xb = ti % 2
                xk = f"xin{xb}"
                rk = f"rope{xb}"
                if ti + 1 < len(tiles):
                    load_tile(ti + 1)
                rmsnorm_tile(xin[xb], xk, C_ATTN, sq, "sq", hT, "hT", rs, "rs")
                if kind == "all":
                    pos = idx
                    for ch in range(4):
                        pi = ps_next()
                        proj_fm(pi, hT, "hT", wkvd, "wkvd", ch * 128)
                        qk_post(pi, C_DIFK, rope[xb], rk, KD[ch, :, pos * TT:(pos + 1) * TT], "KD")
                    v_tm(wkvd, "wkvd", 512, lambda sub, pos=pos: VD[(pos * 4 + sub) * 128:(pos * 4 + sub + 1) * 128, :],
                         "VD")
                    if pos % 8 == 7:
                        j = pos // 8
                        kpos = j * (NHALO + 1) + NHALO
                        for ch in range(4):
                            pi = ps_next()
                            proj_fm(pi, hT, "hT", wkva, "wkva", ch * 128)
                            qk_post(pi, C_DSWK, rope[xb], rk, KA[ch, :, kpos * TT:(kpos + 1) * TT], "KA")
                        v_tm(wkva, "wkva", 512,
                             lambda sub, kpos=kpos: VA[(kpos * 4 + sub) * 128:(kpos * 4 + sub + 1) * 128, :], "VA")
                        w, wk = load_wst(0)
                        for ch in range(4):
                            pi = ps_next()
                            proj_fm(pi, hT, "hT", w, wk, ch * 128)
                            qk_post(pi, C_DSWQ, rope[xb], rk, QA[ch, :, j * TT:(j + 1) * TT], "QA")
                        w, wk = load_wst(1536)
                        for h in range(8):
                            pi = ps_next()
                            proj_fm(pi, hT, "hT", w, wk, h * 64, m=64)
                            silu_fm(pi, 64, GA[h, :, j * TT:(j + 1) * TT], "GA")
                        w, wk = load_wst(2048)
                        for ch in range(4):
                            pi = ps_next()
                            proj_fm(pi, hT, "hT", w, wk, ch * 128)
                            qk_post(pi, C_DIFQ, rope[xb], rk, QD[ch, :, j * TT:(j + 1) * TT], "QD")
                        w, wk = load_wst(3584)
                        for ch in range(4):
                            pi = ps_next()
                            proj_fm(pi, hT, "hT", w, wk, ch * 128)
                            silu_fm(pi, 128, GD[ch, :, j * TT:(j + 1) * TT], "GD")
                else:
                    j, m = idx // NHALO, idx % NHALO
                    kpos = j * (NHALO + 1) + m
                    for ch in range(4):
                        pi = ps_next()
                        proj_fm(pi, hT, "hT", wkva, "wkva", ch * 128)
                        qk_post(pi, C_DSWK, rope[xb], rk, KA[ch, :, kpos * TT:(kpos + 1) * TT], "KA")
                    v_tm(wkva, "wkva", 512,
                         lambda sub, kpos=kpos: VA[(kpos * 4 + sub) * 128:(kpos * 4 + sub + 1) * 128, :], "VA")
            sc.barrier()

        if stop in ("p1", "p1a"):
            return nc
        Wo0v_a = Wo0b[0:512, :].rearrange("(h d) e -> d h e", d=64)
        Wo0v_d = Wo0b[512:1024, :].rearrange("(h d) e -> d h e", d=128)
        VDv = VD.rearrange("(n p) e -> p n e", p=128)
        VAv = VA.rearrange("(n p) e -> p n e", p=128)
        W1v = W1b.rearrange("(k p) e -> p k e", p=128)
        Wo1v = Wo1b.rearrange("(c p) e -> p c e", p=128)
        xav = x_all.rearrange("(k p) t -> p k t", p=128)
        for j in range(NSLOT):
            own = 8 * j + 7
            with ExitStack() as p2:
                qaT = sb("qaT", [128, 4, TT], BF16, p2)
                qdT = sb("qdT", [128, 4, TT], BF16, p2)
                gaT = sb("gaT", [64, 8, TT], BF16, p2)
                gdT = sb("gdT", [128, 4, TT], BF16, p2)
                yaT = sb("yaT", [64, 8, TT], BF16, p2)
                ydT = sb("ydT", [128, 4, TT], BF16, p2)
                woa = sb("woa", [64, 8, D], BF16, p2)
                wod = sb("wod", [128, 4, D], BF16, p2)
                NKB = 3
                kbuf = [sb(f"kbuf{i}", [128, TT], BF16, p2) for i in range(NKB)]
                vbuf = [sb(f"vbuf{i}", [128, 4, 128], BF16, p2) for i in range(NKB)]
                kab = [sb(f"kab{i}", [128, (NHALO + 1) * TT], BF16, p2) for i in range(2)]
                vab = [sb(f"vab{i}", [128, (NHALO + 1) * 4, 128], BF16, p2) for i in range(2)]
                LAG = 2
                NPT = 2 * (LAG + 2)
                pT = [sb(f"pT{i}", [128, TT], BF16, p2) for i in range(NPT)]
                acc = [sb(f"acc{i}", [128, TT], F32, p2) for i in range(4)]
                tf = [sb(f"tf{i}", [128, TT], F32, p2) for i in range(8)]
                sqo = sb("sqo", [128, TT], BF16, p2)
                ptc = [0]

                sc.dma("sp", "qaT", qaT[:], QA.rearrange("c p t -> p c t")[:, :, j * TT:(j + 1) * TT], writes=["qaT"])
                sc.dma("sp", "qdT", qdT[:], QD.rearrange("c p t -> p c t")[:, :, j * TT:(j + 1) * TT], writes=["qdT"])
                sc.dma("sp", "gaT", gaT[:], GA.rearrange("c p t -> p c t")[:, :, j * TT:(j + 1) * TT], writes=["gaT"])
                sc.dma("sp", "gdT", gdT[:], GD.rearrange("c p t -> p c t")[:, :, j * TT:(j + 1) * TT], writes=["gdT"])
                sc.dma("sp", "woa", woa[:], Wo0v_a, writes=["woa"])
                sc.dma("sp", "wod", wod[:], Wo0v_d, writes=["wod"])
                sc.dma("sp", "x1", x1[:], xav[:, :, own * TT:(own + 1) * TT], writes=["x1"])

                def attn_pipeline(steps, emit_front, emit_back):
                    pend = []
                    for st in steps:
                        pend.append(emit_front(st))
                        if len(pend) > LAG:
                            emit_back(pend.pop(0))
                    while pend:
                        emit_back(pend.pop(0))

                def recip_from_acc(accs, m, outs):
                    for a_, o_ in zip(accs, outs):
                        pi = ps_next(2, 8)
                        sc.op("pe", lambda: nc.tensor.matmul(psum[pi][0:m, :], lhsT=onesf[:, 0:m], rhs=acc[a_][:],
                                                             start=True, stop=True),
                              reads=[f"acc{a_}", "onesf"], writes=[f"ps{pi}"])
                        sc.op("act", lambda: nc.scalar.activation(out=tf[o_][0:m, :], in_=psum[pi][0:m, :],
                                                                  func=AF.Ln),
                              reads=[f"ps{pi}"], writes=[f"tf{o_}"])
                        sc.op("act", lambda: nc.scalar.activation(out=tf[o_][0:m, :], in_=tf[o_][0:m, :],
                                                                  func=AF.Exp, scale=-1.0),
                              reads=[f"tf{o_}"], writes=[f"tf{o_}"])

                nkb = (NHALO + 1) * 4
                for ci in range(4):
                    b2 = ci % 2
                    kk, vk = f"kab{b2}", f"vab{b2}"
                    sc.dma("sp", kk, kab[b2][:], KA[ci, :, j * nkb * 128:(j + 1) * nkb * 128], writes=[kk])
                    sc.dma("sp", vk, vab[b2][:], VAv[:, j * nkb:(j + 1) * nkb, ci * 128:(ci + 1) * 128], writes=[vk])
                    aset = (ci % 2) * 2

                    def front_a(kb):
                        ids = []
                        pis = []
                        for hh in range(2):
                            base = hh * 64
                            pi = ps_next(2, 8)
                            pis.append(pi)
                            sc.op("pe", lambda: nc.tensor.matmul(
                                psum[pi][:], lhsT=kab[b2][base:base + 64, kb * 128:(kb + 1) * 128],
                                rhs=qaT[base:base + 64, ci, :], start=True, stop=True),
                                reads=[kk, "qaT"], writes=[f"ps{pi}"])
                        bcol = (C_HB + 4 * j + kb // 4) if kb < NHALO * 4 else C_ZERO
                        m0 = M_DSW + (19 - kb) * 128
                        for hh in range(2):
                            pi = pis[hh]
                            pb_ = ptc[0] % NPT
                            ptc[0] += 1
                            ids.append(pb_)
                            sc.op("act", lambda: nc.scalar.activation(out=pT[pb_][:], in_=psum[pi][:], func=AF.Exp,
                                                                      scale=0.125, bias=ccol(bcol)),
                                  reads=[f"ps{pi}", "cst"], writes=[f"pT{pb_}"])
                            sc.op("pool", lambda: nc.gpsimd.tensor_tensor(out=pT[pb_][:], in0=pT[pb_][:],
                                                                          in1=cmat[:, m0:m0 + 512], op=ALU.mult),
                                  reads=[f"pT{pb_}", "cmat"], writes=[f"pT{pb_}"])
                            a_ = aset + hh
                            if kb == 0:
                                sc.op("dve", lambda: nc.vector.tensor_copy(out=acc[a_][:], in_=pT[pb_][:]),
                                      reads=[f"pT{pb_}"], writes=[f"acc{a_}"])
                            else:
                                sc.op("dve", lambda: nc.vector.tensor_tensor(out=acc[a_][:], in0=acc[a_][:],
                                                                             in1=pT[pb_][:], op=ALU.add),
                                      reads=[f"pT{pb_}", f"acc{a_}"], writes=[f"acc{a_}"])
                        return (kb, ids)

                    def back_a(item):
                        kb, ids = item
                        for hh in range(2):
                            pb_ = ids[hh]
                            sc.op("pe", lambda: nc.tensor.matmul(
                                psum[hh][0:64, :], lhsT=vab[b2][:, kb, hh * 64:(hh + 1) * 64], rhs=pT[pb_][:],
                                start=(kb == 0), stop=(kb == nkb - 1)),
                                reads=[vk, f"pT{pb_}"], writes=[f"ps{hh}"])

                    attn_pipeline(list(range(nkb)), front_a, back_a)
                    recip_from_acc([aset, aset + 1], 64, [0, 1])
                    for hh in range(2):
                        h = 2 * ci + hh
                        sc.op("pool", lambda: nc.gpsimd.tensor_tensor(out=tf[hh][0:64, :], in0=tf[hh][0:64, :],
                                                                      in1=gaT[:, h, :], op=ALU.mult),
                              reads=[f"tf{hh}", "gaT"], writes=[f"tf{hh}"])
                        sc.op("dve", lambda: nc.vector.tensor_tensor(out=yaT[:, h, :], in0=psum[hh][0:64, :],
                                                                     in1=tf[hh][0:64, :], op=ALU.mult),
                              reads=[f"ps{hh}", f"tf{hh}"], writes=["yaT"])

                npos = 8 * j + 8
                for h in range(4):
                    aset = (h % 2) * 2

                    def front_d(st):
                        pos, kb = st
                        kbi = (h * npos + pos) % NKB
                        kk, vk = f"kbuf{kbi}", f"vbuf{kbi}"
                        if kb == 0:
                            sc.dma("sp", kk, kbuf[kbi][:], KD[h, :, pos * TT:(pos + 1) * TT], writes=[kk])
                            sc.dma("sp", vk, vbuf[kbi][:], VDv[:, pos * 4:(pos + 1) * 4, h * 128:(h + 1) * 128],
                                   writes=[vk])
                        diag = (pos == own)
                        bcol = (C_CB + (pos - 8 * j)) if pos >= 8 * j else C_ZERO
                        q0 = kb * 128 if diag else 0
                        pis = []
                        ids = []
                        for comp in range(2):
                            base = comp * 64
                            pi = ps_next(2, 8)
                            pis.append(pi)
                            sc.op("pe", lambda: nc.tensor.matmul(
                                psum[pi][:, q0:TT], lhsT=kbuf[kbi][base:base + 64, kb * 128:(kb + 1) * 128],
                                rhs=qdT[base:base + 64, h, q0:TT], start=True, stop=True),
                                reads=[kk, "qdT"], writes=[f"ps{pi}"])
                        for comp in range(2):
                            pi = pis[comp]
                            pb_ = ptc[0] % NPT
                            ptc[0] += 1
                            ids.append(pb_)
                            sc.op("act", lambda: nc.scalar.activation(
                                out=pT[pb_][:, q0:TT], in_=psum[pi][:, q0:TT], func=AF.Exp, scale=0.125,
                                bias=ccol(bcol)), reads=[f"ps{pi}", "cst"], writes=[f"pT{pb_}"])
                            if diag:
                                sc.op("pool", lambda: nc.gpsimd.tensor_tensor(
                                    out=pT[pb_][:, q0:q0 + 128], in0=pT[pb_][:, q0:q0 + 128], in1=tri,
                                    op=ALU.mult), reads=[f"pT{pb_}", "cmat"], writes=[f"pT{pb_}"])
                            a_ = aset + comp
                            eng = "dve" if comp == 0 else "pool"
                            E_ = nc.vector if comp == 0 else nc.gpsimd
                            if pos == 0 and kb == 0:
                                sc.op(eng, lambda: E_.tensor_copy(out=acc[a_][:], in_=pT[pb_][:]),
                                      reads=[f"pT{pb_}"], writes=[f"acc{a_}"])
                            else:
                                sc.op(eng, lambda: E_.tensor_tensor(out=acc[a_][:, q0:TT], in0=acc[a_][:, q0:TT],
                                                                    in1=pT[pb_][:, q0:TT], op=ALU.add),
                                      reads=[f"pT{pb_}", f"acc{a_}"], writes=[f"acc{a_}"])
                        return (pos, kb, q0, kbi, ids)

                    def back_d(item):
                        pos, kb, q0, kbi, ids = item
                        vk = f"vbuf{kbi}"
                        st = (pos == 0 and kb == 0)
                        last = (pos == npos - 1 and kb == 3)
                        for comp in range(2):
                            pb_ = ids[comp]
                            sc.op("pe", lambda: nc.tensor.matmul(
                                psum[comp][:, q0:TT], lhsT=vbuf[kbi][:, kb, :], rhs=pT[pb_][:, q0:TT],
                                start=st, stop=last), reads=[vk, f"pT{pb_}"], writes=[f"ps{comp}"])

                    attn_pipeline([(pos, kb) for pos in range(npos) for kb in range(4)], front_d, back_d)
                    recip_from_acc([aset, aset + 1], 128, [0, 1])
                    for comp in range(2):
                        sc.op("dve", lambda comp=comp: nc.vector.tensor_tensor(
                            out=tf[2 + comp][:], in0=psum[comp][:], in1=tf[comp][:], op=ALU.mult),
                            reads=[f"ps{comp}", f"tf{comp}"], writes=[f"tf{2 + comp}"])
                    sc.op("dve", lambda: nc.vector.scalar_tensor_tensor(
                        out=tf[4][:], in0=tf[3][:], scalar=lamc[:, 0:1], in1=tf[2][:], op0=ALU.mult, op1=ALU.add),
                        reads=["tf3", "tf2", "lamc"], writes=["tf4"])
                    sc.op("act", lambda: nc.scalar.activation(out=sqo[:], in_=tf[4][:], func=AF.Square),
                          reads=["tf4"], writes=["sqo"])
                    pi = ps_next(2, 8)
                    sc.op("pe", lambda: nc.tensor.matmul(psum[pi][:], lhsT=ones, rhs=sqo[:], start=True, stop=True),
                          reads=["sqo", "cmat"], writes=[f"ps{pi}"])
                    sc.op("act", lambda: nc.scalar.activation(out=tf[5][:], in_=psum[pi][:], func=AF.Ln,
                                                              scale=1.0 / 128, bias=ccol(C_EPSR)),
                          reads=[f"ps{pi}", "cst"], writes=["tf5"])
                    sc.op("act", lambda: nc.scalar.activation(out=tf[5][:], in_=tf[5][:], func=AF.Exp, scale=-0.5),
                          reads=["tf5"], writes=["tf5"])
                    sc.op("pool", lambda: nc.gpsimd.tensor_tensor(out=tf[5][:], in0=tf[5][:], in1=gdT[:, h, :],
                                                                  op=ALU.mult),
                          reads=["tf5", "gdT"], writes=["tf5"])
                    sc.op("dve", lambda: nc.vector.scalar_tensor_tensor(
                        out=ydT[:, h, :], in0=tf[4][:], scalar=lamc[:, 1:2], in1=tf[5][:],
                        op0=ALU.mult, op1=ALU.mult), reads=["tf4", "tf5", "lamc"], writes=["ydT"])

                for dc in range(8):
                    pi = ps_next(2, 8)
                    for h in range(8):
                        sc.op("pe", lambda h=h: nc.tensor.matmul(
                            psum[pi][:], lhsT=woa[:, h, dc * 128:(dc + 1) * 128], rhs=yaT[:, h, :],
                            start=(h == 0), stop=False), reads=["woa", "yaT"], writes=[f"ps{pi}"])
                    for h in range(4):
                        sc.op("pe", lambda h=h: nc.tensor.matmul(
                            psum[pi][:], lhsT=wod[:, h, dc * 128:(dc + 1) * 128], rhs=ydT[:, h, :],
                            start=False, stop=(h == 3)), reads=["wod", "ydT"], writes=[f"ps{pi}"])
                    sc.op("dve", lambda: nc.vector.tensor_tensor(out=x1[:, dc, :], in0=psum[pi][:], in1=x1[:, dc, :],
                                                                 op=ALU.add),
                          reads=[f"ps{pi}", "x1"], writes=["x1"])
                sc.barrier()
            if stop == "p2":
                return nc

            with ExitStack() as p3:
                sq = sb("sq1", [128, 8, TT], BF16, p3)
                h1 = sb("h1", [128, 8, TT], BF16, p3)
                rs = sb("rs1", [128, TT], F32, p3)
                wst = [sb(f"w1st{i}", [128, 4096], BF16, p3) for i in range(2)]
                uT = sb("uT", [128, 16, TT], BF16, p3)
                gsT = sb("gsT", [128, 16, TT], BF16, p3)
                gv = sb("gv", [128, 4, 2048], BF16, p3)
                vtmp = sb("vtmp", [128, 2048], F32, p3)
                lng = sb("lng", [128, 2, 2048], F32, p3)
                wsf = sb("wsf", [128, 8, 128], BF16, p3)
                bsb = sb("bsb_sb", [128, 8, 128], F32, p3)
                stats = sb("stats", [128, 4, 6], F32, p3)
                mv = sb("mv", [128, 4], F32, p3)
                tf = [sb(f"tg{i}", [128, TT], F32, p3) for i in range(3)]
                wsb = [0]

                def load_w1(src_ap, shape3):
                    b = wsb[0] % 2
                    wsb[0] += 1
                    a, bb = shape3
                    view = wst[b][:, 0:a * bb].rearrange("p (a b) -> p a b", b=bb)
                    sc.dma("sp", f"w1st{b}", view, src_ap, writes=[f"w1st{b}"])
                    return view, f"w1st{b}"

                sc.dma("sp", "lng", lng[:, 0, :], lngb_d[0:1, :].partition_broadcast(128), writes=["lng"])
                sc.dma("sp", "lng", lng[:, 1, :], lngb_d[1:2, :].partition_broadcast(128), writes=["lng"])
                sc.dma("pool", "wsf", wsf[:].rearrange("p g t -> p (g t)"), wsT_d, writes=["wsf"])
                sc.dma("sp", "bsb", bsb[:].rearrange("p g t -> p (g t)"), bsb_d.partition_broadcast(128),
                       writes=["bsb"])
                for g in range(8):
                    sc.op("pool", lambda g=g: nc.gpsimd.tensor_tensor(out=wsf[:, g, :], in0=wsf[:, g, :], in1=tri,
                                                                      op=ALU.mult),
                          reads=["wsf", "cmat"], writes=["wsf"])

                rmsnorm_tile(x1, "x1", C_SGUN, sq, "sq1", h1, "h1", rs, "rs1")
                for wc in range(4):
                    w, wk = load_w1(W1v[:, :, wc * 512:(wc + 1) * 512], (8, 512))
                    for s4 in range(4):
                        pi = ps_next()
                        proj_fm(pi, h1, "h1", w, wk, s4 * 128)
                        c = wc * 4 + s4
                        sc.op("act", lambda c=c: nc.scalar.activation(out=uT[:, c, :], in_=psum[pi][:], func=AF.Gelu),
                              reads=[f"ps{pi}"], writes=["uT"])
                for wc in range(4):
                    w, wk = load_w1(W1v[:, :, 2048 + wc * 512:2048 + (wc + 1) * 512], (8, 512))
                    for sub in range(4):
                        pi = ps_next()
                        for k in range(8):
                            sc.op("pe", lambda k=k: nc.tensor.matmul(
                                psum[pi][:], lhsT=h1[:, k, sub * 128:(sub + 1) * 128], rhs=w[:, k, :],
                                start=(k == 0), stop=(k == 7)), reads=["h1", wk], writes=[f"ps{pi}"])
                        sc.op("act", lambda: nc.scalar.activation(out=gv[:, sub, wc * 512:(wc + 1) * 512],
                                                                  in_=psum[pi][:], func=AF.Gelu),
                              reads=[f"ps{pi}"], writes=[f"gv{sub}"])
                for wc in range(4):
                    w, wk = load_w1(W1v[:, :, 4096 + wc * 512:4096 + (wc + 1) * 512], (8, 512))
                    for s4 in range(4):
                        pi = ps_next()
                        proj_fm(pi, h1, "h1", w, wk, s4 * 128)
                        c = wc * 4 + s4
                        sc.op("act", lambda c=c: nc.scalar.activation(out=gsT[:, c, :], in_=psum[pi][:], func=AF.Silu),
                              reads=[f"ps{pi}"], writes=["gsT"])
                for sub in range(4):
                    for q in range(4):
                        sc.op("dve", lambda q=q: nc.vector.bn_stats(out=stats[:, q, :],
                                                                    in_=gv[:, sub, q * 512:(q + 1) * 512]),
                              reads=[f"gv{sub}"], writes=["stats"])
                    sc.op("dve", lambda: nc.vector.bn_aggr(out=mv[:, 0:2], in_=stats[:].rearrange("p a b -> p (a b)")),
                          reads=["stats"], writes=["mv"])
                    sc.op("act", lambda: nc.scalar.activation(out=mv[:, 2:3], in_=mv[:, 1:2], func=AF.Ln,
                                                              bias=ccol(C_EPSL)),
                          reads=["mv", "cst"], writes=["mv"])
                    sc.op("act", lambda: nc.scalar.activation(out=mv[:, 3:4], in_=mv[:, 2:3], func=AF.Exp, scale=-0.5),
                          reads=["mv"], writes=["mv"])
                    sc.op("dve", lambda: nc.vector.tensor_scalar(
                        out=vtmp[:], in0=gv[:, sub, :], scalar1=mv[:, 0:1], scalar2=mv[:, 3:4],
                        op0=ALU.subtract, op1=ALU.mult), reads=[f"gv{sub}", "mv"], writes=["vtmp"])
                    sc.op("pool", lambda: nc.gpsimd.tensor_tensor(out=vtmp[:], in0=vtmp[:], in1=lng[:, 0, :],
                                                                  op=ALU.mult),
                          reads=["vtmp", "lng"], writes=["vtmp"])
                    sc.op("pool", lambda: nc.gpsimd.tensor_tensor(out=gv[:, sub, :], in0=vtmp[:], in1=lng[:, 1, :],
                                                                  op=ALU.add),
                          reads=["vtmp", "lng"], writes=[f"gv{sub}"])
                for c in range(16):
                    g = c // 2
                    pi = ps_next()
                    for sub in range(4):
                        sc.op("pe", lambda sub=sub: nc.tensor.matmul(
                            psum[pi][:, sub * 128:(sub + 1) * 128], lhsT=gv[:, sub, c * 128:(c + 1) * 128],
                            rhs=wsf[:, g, :], start=True, stop=True),
                            reads=[f"gv{sub}", "wsf"], writes=[f"ps{pi}"])
                    tb = c % 3
                    for sub in range(4):
                        sc.op("dve", lambda sub=sub: nc.vector.tensor_tensor(
                            out=tf[tb][:, sub * 128:(sub + 1) * 128], in0=psum[pi][:, sub * 128:(sub + 1) * 128],
                            in1=bsb[:, g, :], op=ALU.add), reads=[f"ps{pi}", "bsb"], writes=[f"tg{tb}"])
                    sc.op("pool", lambda: nc.gpsimd.tensor_tensor(out=gsT[:, c, :], in0=gsT[:, c, :], in1=uT[:, c, :],
                                                                  op=ALU.mult),
                          reads=["gsT", "uT"], writes=["gsT"])
                    sc.op("dve", lambda: nc.vector.tensor_tensor(out=uT[:, c, :], in0=tf[tb][:], in1=gsT[:, c, :],
                                                                 op=ALU.mult),
                          reads=[f"tg{tb}", "gsT"], writes=["uT"])
                for q4 in range(4):
                    w, wk = load_w1(Wo1v[:, :, q4 * 256:(q4 + 1) * 256], (16, 256))
                    for d2 in range(2):
                        dc = q4 * 2 + d2
                        pi = ps_next()
                        for c in range(16):
                            sc.op("pe", lambda c=c: nc.tensor.matmul(
                                psum[pi][:], lhsT=w[:, c, d2 * 128:(d2 + 1) * 128], rhs=uT[:, c, :],
                                start=(c == 0), stop=(c == 15)), reads=[wk, "uT"], writes=[f"ps{pi}"])
                        tb = dc % 3
                        sc.op("dve", lambda: nc.vector.tensor_tensor(out=tf[tb][:], in0=psum[pi][:], in1=x1[:, dc, :],
                                                                     op=ALU.add),
                              reads=[f"ps{pi}", "x1"], writes=[f"tg{tb}"])
                        sc.dma("sp", f"tg{tb}", out_d[dc * 128:(dc + 1) * 128, j * TT:(j + 1) * TT], tf[tb][:],
                               reads=[f"tg{tb}"], writes=["out"])
                sc.barrier()
            if stop == "p3":
                return nc
    return nc


def _tile_order(c):
    order = []
    for j in range(NSLOT):
        order += [8 * j + i for i in range(c + 1, 8)] + [8 * j + i for i in range(0, c)] + [8 * j + c]
    return order


def _rope_tables(positions):
    rot = 16
    inv = 1.0 / (500000.0 ** (np.arange(0, rot, 2, dtype=np.float32) / rot))
    ang = positions.astype(np.float32)[:, None] * inv[None, :].astype(np.float32)
    cos, sin = np.cos(ang).astype(np.float32), np.sin(ang).astype(np.float32)
    T = positions.shape[0]
    cf = np.ones((128, T), np.float32)
    sf = np.zeros((128, T), np.float32)
    for hb in (0, 64):
        cf[hb:hb + 8] = cos.T
        cf[hb + 8:hb + 16] = cos.T
        sf[hb:hb + 8] = sin.T
        sf[hb + 8:hb + 16] = sin.T
    return np.stack([cf, sf], 0)


def _const_mats():
    cm = np.zeros((128, NCM), np.float32)
    cm[:, M_ONES:M_ONES + 128] = 1.0
    cm[0:64, M_BONES:M_BONES + 64] = 1.0
    cm[64:128, M_BONES + 64:M_BONES + 128] = 1.0
    for m in range(128):
        dl = m % 64
        if dl < 8:
            cm[m + 8, M_ROT + m] = -1.0
        elif dl < 16:
            cm[m - 8, M_ROT + m] = 1.0
    kk = np.arange(128)[:, None]
    qq = np.arange(128)[None, :]
    cm[:, M_TRI:M_TRI + 128] = (kk <= qq).astype(np.float32)
    for idx in range(23):
        delta = idx - 3
        d = 128 * delta + qq - kk
        m = ((d >= 0) & (d <= 128)).astype(np.float32)
        m += ((d % 4 == 0) & (d >= 0) & (d <= 512)).astype(np.float32)
        m += ((d % 16 == 0) & (d >= 0) & (d <= 2048)).astype(np.float32)
        cm[:, M_DSW + idx * 128:M_DSW + (idx + 1) * 128] = m
    return cm.astype(ml_dtypes.bfloat16)


_NC_CACHE = {}


def _prep(x, att_norm, att_w_in, dsw_q_norm, dsw_k_norm, diff_q_norm, diff_k_norm,
           diff_lam_q1, diff_lam_k1, diff_lam_q2, diff_lam_k2, diff_subln, att_w_out,
           sgu_norm, sgu_w_in, sgu_ln_g, sgu_ln_b, sgu_w_s, sgu_b_s, sgu_w_out):
    f = np.float32
    xT = np.ascontiguousarray(np.asarray(x, f)[0].T)
    xT_tiles = xT.reshape(D, NT, TT)
    cm = _const_mats()
    w_in0 = np.ascontiguousarray(np.asarray(att_w_in, f)[0])
    w_out0 = np.ascontiguousarray(np.asarray(att_w_out, f)[0])
    w_in1 = np.ascontiguousarray(np.asarray(sgu_w_in, f)[0])
    w_out1 = np.ascontiguousarray(np.asarray(sgu_w_out, f)[0])
    lngb = np.ascontiguousarray(np.stack([np.asarray(sgu_ln_g, f)[0], np.asarray(sgu_ln_b, f)[0]], 0))
    wsT = np.ascontiguousarray(np.asarray(sgu_w_s, f)[0].transpose(2, 0, 1).reshape(128, 8 * 128))
    bsb = np.ascontiguousarray(np.asarray(sgu_b_s, f)[0].reshape(1, 8 * 128))
    in_maps = []
    orders = []
    for c in range(NCORE):
        order = _tile_order(c)
        orders.append(order)
        x_all = np.ascontiguousarray(xT_tiles[:, order, :].reshape(D, S))
        pos_all = (np.asarray(order)[:, None] * TT + np.arange(TT)[None, :]).reshape(-1)
        halo_tiles = []
        for j in range(NSLOT):
            halo_tiles += [8 * j + c - NHALO + m for m in range(NHALO)]
        x_halo = np.zeros((D, NSLOT * NHALO, TT), f)
        pos_halo = np.zeros((NSLOT * NHALO, TT), np.int64)
        for i, t in enumerate(halo_tiles):
            if t >= 0:
                x_halo[:, i, :] = xT_tiles[:, t, :]
                pos_halo[i] = t * TT + np.arange(TT)
        cst = np.zeros((128, NCST), f)
        cst[:, C_ATTN:C_ATTN + 8] = np.asarray(att_norm, f)[0].reshape(8, 128).T
        cst[:, C_SGUN:C_SGUN + 8] = np.asarray(sgu_norm, f)[0].reshape(8, 128).T
        cst[:, C_DSWQ] = np.tile(np.asarray(dsw_q_norm, f)[0], 2)
        cst[:, C_DSWK] = np.tile(np.asarray(dsw_k_norm, f)[0], 2)
        cst[:, C_DIFQ] = np.tile(np.asarray(diff_q_norm, f)[0], 2)
        cst[:, C_DIFK] = np.tile(np.asarray(diff_k_norm, f)[0], 2)
        cst[:, C_SUBLN] = np.asarray(diff_subln, f)[0]
        for i in range(8):
            cst[:, C_CB + i] = 0.0 if (i >= 7 - c) else NEG
        for j in range(NSLOT):
            for m in range(NHALO):
                cst[:, C_HB + 4 * j + m] = 0.0 if (8 * j + c - NHALO + m) >= 0 else NEG
        cst[:, C_LQ1:C_LQ1 + 64] = np.asarray(diff_lam_q1, f)[0][None, :]
        cst[:, C_LK1:C_LK1 + 64] = np.asarray(diff_lam_k1, f)[0][None, :]
        cst[:, C_LQ2:C_LQ2 + 64] = np.asarray(diff_lam_q2, f)[0][None, :]
        cst[:, C_LK2:C_LK2 + 64] = np.asarray(diff_lam_k2, f)[0][None, :]
        cst[:, C_EPSR] = RMS_EPS
        cst[:, C_EPSL] = LN_EPS
        cst[:, C_ONE] = 1.0
        in_maps.append({
            "x_all": x_all,
            "x_halo": np.ascontiguousarray(x_halo.reshape(D, NSLOT * NHALO * TT)),
            "rope_all": _rope_tables(pos_all),
            "rope_halo": _rope_tables(pos_halo.reshape(-1)),
            "w_in0": w_in0, "w_out0": w_out0, "w_in1": w_in1, "w_out1": w_out1,
            "cst": cst, "cmat": cm, "lngb": lngb, "wsT": wsT, "bsb": bsb,
        })
    return in_maps


def kernel(**inputs):
    f = np.float32
    in_maps = _prep(**inputs)
    if "nc" not in _NC_CACHE:
        _NC_CACHE["nc"] = build_program()
    nc = _NC_CACHE["nc"]
    res = run_bass_kernel_spmd(nc, in_maps, core_ids=list(range(NCORE)))
    out = np.zeros((S, D), f)
    for c in range(NCORE):
        oT = np.asarray(res.results[c]["outT"], f)
        for j in range(NSLOT):
            t = 8 * j + c
            out[t * TT:(t + 1) * TT, :] = oT[:, j * TT:(j + 1) * TT].T
    return out[None]
```

```python
import math
from contextlib import ExitStack

import ml_dtypes
import numpy as np

import concourse.bass as bass
import concourse.mybir as mybir
from concourse.bass_utils import run_bass_kernel_spmd

F32 = mybir.dt.float32
BF16 = mybir.dt.bfloat16
AF = mybir.ActivationFunctionType
ALU = mybir.AluOpType

S = 16384
D = 1024
TT = 512
NT = S // TT
NCORE = 8
NSLOT = NT // NCORE
NHALO = 4
NEG = -30000.0
RMS_EPS = 1e-6
LN_EPS = 1e-5
LAM_INIT0 = 0.8 - 0.6 * math.exp(-0.3 * 0)

C_ATTN = 0
C_SGUN = 8
C_DSWQ = 16
C_DSWK = 17
C_DIFQ = 18
C_DIFK = 19
C_SUBLN = 20
C_CB = 21
C_HB = 29
C_ZERO = 45
C_LQ1 = 46
C_LK1 = 110
C_LQ2 = 174
C_LK2 = 238
C_EPSR = 302
C_EPSL = 303
C_ONE = 304
NCST = 305
M_ONES = 0
M_BONES = 128
M_ROT = 256
M_TRI = 384
M_DSW = 512
NCM = 512 + 23 * 128


class Sched:
    def __init__(self, nc, es):
        self.nc = nc
        self.es = es
        self.E = {"pe": nc.tensor, "act": nc.scalar, "dve": nc.vector, "pool": nc.gpsimd, "sp": nc.sync}
        self.semh = {}
        self.cnt = {}
        for e in self.E:
            self.semh[e] = es.enter_context(nc.semaphore("s_" + e))
            self.cnt[e] = 0
        self.known = {e: {} for e in self.E}
        self.lastw = {}
        self.readers = {}
        self.nwait = 0

    def dsem(self, name):
        k = "d_" + name
        if k not in self.semh:
            self.semh[k] = self.es.enter_context(self.nc.semaphore(k))
            self.cnt[k] = 0
        return k

    def _wait(self, eng, deps):
        need = {}
        for (sk, val) in deps:
            if sk == eng and eng in ("pe", "sp"):
                continue
            if val > need.get(sk, 0):
                need[sk] = val
        for sk, val in need.items():
            if self.known[eng].get(sk, 0) >= val:
                continue
            self.E[eng].wait_ge(self.semh[sk], val)
            self.known[eng][sk] = val
            self.nwait += 1

    def _deps(self, reads, writes):
        deps = []
        for k in reads:
            if k in self.lastw:
                deps.append(self.lastw[k])
        for k in writes:
            if k in self.lastw:
                deps.append(self.lastw[k])
            deps.extend(self.readers.get(k, {}).items())
        return deps

    def _record(self, ev, reads, writes):
        for k in reads:
            r = self.readers.setdefault(k, {})
            if ev[1] > r.get(ev[0], 0):
                r[ev[0]] = ev[1]
        for k in writes:
            self.lastw[k] = ev
            self.readers[k] = {}

    def op(self, eng, fn, reads=(), writes=()):
        self._wait(eng, self._deps(reads, writes))
        ins = fn()
        self.cnt[eng] += 1
        ins.then_inc(self.semh[eng], 1)
        self._record((eng, self.cnt[eng]), reads, writes)

    def dma(self, q, sem, out, in_, reads=(), writes=()):
        sk = self.dsem(sem)
        self._wait(q, self._deps(reads, writes))
        ins = self.E[q].dma_start(out=out, in_=in_)
        self.cnt[sk] += 16
        ins.then_inc(self.semh[sk], 16)
        self._record((sk, self.cnt[sk]), reads, writes)

    def barrier(self):
        for e in self.E:
            for sk, c in self.cnt.items():
                if sk == e and e in ("pe", "sp"):
                    continue
                if c > 0 and self.known[e].get(sk, 0) < c:
                    self.E[e].wait_ge(self.semh[sk], c)
                    self.known[e][sk] = c
        self.lastw = {}
        self.readers = {}


def build_program(stop=None):
    nc = bass.Bass("TRN2", target_bir_lowering=False)
    x_all = nc.dram_tensor("x_all", [D, S], F32, kind="ExternalInput").ap()
    x_halo = nc.dram_tensor("x_halo", [D, NSLOT * NHALO * TT], F32, kind="ExternalInput").ap()
    rope_all = nc.dram_tensor("rope_all", [2, 128, S], F32, kind="ExternalInput").ap()
    rope_halo = nc.dram_tensor("rope_halo", [2, 128, NSLOT * NHALO * TT], F32, kind="ExternalInput").ap()
    w_in0 = nc.dram_tensor("w_in0", [D, 4096], F32, kind="ExternalInput").ap()
    w_out0 = nc.dram_tensor("w_out0", [D, D], F32, kind="ExternalInput").ap()
    w_in1 = nc.dram_tensor("w_in1", [D, 6144], F32, kind="ExternalInput").ap()
    w_out1 = nc.dram_tensor("w_out1", [2048, D], F32, kind="ExternalInput").ap()
    cst_d = nc.dram_tensor("cst", [128, NCST], F32, kind="ExternalInput").ap()
    cmat_d = nc.dram_tensor("cmat", [128, NCM], BF16, kind="ExternalInput").ap()
    lngb_d = nc.dram_tensor("lngb", [2, 2048], F32, kind="ExternalInput").ap()
    wsT_d = nc.dram_tensor("wsT", [128, 8 * 128], F32, kind="ExternalInput").ap()
    bsb_d = nc.dram_tensor("bsb", [1, 8 * 128], F32, kind="ExternalInput").ap()
    out_d = nc.dram_tensor("outT", [D, NSLOT * TT], F32, kind="ExternalOutput").ap()
    W0b = nc.dram_tensor("W0b", [D, 4096], BF16).ap()
    Wo0b = nc.dram_tensor("Wo0b", [D, D], BF16).ap()
    W1b = nc.dram_tensor("W1b", [D, 6144], BF16).ap()
    Wo1b = nc.dram_tensor("Wo1b", [2048, D], BF16).ap()
    KD = nc.dram_tensor("KD", [4, 128, S], BF16).ap()
    VD = nc.dram_tensor("VD", [S, 512], BF16).ap()
    NKA = NSLOT * (NHALO + 1) * TT
    KA = nc.dram_tensor("KA", [4, 128, NKA], BF16).ap()
    VA = nc.dram_tensor("VA", [NKA, 512], BF16).ap()
    QA = nc.dram_tensor("QA", [4, 128, NSLOT * TT], BF16).ap()
    QD = nc.dram_tensor("QD", [4, 128, NSLOT * TT], BF16).ap()
    GA = nc.dram_tensor("GA", [8, 64, NSLOT * TT], BF16).ap()
    GD = nc.dram_tensor("GD", [4, 128, NSLOT * TT], BF16).ap()

    with ExitStack() as es:
        es.enter_context(nc.allow_low_precision("bf16 matmul operands, fp32 accumulation"))
        es.enter_context(nc.allow_non_contiguous_dma("strided weight / scratch tiles"))
        sc = Sched(nc, es)

        uid = [0]

        def sb(name, shape, dt, stack=es):
            uid[0] += 1
            return stack.enter_context(nc.sbuf_tensor(f"{name}_{uid[0]}", shape, dt))

        cst = sb("cst_sb", [128, NCST], F32)
        cmat = sb("cmat_sb", [128, NCM], BF16)
        lamc = sb("lamc", [128, 8], F32)
        x1 = sb("x1", [128, 8, TT], F32)
        onesf = sb("onesf", [128, 128], F32)
        psum = [es.enter_context(nc.psum_tensor(f"ps{i}", [128, TT], F32)) for i in range(8)]
        ps_rr = [0]

        def ps_next(lo=0, hi=8):
            i = lo + (ps_rr[0] % (hi - lo))
            ps_rr[0] += 1
            return i

        ones = cmat[:, M_ONES:M_ONES + 128]
        bones = cmat[:, M_BONES:M_BONES + 128]
        rotm = cmat[:, M_ROT:M_ROT + 128]
        tri = cmat[:, M_TRI:M_TRI + 128]

        def ccol(c, n=128):
            return cst[0:n, c:c + 1]

        sc.dma("sp", "cst", cst[:], cst_d, writes=["cst"])
        sc.dma("sp", "cmat", cmat[:], cmat_d, writes=["cmat"])
        for (src, dst, rows, nm) in ((w_in0, W0b, D, "W0b"), (w_out0, Wo0b, D, "Wo0b"),
                                     (w_in1, W1b, D, "W1b"), (w_out1, Wo1b, 2048, "Wo1b")):
            step = 128
            for r0 in range(0, rows, step):
                sc.dma("pool", "wcast", dst[r0:r0 + step, :], src[r0:r0 + step, :], writes=[nm])
                done = sc.cnt["d_wcast"] - 32
                if done > 0:
                    nc.gpsimd.wait_ge(sc.semh["d_wcast"], done)
                    sc.known["pool"]["d_wcast"] = done
        sc.op("dve", lambda: nc.vector.memset(onesf[:], 1.0), writes=["onesf"])
        with ExitStack() as ls:
            lt = sb("lam_t", [128, 128], F32, ls)
            for i, (ca, cb) in enumerate(((C_LQ1, C_LK1), (C_LQ2, C_LK2))):
                sc.op("dve", lambda ca=ca, cb=cb, i=i: nc.vector.tensor_tensor(
                    out=lt[:, i * 64:(i + 1) * 64], in0=cst[:, ca:ca + 64], in1=cst[:, cb:cb + 64], op=ALU.mult),
                    reads=["cst"], writes=["lt"])
            for i in range(2):
                sc.op("dve", lambda i=i: nc.vector.reduce_sum(
                    out=lamc[:, 2 + i:3 + i], in_=lt[:, i * 64:(i + 1) * 64], axis=mybir.AxisListType.X),
                    reads=["lt"], writes=["lamc"])
            sc.op("act", lambda: nc.scalar.activation(out=lamc[:, 4:6], in_=lamc[:, 2:4], func=AF.Exp),
                  reads=["lamc"], writes=["lamc"])
            sc.op("dve", lambda: nc.vector.scalar_tensor_tensor(
                out=lamc[:, 0:1], in0=lamc[:, 5:6], scalar=-LAM_INIT0, in1=lamc[:, 4:5],
                op0=ALU.add, op1=ALU.subtract), reads=["lamc"], writes=["lamc"])
            sc.op("dve", lambda: nc.vector.tensor_scalar(
                out=lamc[:, 1:2], in0=cst[:, C_SUBLN:C_SUBLN + 1], scalar1=1.0 - LAM_INIT0, scalar2=None,
                op0=ALU.mult), reads=["cst", "lamc"], writes=["lamc"])
            sc.barrier()

        def rmsnorm_tile(xin, xk, gcol0, sq, sqk, hT, hk, rs, rsk):
            sc.op("act", lambda: nc.scalar.activation(out=sq[:], in_=xin[:], func=AF.Square),
                  reads=[xk], writes=[sqk])
            pi = ps_next()
            for k in range(8):
                sc.op("pe", lambda k=k: nc.tensor.matmul(psum[pi][:], lhsT=ones, rhs=sq[:, k, :],
                                                       start=(k == 0), stop=(k == 7)),
                      reads=[sqk, "cmat"], writes=[f"ps{pi}"])
            sc.op("act", lambda: nc.scalar.activation(out=rs[:], in_=psum[pi][:], func=AF.Ln,
                                                      scale=1.0 / D, bias=ccol(C_EPSR)),
                  reads=[f"ps{pi}", "cst"], writes=[rsk])
            sc.op("act", lambda: nc.scalar.activation(out=rs[:], in_=rs[:], func=AF.Exp, scale=-0.5),
                  reads=[rsk], writes=[rsk])
            for k in range(8):
                sc.op("dve", lambda k=k: nc.vector.scalar_tensor_tensor(
                    out=hT[:, k, :], in0=xin[:, k, :], scalar=cst[:, gcol0 + k:gcol0 + k + 1], in1=rs[:],
                    op0=ALU.mult, op1=ALU.mult), reads=[xk, rsk, "cst"], writes=[hk])

        def proj_fm(pi, hT, hk, w, wk, c0, m=128, prow=128):
            for k in range(8):
                sc.op("pe", lambda k=k: nc.tensor.matmul(psum[pi][0:m, :], lhsT=w[:, k, c0:c0 + m], rhs=hT[:, k, :],
                                                       start=(k == 0), stop=(k == 7)),
                      reads=[hk, wk], writes=[f"ps{pi}"])

        if stop == "setup":
            return nc
        with ExitStack() as p1:
            xin = [sb(f"xin{i}", [128, 8, TT], F32, p1) for i in range(2)]
            sq = [sb(f"sq{i}", [128, 8, TT], BF16, p1) for i in range(2)]
            hTs = [sb(f"hT{i}", [128, 8, TT], BF16, p1) for i in range(2)]
            rs = [sb(f"rs{i}", [128, TT], F32, p1) for i in range(2)]
            wkvd = sb("wkvd", [128, 8, 1024], BF16, p1)
            wkva = sb("wkva", [128, 8, 1024], BF16, p1)
            wst = [sb(f"wst{i}", [128, 8, 512], BF16, p1) for i in range(2)]
            rope = [sb(f"rope{i}", [128, 2, TT], F32, p1) for i in range(2)]
            NPB = 3
            sqc = [sb(f"sqc{i}", [128, TT], BF16, p1) for i in range(NPB)]
            qa_ = [sb(f"qA{i}", [128, TT], BF16, p1) for i in range(NPB)]
            rsc = [sb(f"rsc{i}", [128, TT], F32, p1) for i in range(NPB)]
            t1 = [sb(f"t1{i}", [128, TT], F32, p1) for i in range(NPB)]
            t2 = [sb(f"t2{i}", [128, TT], F32, p1) for i in range(NPB)]
            ko = [sb(f"ko{i}", [128, TT], BF16, p1) for i in range(NPB)]
            vst = [sb(f"vst{i}", [128, TT], BF16, p1) for i in range(2)]
            pb = [0]
            vb = [0]
            wsb = [0]

            W0v = W0b.rearrange("(k p) e -> p k e", p=128)
            sc.dma("sp", "wkvd", wkvd[:, :, 0:512], W0v[:, :, 2560:3072], reads=["W0b"], writes=["wkvd"])
            sc.dma("sp", "wkvd", wkvd[:, :, 512:1024], W0v[:, :, 3072:3584], reads=["W0b"], writes=["wkvd"])
            sc.dma("sp", "wkva", wkva[:, :, 0:512], W0v[:, :, 512:1024], reads=["W0b"], writes=["wkva"])
            sc.dma("sp", "wkva", wkva[:, :, 512:1024], W0v[:, :, 1024:1536], reads=["W0b"], writes=["wkva"])

            def qk_post(pi, gcol, ropet, ropek, dst_ap, dkey):
                b = pb[0] % NPB
                pb[0] += 1
                sc.op("act", lambda: nc.scalar.activation(out=sqc[b][:], in_=psum[pi][:], func=AF.Square),
                      reads=[f"ps{pi}"], writes=[f"sqc{b}"])
                sc.op("act", lambda: nc.scalar.activation(out=qa_[b][:], in_=psum[pi][:], func=AF.Identity,
                                                          scale=ccol(gcol)),
                      reads=[f"ps{pi}", "cst"], writes=[f"qA{b}"])
                p2 = ps_next()
                sc.op("pe", lambda: nc.tensor.matmul(psum[p2][:], lhsT=bones, rhs=sqc[b][:], start=True, stop=True),
                      reads=[f"sqc{b}", "cmat"], writes=[f"ps{p2}"])
                p3 = ps_next()
                sc.op("pe", lambda: nc.tensor.matmul(psum[p3][:], lhsT=rotm, rhs=qa_[b][:], start=True, stop=True),
                      reads=[f"qA{b}", "cmat"], writes=[f"ps{p3}"])
                sc.op("act", lambda: nc.scalar.activation(out=rsc[b][:], in_=psum[p2][:], func=AF.Ln,
                                                          scale=1.0 / 64, bias=ccol(C_EPSR)),
                      reads=[f"ps{p2}", "cst"], writes=[f"rsc{b}"])
                sc.op("act", lambda: nc.scalar.activation(out=rsc[b][:], in_=rsc[b][:], func=AF.Exp, scale=-0.5),
                      reads=[f"rsc{b}"], writes=[f"rsc{b}"])
                sc.op("pool", lambda: nc.gpsimd.tensor_tensor(out=t1[b][:], in0=qa_[b][:], in1=ropet[:, 0, :],
                                                              op=ALU.mult),
                      reads=[f"qA{b}", ropek], writes=[f"t1{b}"])
                sc.op("dve", lambda: nc.vector.tensor_tensor(out=t2[b][:], in0=psum[p3][:], in1=ropet[:, 1, :],
                                                             op=ALU.mult),
                      reads=[f"ps{p3}", ropek], writes=[f"t2{b}"])
                sc.op("pool", lambda: nc.gpsimd.tensor_tensor(out=t1[b][:], in0=t1[b][:], in1=t2[b][:], op=ALU.add),
                      reads=[f"t1{b}", f"t2{b}"], writes=[f"t1{b}"])
                sc.op("dve", lambda: nc.vector.tensor_tensor(out=ko[b][:], in0=t1[b][:], in1=rsc[b][:], op=ALU.mult),
                      reads=[f"t1{b}", f"rsc{b}"], writes=[f"ko{b}"])
                sc.dma("sp", f"ko{b}", dst_ap, ko[b][:], reads=[f"ko{b}"], writes=[dkey])

            def v_tm(hT, hk, w, wk, c0, dst_rows_fn, dkey):
                for sub in range(4):
                    pi = ps_next()
                    for k in range(8):
                        sc.op("pe", lambda k=k: nc.tensor.matmul(
                            psum[pi][:], lhsT=hT[:, k, sub * 128:(sub + 1) * 128], rhs=w[:, k, c0:c0 + 512],
                            start=(k == 0), stop=(k == 7)), reads=[hk, wk], writes=[f"ps{pi}"])
                    b = vb[0] % 2
                    vb[0] += 1
                    sc.op("act", lambda: nc.scalar.activation(out=vst[b][:], in_=psum[pi][:], func=AF.Copy),
                          reads=[f"ps{pi}"], writes=[f"vst{b}"])
                    sc.dma("sp", f"vst{b}", dst_rows_fn(sub), vst[b][:], reads=[f"vst{b}"], writes=[dkey])

            def silu_fm(pi, m, dst_ap, dkey):
                b = pb[0] % NPB
                pb[0] += 1
                sc.op("act", lambda: nc.scalar.activation(out=ko[b][0:m, :], in_=psum[pi][0:m, :], func=AF.Silu),
                      reads=[f"ps{pi}"], writes=[f"ko{b}"])
                sc.dma("sp", f"ko{b}", dst_ap, ko[b][0:m, :], reads=[f"ko{b}"], writes=[dkey])

            def load_wst(c0):
                b = wsb[0] % 2
                wsb[0] += 1
                sc.dma("sp", f"wst{b}", wst[b][:], W0v[:, :, c0:c0 + 512], reads=["W0b"], writes=[f"wst{b}"])
                return wst[b], f"wst{b}"

            tiles = [("all", pos) for pos in range(NT)] + [("halo", i) for i in range(NSLOT * NHALO)]
            if stop == "p1a":
                tiles = tiles[:8]
            xav = x_all.rearrange("(k p) t -> p k t", p=128)
            xhv = x_halo.rearrange("(k p) t -> p k t", p=128)

            def load_x(ti):
                kind, idx = tiles[ti]
                xb = ti % 2
                srcx = xav if kind == "all" else xhv
                sc.dma("sp", f"xin{xb}", xin[xb][:], srcx[:, :, idx * TT:(idx + 1) * TT], writes=[f"xin{xb}"])

            def load_rope(ti):
                kind, idx = tiles[ti]
                xb = ti % 2
                srcr = rope_all if kind == "all" else rope_halo
                sc.dma("sp", f"rope{xb}", rope[xb][:],
                       srcr.rearrange("c p t -> p c t")[:, :, idx * TT:(idx + 1) * TT], writes=[f"rope{xb}"])

            def front(ti):
                xb = ti % 2
                rmsnorm_tile(xin[xb], f"xin{xb}", C_ATTN, sq[xb], f"sq{xb}", hTs[xb], f"hT{xb}", rs[xb], f"rs{xb}")

            load_x(0)
            load_rope(0)
            front(0)
            if len(tiles) > 1:
                load_x(1)
            for ti, (kind, idx) in enumerate(tiles):
                xb = ti % 2
                xk = f"xin{xb}"
                rk = f"rope{xb}"
                hT, hk = hTs[xb], f"hT{xb}"
                if ti + 1 < len(tiles):
                    load_rope(ti + 1)
                    front(ti + 1)
                if ti + 2 < len(tiles):
                    load_x(ti + 2)
                if kind == "all":
                    pos = idx
                    for ch in range(4):
                        pi = ps_next()
                        proj_fm(pi, hT, hk, wkvd, "wkvd", ch * 128)
                        qk_post(pi, C_DIFK, rope[xb], rk, KD[ch, :, pos * TT:(pos + 1) * TT], "KD")
                    v_tm(hT, hk, wkvd, "wkvd", 512, lambda sub, pos=pos: VD[(pos * 4 + sub) * 128:(pos * 4 + sub + 1) * 128, :],
                         "VD")
                    if pos % 8 == 7:
                        j = pos // 8
                        kpos = j * (NHALO + 1) + NHALO
                        for ch in range(4):
                            pi = ps_next()
                            proj_fm(pi, hT, hk, wkva, "wkva", ch * 128)
                            qk_post(pi, C_DSWK, rope[xb], rk, KA[ch, :, kpos * TT:(kpos + 1) * TT], "KA")
                        v_tm(hT, hk, wkva, "wkva", 512,
                             lambda sub, kpos=kpos: VA[(kpos * 4 + sub) * 128:(kpos * 4 + sub + 1) * 128, :], "VA")
                        w, wk = load_wst(0)
                        for ch in range(4):
                            pi = ps_next()
                            proj_fm(pi, hT, hk, w, wk, ch * 128)
                            qk_post(pi, C_DSWQ, rope[xb], rk, QA[ch, :, j * TT:(j + 1) * TT], "QA")
                        w, wk = load_wst(1536)
                        for h in range(8):
                            pi = ps_next()
                            proj_fm(pi, hT, hk, w, wk, h * 64, m=64)
                            silu_fm(pi, 64, GA[h, :, j * TT:(j + 1) * TT], "GA")
                        w, wk = load_wst(2048)
                        for ch in range(4):
                            pi = ps_next()
                            proj_fm(pi, hT, hk, w, wk, ch * 128)
                            qk_post(pi, C_DIFQ, rope[xb], rk, QD[ch, :, j * TT:(j + 1) * TT], "QD")
                        w, wk = load_wst(3584)
                        for ch in range(4):
                            pi = ps_next()
                            proj_fm(pi, hT, hk, w, wk, ch * 128)
                            silu_fm(pi, 128, GD[ch, :, j * TT:(j + 1) * TT], "GD")
                else:
                    j, m = idx // NHALO, idx % NHALO
                    kpos = j * (NHALO + 1) + m
                    for ch in range(4):
                        pi = ps_next()
                        proj_fm(pi, hT, hk, wkva, "wkva", ch * 128)
                        qk_post(pi, C_DSWK, rope[xb], rk, KA[ch, :, kpos * TT:(kpos + 1) * TT], "KA")
                    v_tm(hT, hk, wkva, "wkva", 512,
                         lambda sub, kpos=kpos: VA[(kpos * 4 + sub) * 128:(kpos * 4 + sub + 1) * 128, :], "VA")
            sc.barrier()

        if stop in ("p1", "p1a"):
            return nc
        Wo0v_a = Wo0b[0:512, :].rearrange("(h d) e -> d h e", d=64)
        Wo0v_d = Wo0b[512:1024, :].rearrange("(h d) e -> d h e", d=128)
        VDv = VD.rearrange("(n p) e -> p n e", p=128)
        VAv = VA.rearrange("(n p) e -> p n e", p=128)
        W1v = W1b.rearrange("(k p) e -> p k e", p=128)
        Wo1v = Wo1b.rearrange("(c p) e -> p c e", p=128)
        xav = x_all.rearrange("(k p) t -> p k t", p=128)
        for j in range(NSLOT):
            own = 8 * j + 7
            with ExitStack() as p2:
                qaT = sb("qaT", [128, 4, TT], BF16, p2)
                qdT = sb("qdT", [128, 4, TT], BF16, p2)
                gaT = sb("gaT", [64, 8, TT], BF16, p2)
                gdT = sb("gdT", [128, 4, TT], BF16, p2)
                yaT = sb("yaT", [64, 8, TT], BF16, p2)
                ydT = sb("ydT", [128, 4, TT], BF16, p2)
                woa = sb("woa", [64, 8, D], BF16, p2)
                wod = sb("wod", [128, 4, D], BF16, p2)
                NKB = 3
                kbuf = [sb(f"kbuf{i}", [128, TT], BF16, p2) for i in range(NKB)]
                vbuf = [sb(f"vbuf{i}", [128, 4, 128], BF16, p2) for i in range(NKB)]
                kab = [sb(f"kab{i}", [128, (NHALO + 1) * TT], BF16, p2) for i in range(2)]
                vab = [sb(f"vab{i}", [128, (NHALO + 1) * 4, 128], BF16, p2) for i in range(2)]
                LAG = 2
                NPT = 2 * (LAG + 2)
                pT = [sb(f"pT{i}", [128, TT], BF16, p2) for i in range(NPT)]
                acc = [sb(f"acc{i}", [128, TT], F32, p2) for i in range(4)]
                tf = [sb(f"tf{i}", [128, TT], F32, p2) for i in range(8)]
                sqo = sb("sqo", [128, TT], BF16, p2)
                ptc = [0]

                sc.dma("sp", "qaT", qaT[:], QA.rearrange("c p t -> p c t")[:, :, j * TT:(j + 1) * TT], writes=["qaT"])
                sc.dma("sp", "qdT", qdT[:], QD.rearrange("c p t -> p c t")[:, :, j * TT:(j + 1) * TT], writes=["qdT"])
                sc.dma("sp", "gaT", gaT[:], GA.rearrange("c p t -> p c t")[:, :, j * TT:(j + 1) * TT], writes=["gaT"])
                sc.dma("sp", "gdT", gdT[:], GD.rearrange("c p t -> p c t")[:, :, j * TT:(j + 1) * TT], writes=["gdT"])
                sc.dma("sp", "woa", woa[:], Wo0v_a, writes=["woa"])
                sc.dma("sp", "wod", wod[:], Wo0v_d, writes=["wod"])
                sc.dma("sp", "x1", x1[:], xav[:, :, own * TT:(own + 1) * TT], writes=["x1"])

                def attn_pipeline(steps, emit_front, emit_back):
                    pend = []
                    for st in steps:
                        pend.append(emit_front(st))
                        if len(pend) > LAG:
                            emit_back(pend.pop(0))
                    while pend:
                        emit_back(pend.pop(0))

                def recip_from_acc(accs, m, outs):
                    for a_, o_ in zip(accs, outs):
                        pi = ps_next(2, 8)
                        sc.op("pe", lambda: nc.tensor.matmul(psum[pi][0:m, :], lhsT=onesf[:, 0:m], rhs=acc[a_][:],
                                                             start=True, stop=True),
                              reads=[f"acc{a_}", "onesf"], writes=[f"ps{pi}"])
                        sc.op("act", lambda: nc.scalar.activation(out=tf[o_][0:m, :], in_=psum[pi][0:m, :],
                                                                  func=AF.Ln),
                              reads=[f"ps{pi}"], writes=[f"tf{o_}"])
                        sc.op("act", lambda: nc.scalar.activation(out=tf[o_][0:m, :], in_=tf[o_][0:m, :],
                                                                  func=AF.Exp, scale=-1.0),
                              reads=[f"tf{o_}"], writes=[f"tf{o_}"])

                nkb = (NHALO + 1) * 4
                for ci in range(4):
                    b2 = ci % 2
                    kk, vk = f"kab{b2}", f"vab{b2}"
                    sc.dma("sp", kk, kab[b2][:], KA[ci, :, j * nkb * 128:(j + 1) * nkb * 128], writes=[kk])
                    sc.dma("sp", vk, vab[b2][:], VAv[:, j * nkb:(j + 1) * nkb, ci * 128:(ci + 1) * 128], writes=[vk])
                    aset = (ci % 2) * 2

                    def front_a(kb):
                        ids = []
                        pis = []
                        for hh in range(2):
                            base = hh * 64
                            pi = ps_next(2, 8)
                            pis.append(pi)
                            sc.op("pe", lambda: nc.tensor.matmul(
                                psum[pi][:], lhsT=kab[b2][base:base + 64, kb * 128:(kb + 1) * 128],
                                rhs=qaT[base:base + 64, ci, :], start=True, stop=True),
                                reads=[kk, "qaT"], writes=[f"ps{pi}"])
                        bcol = (C_HB + 4 * j + kb // 4) if kb < NHALO * 4 else C_ZERO
                        m0 = M_DSW + (19 - kb) * 128
                        for hh in range(2):
                            pi = pis[hh]
                            pb_ = ptc[0] % NPT
                            ptc[0] += 1
                            ids.append(pb_)
                            sc.op("act", lambda: nc.scalar.activation(out=pT[pb_][:], in_=psum[pi][:], func=AF.Exp,
                                                                      scale=0.125, bias=ccol(bcol)),
                                  reads=[f"ps{pi}", "cst"], writes=[f"pT{pb_}"])
                            sc.op("pool", lambda: nc.gpsimd.tensor_tensor(out=pT[pb_][:], in0=pT[pb_][:],
                                                                          in1=cmat[:, m0:m0 + 512], op=ALU.mult),
                                  reads=[f"pT{pb_}", "cmat"], writes=[f"pT{pb_}"])
                            a_ = aset + hh
                            if kb == 0:
                                sc.op("dve", lambda: nc.vector.tensor_copy(out=acc[a_][:], in_=pT[pb_][:]),
                                      reads=[f"pT{pb_}"], writes=[f"acc{a_}"])
                            else:
                                sc.op("dve", lambda: nc.vector.tensor_tensor(out=acc[a_][:], in0=acc[a_][:],
                                                                             in1=pT[pb_][:], op=ALU.add),
                                      reads=[f"pT{pb_}", f"acc{a_}"], writes=[f"acc{a_}"])
                        return (kb, ids)

                    def back_a(item):
                        kb, ids = item
                        for hh in range(2):
                            pb_ = ids[hh]
                            sc.op("pe", lambda: nc.tensor.matmul(
                                psum[hh][0:64, :], lhsT=vab[b2][:, kb, hh * 64:(hh + 1) * 64], rhs=pT[pb_][:],
                                start=(kb == 0), stop=(kb == nkb - 1)),
                                reads=[vk, f"pT{pb_}"], writes=[f"ps{hh}"])

                    attn_pipeline(list(range(nkb)), front_a, back_a)
                    recip_from_acc([aset, aset + 1], 64, [0, 1])
                    for hh in range(2):
                        h = 2 * ci + hh
                        sc.op("pool", lambda: nc.gpsimd.tensor_tensor(out=tf[hh][0:64, :], in0=tf[hh][0:64, :],
                                                                      in1=gaT[:, h, :], op=ALU.mult),
                              reads=[f"tf{hh}", "gaT"], writes=[f"tf{hh}"])
                        sc.op("dve", lambda: nc.vector.tensor_tensor(out=yaT[:, h, :], in0=psum[hh][0:64, :],
                                                                     in1=tf[hh][0:64, :], op=ALU.mult),
                              reads=[f"ps{hh}", f"tf{hh}"], writes=["yaT"])

                npos = 8 * j + 8
                for h in range(4):
                    aset = (h % 2) * 2

                    def front_d(st):
                        pos, kb = st
                        kbi = (h * npos + pos) % NKB
                        kk, vk = f"kbuf{kbi}", f"vbuf{kbi}"
                        if kb == 0:
                            sc.dma("sp", kk, kbuf[kbi][:], KD[h, :, pos * TT:(pos + 1) * TT], writes=[kk])
                            sc.dma("sp", vk, vbuf[kbi][:], VDv[:, pos * 4:(pos + 1) * 4, h * 128:(h + 1) * 128],
                                   writes=[vk])
                        diag = (pos == own)
                        bcol = (C_CB + (pos - 8 * j)) if pos >= 8 * j else C_ZERO
                        q0 = kb * 128 if diag else 0
                        pis = []
                        ids = []
                        for comp in range(2):
                            base = comp * 64
                            pi = ps_next(2, 8)
                            pis.append(pi)
                            sc.op("pe", lambda: nc.tensor.matmul(
                                psum[pi][:, q0:TT], lhsT=kbuf[kbi][base:base + 64, kb * 128:(kb + 1) * 128],
                                rhs=qdT[base:base + 64, h, q0:TT], start=True, stop=True),
                                reads=[kk, "qdT"], writes=[f"ps{pi}"])
                        for comp in range(2):
                            pi = pis[comp]
                            pb_ = ptc[0] % NPT
                            ptc[0] += 1
                            ids.append(pb_)
                            sc.op("act", lambda: nc.scalar.activation(
                                out=pT[pb_][:, q0:TT], in_=psum[pi][:, q0:TT], func=AF.Exp, scale=0.125,
                                bias=ccol(bcol)), reads=[f"ps{pi}", "cst"], writes=[f"pT{pb_}"])
                            if diag:
                                sc.op("pool", lambda: nc.gpsimd.tensor_tensor(
                                    out=pT[pb_][:, q0:q0 + 128], in0=pT[pb_][:, q0:q0 + 128], in1=tri,
                                    op=ALU.mult), reads=[f"pT{pb_}", "cmat"], writes=[f"pT{pb_}"])
                            a_ = aset + comp
                            eng = "dve" if comp == 0 else "pool"
                            E_ = nc.vector if comp == 0 else nc.gpsimd
                            if pos == 0 and kb == 0:
                                sc.op(eng, lambda: E_.tensor_copy(out=acc[a_][:], in_=pT[pb_][:]),
                                      reads=[f"pT{pb_}"], writes=[f"acc{a_}"])
                            else:
                                sc.op(eng, lambda: E_.tensor_tensor(out=acc[a_][:, q0:TT], in0=acc[a_][:, q0:TT],
                                                                    in1=pT[pb_][:, q0:TT], op=ALU.add),
                                      reads=[f"pT{pb_}", f"acc{a_}"], writes=[f"acc{a_}"])
                        return (pos, kb, q0, kbi, ids)

                    def back_d(item):
                        pos, kb, q0, kbi, ids = item
                        vk = f"vbuf{kbi}"
                        st = (pos == 0 and kb == 0)
                        last = (pos == npos - 1 and kb == 3)
                        for comp in range(2):
                            pb_ = ids[comp]
                            sc.op("pe", lambda: nc.tensor.matmul(
                                psum[comp][:, q0:TT], lhsT=vbuf[kbi][:, kb, :], rhs=pT[pb_][:, q0:TT],
                                start=st, stop=last), reads=[vk, f"pT{pb_}"], writes=[f"ps{comp}"])

                    attn_pipeline([(pos, kb) for pos in range(npos) for kb in range(4)], front_d, back_d)
                    recip_from_acc([aset, aset + 1], 128, [0, 1])
                    for comp in range(2):
                        sc.op("dve", lambda comp=comp: nc.vector.tensor_tensor(
                            out=tf[2 + comp][:], in0=psum[comp][:], in1=tf[comp][:], op=ALU.mult),
                            reads=[f"ps{comp}", f"tf{comp}"], writes=[f"tf{2 + comp}"])
                    sc.op("dve", lambda: nc.vector.scalar_tensor_tensor(
                        out=tf[4][:], in0=tf[3][:], scalar=lamc[:, 0:1], in1=tf[2][:], op0=ALU.mult, op1=ALU.add),
                        reads=["tf3", "tf2", "lamc"], writes=["tf4"])
                    sc.op("act", lambda: nc.scalar.activation(out=sqo[:], in_=tf[4][:], func=AF.Square),
                          reads=["tf4"], writes=["sqo"])
                    pi = ps_next(2, 8)
                    sc.op("pe", lambda: nc.tensor.matmul(psum[pi][:], lhsT=ones, rhs=sqo[:], start=True, stop=True),
                          reads=["sqo", "cmat"], writes=[f"ps{pi}"])
                    sc.op("act", lambda: nc.scalar.activation(out=tf[5][:], in_=psum[pi][:], func=AF.Ln,
                                                              scale=1.0 / 128, bias=ccol(C_EPSR)),
                          reads=[f"ps{pi}", "cst"], writes=["tf5"])
                    sc.op("act", lambda: nc.scalar.activation(out=tf[5][:], in_=tf[5][:], func=AF.Exp, scale=-0.5),
                          reads=["tf5"], writes=["tf5"])
                    sc.op("pool", lambda: nc.gpsimd.tensor_tensor(out=tf[5][:], in0=tf[5][:], in1=gdT[:, h, :],
                                                                  op=ALU.mult),
                          reads=["tf5", "gdT"], writes=["tf5"])
                    sc.op("dve", lambda: nc.vector.scalar_tensor_tensor(
                        out=ydT[:, h, :], in0=tf[4][:], scalar=lamc[:, 1:2], in1=tf[5][:],
                        op0=ALU.mult, op1=ALU.mult), reads=["tf4", "tf5", "lamc"], writes=["ydT"])

                for dc in range(8):
                    pi = ps_next(2, 8)
                    for h in range(8):
                        sc.op("pe", lambda h=h: nc.tensor.matmul(
                            psum[pi][:], lhsT=woa[:, h, dc * 128:(dc + 1) * 128], rhs=yaT[:, h, :],
                            start=(h == 0), stop=False), reads=["woa", "yaT"], writes=[f"ps{pi}"])
                    for h in range(4):
                        sc.op("pe", lambda h=h: nc.tensor.matmul(
                            psum[pi][:], lhsT=wod[:, h, dc * 128:(dc + 1) * 128], rhs=ydT[:, h, :],
                            start=False, stop=(h == 3)), reads=["wod", "ydT"], writes=[f"ps{pi}"])
                    sc.op("dve", lambda: nc.vector.tensor_tensor(out=x1[:, dc, :], in0=psum[pi][:], in1=x1[:, dc, :],
                                                                 op=ALU.add),
                          reads=[f"ps{pi}", "x1"], writes=["x1"])
                sc.barrier()
            if stop == "p2":
                return nc

            with ExitStack() as p3:
                sq = sb("sq1", [128, 8, TT], BF16, p3)
                h1 = sb("h1", [128, 8, TT], BF16, p3)
                rs = sb("rs1", [128, TT], F32, p3)
                wst = [sb(f"w1st{i}", [128, 4096], BF16, p3) for i in range(2)]
                uT = sb("uT", [128, 16, TT], BF16, p3)
                gsT = sb("gsT", [128, 16, TT], BF16, p3)
                gv = sb("gv", [128, 4, 2048], BF16, p3)
                vtmp = sb("vtmp", [128, 2048], F32, p3)
                lng = sb("lng", [128, 2, 2048], F32, p3)
                wsf = sb("wsf", [128, 8, 128], BF16, p3)
                bsb = sb("bsb_sb", [128, 8, 128], F32, p3)
                stats = sb("stats", [128, 4, 6], F32, p3)
                mv = sb("mv", [128, 4], F32, p3)
                tf = [sb(f"tg{i}", [128, TT], F32, p3) for i in range(3)]
                wsb = [0]

                def load_w1(src_ap, shape3):
                    b = wsb[0] % 2
                    wsb[0] += 1
                    a, bb = shape3
                    view = wst[b][:, 0:a * bb].rearrange("p (a b) -> p a b", b=bb)
                    sc.dma("sp", f"w1st{b}", view, src_ap, writes=[f"w1st{b}"])
                    return view, f"w1st{b}"

                sc.dma("sp", "lng", lng[:, 0, :], lngb_d[0:1, :].partition_broadcast(128), writes=["lng"])
                sc.dma("sp", "lng", lng[:, 1, :], lngb_d[1:2, :].partition_broadcast(128), writes=["lng"])
                sc.dma("pool", "wsf", wsf[:].rearrange("p g t -> p (g t)"), wsT_d, writes=["wsf"])
                sc.dma("sp", "bsb", bsb[:].rearrange("p g t -> p (g t)"), bsb_d.partition_broadcast(128),
                       writes=["bsb"])
                for g in range(8):
                    sc.op("pool", lambda g=g: nc.gpsimd.tensor_tensor(out=wsf[:, g, :], in0=wsf[:, g, :], in1=tri,
                                                                      op=ALU.mult),
                          reads=["wsf", "cmat"], writes=["wsf"])

                rmsnorm_tile(x1, "x1", C_SGUN, sq, "sq1", h1, "h1", rs, "rs1")
                for wc in range(4):
                    w, wk = load_w1(W1v[:, :, 2048 + wc * 512:2048 + (wc + 1) * 512], (8, 512))
                    for sub in range(4):
                        pi = ps_next()
                        for k in range(8):
                            sc.op("pe", lambda k=k: nc.tensor.matmul(
                                psum[pi][:], lhsT=h1[:, k, sub * 128:(sub + 1) * 128], rhs=w[:, k, :],
                                start=(k == 0), stop=(k == 7)), reads=["h1", wk], writes=[f"ps{pi}"])
                        sc.op("act", lambda: nc.scalar.activation(out=gv[:, sub, wc * 512:(wc + 1) * 512],
                                                                  in_=psum[pi][:], func=AF.Gelu),
                              reads=[f"ps{pi}"], writes=[f"gv{sub}"])
                for sub in range(4):
                    for q in range(4):
                        sc.op("dve", lambda q=q: nc.vector.bn_stats(out=stats[:, q, :],
                                                                    in_=gv[:, sub, q * 512:(q + 1) * 512]),
                              reads=[f"gv{sub}"], writes=["stats"])
                    sc.op("dve", lambda: nc.vector.bn_aggr(out=mv[:, 0:2], in_=stats[:].rearrange("p a b -> p (a b)")),
                          reads=["stats"], writes=["mv"])
                    sc.op("act", lambda: nc.scalar.activation(out=mv[:, 2:3], in_=mv[:, 1:2], func=AF.Ln,
                                                              bias=ccol(C_EPSL)),
                          reads=["mv", "cst"], writes=["mv"])
                    sc.op("act", lambda: nc.scalar.activation(out=mv[:, 3:4], in_=mv[:, 2:3], func=AF.Exp, scale=-0.5),
                          reads=["mv"], writes=["mv"])
                    sc.op("dve", lambda: nc.vector.tensor_scalar(
                        out=vtmp[:], in0=gv[:, sub, :], scalar1=mv[:, 0:1], scalar2=mv[:, 3:4],
                        op0=ALU.subtract, op1=ALU.mult), reads=[f"gv{sub}", "mv"], writes=["vtmp"])
                    sc.op("pool", lambda: nc.gpsimd.tensor_tensor(out=vtmp[:], in0=vtmp[:], in1=lng[:, 0, :],
                                                                  op=ALU.mult),
                          reads=["vtmp", "lng"], writes=["vtmp"])
                    sc.op("pool", lambda: nc.gpsimd.tensor_tensor(out=gv[:, sub, :], in0=vtmp[:], in1=lng[:, 1, :],
                                                                  op=ALU.add),
                          reads=["vtmp", "lng"], writes=[f"gv{sub}"])
                for wc in range(4):
                    w, wk = load_w1(W1v[:, :, wc * 512:(wc + 1) * 512], (8, 512))
                    for s4 in range(4):
                        pi = ps_next()
                        proj_fm(pi, h1, "h1", w, wk, s4 * 128)
                        c = wc * 4 + s4
                        sc.op("act", lambda c=c: nc.scalar.activation(out=uT[:, c, :], in_=psum[pi][:], func=AF.Gelu),
                              reads=[f"ps{pi}"], writes=["uT"])
                for wc in range(4):
                    w, wk = load_w1(W1v[:, :, 4096 + wc * 512:4096 + (wc + 1) * 512], (8, 512))
                    for s4 in range(4):
                        pi = ps_next()
                        proj_fm(pi, h1, "h1", w, wk, s4 * 128)
                        c = wc * 4 + s4
                        sc.op("act", lambda c=c: nc.scalar.activation(out=gsT[:, c, :], in_=psum[pi][:], func=AF.Silu),
                              reads=[f"ps{pi}"], writes=["gsT"])
                for c in range(16):
                    g = c // 2
                    pi = ps_next()
                    for sub in range(4):
                        sc.op("pe", lambda sub=sub: nc.tensor.matmul(
                            psum[pi][:, sub * 128:(sub + 1) * 128], lhsT=gv[:, sub, c * 128:(c + 1) * 128],
                            rhs=wsf[:, g, :], start=True, stop=True),
                            reads=[f"gv{sub}", "wsf"], writes=[f"ps{pi}"])
                    tb = c % 3
                    for sub in range(4):
                        sc.op("dve", lambda sub=sub: nc.vector.tensor_tensor(
                            out=tf[tb][:, sub * 128:(sub + 1) * 128], in0=psum[pi][:, sub * 128:(sub + 1) * 128],
                            in1=bsb[:, g, :], op=ALU.add), reads=[f"ps{pi}", "bsb"], writes=[f"tg{tb}"])
                    sc.op("pool", lambda: nc.gpsimd.tensor_tensor(out=gsT[:, c, :], in0=gsT[:, c, :], in1=uT[:, c, :],
                                                                  op=ALU.mult),
                          reads=["gsT", "uT"], writes=["gsT"])
                    sc.op("dve", lambda: nc.vector.tensor_tensor(out=uT[:, c, :], in0=tf[tb][:], in1=gsT[:, c, :],
                                                                 op=ALU.mult),
                          reads=[f"tg{tb}", "gsT"], writes=["uT"])
                for q4 in range(4):
                    w, wk = load_w1(Wo1v[:, :, q4 * 256:(q4 + 1) * 256], (16, 256))
                    for d2 in range(2):
                        dc = q4 * 2 + d2
                        pi = ps_next()
                        for c in range(16):
                            sc.op("pe", lambda c=c: nc.tensor.matmul(
                                psum[pi][:], lhsT=w[:, c, d2 * 128:(d2 + 1) * 128], rhs=uT[:, c, :],
                                start=(c == 0), stop=(c == 15)), reads=[wk, "uT"], writes=[f"ps{pi}"])
                        tb = dc % 3
                        sc.op("dve", lambda: nc.vector.tensor_tensor(out=tf[tb][:], in0=psum[pi][:], in1=x1[:, dc, :],
                                                                     op=ALU.add),
                              reads=[f"ps{pi}", "x1"], writes=[f"tg{tb}"])
                        sc.dma("sp", f"tg{tb}", out_d[dc * 128:(dc + 1) * 128, j * TT:(j + 1) * TT], tf[tb][:],
                               reads=[f"tg{tb}"], writes=["out"])
                sc.barrier()
            if stop == "p3":
                return nc
    return nc


def _tile_order(c):
    order = []
    for j in range(NSLOT):
        order += [8 * j + i for i in range(c + 1, 8)] + [8 * j + i for i in range(0, c)] + [8 * j + c]
    return order


def _rope_tables(positions):
    rot = 16
    inv = 1.0 / (500000.0 ** (np.arange(0, rot, 2, dtype=np.float32) / rot))
    ang = positions.astype(np.float32)[:, None] * inv[None, :].astype(np.float32)
    cos, sin = np.cos(ang).astype(np.float32), np.sin(ang).astype(np.float32)
    T = positions.shape[0]
    cf = np.ones((128, T), np.float32)
    sf = np.zeros((128, T), np.float32)
    for hb in (0, 64):
        cf[hb:hb + 8] = cos.T
        cf[hb + 8:hb + 16] = cos.T
        sf[hb:hb + 8] = sin.T
        sf[hb + 8:hb + 16] = sin.T
    return np.stack([cf, sf], 0)


def _const_mats():
    cm = np.zeros((128, NCM), np.float32)
    cm[:, M_ONES:M_ONES + 128] = 1.0
    cm[0:64, M_BONES:M_BONES + 64] = 1.0
    cm[64:128, M_BONES + 64:M_BONES + 128] = 1.0
    for m in range(128):
        dl = m % 64
        if dl < 8:
            cm[m + 8, M_ROT + m] = -1.0
        elif dl < 16:
            cm[m - 8, M_ROT + m] = 1.0
    kk = np.arange(128)[:, None]
    qq = np.arange(128)[None, :]
    cm[:, M_TRI:M_TRI + 128] = (kk <= qq).astype(np.float32)
    for idx in range(23):
        delta = idx - 3
        d = 128 * delta + qq - kk
        m = ((d >= 0) & (d <= 128)).astype(np.float32)
        m += ((d % 4 == 0) & (d >= 0) & (d <= 512)).astype(np.float32)
        m += ((d % 16 == 0) & (d >= 0) & (d <= 2048)).astype(np.float32)
        cm[:, M_DSW + idx * 128:M_DSW + (idx + 1) * 128] = m
    return cm.astype(ml_dtypes.bfloat16)


_NC_CACHE = {}


def _prep(x, att_norm, att_w_in, dsw_q_norm, dsw_k_norm, diff_q_norm, diff_k_norm,
           diff_lam_q1, diff_lam_k1, diff_lam_q2, diff_lam_k2, diff_subln, att_w_out,
           sgu_norm, sgu_w_in, sgu_ln_g, sgu_ln_b, sgu_w_s, sgu_b_s, sgu_w_out):
    f = np.float32
    xT = np.ascontiguousarray(np.asarray(x, f)[0].T)
    xT_tiles = xT.reshape(D, NT, TT)
    cm = _const_mats()
    w_in0 = np.ascontiguousarray(np.asarray(att_w_in, f)[0])
    w_out0 = np.ascontiguousarray(np.asarray(att_w_out, f)[0])
    w_in1 = np.ascontiguousarray(np.asarray(sgu_w_in, f)[0])
    w_out1 = np.ascontiguousarray(np.asarray(sgu_w_out, f)[0])
    lngb = np.ascontiguousarray(np.stack([np.asarray(sgu_ln_g, f)[0], np.asarray(sgu_ln_b, f)[0]], 0))
    wsT = np.ascontiguousarray(np.asarray(sgu_w_s, f)[0].transpose(2, 0, 1).reshape(128, 8 * 128))
    bsb = np.ascontiguousarray(np.asarray(sgu_b_s, f)[0].reshape(1, 8 * 128))
    in_maps = []
    orders = []
    for c in range(NCORE):
        order = _tile_order(c)
        orders.append(order)
        x_all = np.ascontiguousarray(xT_tiles[:, order, :].reshape(D, S))
        pos_all = (np.asarray(order)[:, None] * TT + np.arange(TT)[None, :]).reshape(-1)
        halo_tiles = []
        for j in range(NSLOT):
            halo_tiles += [8 * j + c - NHALO + m for m in range(NHALO)]
        x_halo = np.zeros((D, NSLOT * NHALO, TT), f)
        pos_halo = np.zeros((NSLOT * NHALO, TT), np.int64)
        for i, t in enumerate(halo_tiles):
            if t >= 0:
                x_halo[:, i, :] = xT_tiles[:, t, :]
                pos_halo[i] = t * TT + np.arange(TT)
        cst = np.zeros((128, NCST), f)
        cst[:, C_ATTN:C_ATTN + 8] = np.asarray(att_norm, f)[0].reshape(8, 128).T
        cst[:, C_SGUN:C_SGUN + 8] = np.asarray(sgu_norm, f)[0].reshape(8, 128).T
        cst[:, C_DSWQ] = np.tile(np.asarray(dsw_q_norm, f)[0], 2)
        cst[:, C_DSWK] = np.tile(np.asarray(dsw_k_norm, f)[0], 2)
        cst[:, C_DIFQ] = np.tile(np.asarray(diff_q_norm, f)[0], 2)
        cst[:, C_DIFK] = np.tile(np.asarray(diff_k_norm, f)[0], 2)
        cst[:, C_SUBLN] = np.asarray(diff_subln, f)[0]
        for i in range(8):
            cst[:, C_CB + i] = 0.0 if (i >= 7 - c) else NEG
        for j in range(NSLOT):
            for m in range(NHALO):
                cst[:, C_HB + 4 * j + m] = 0.0 if (8 * j + c - NHALO + m) >= 0 else NEG
        cst[:, C_LQ1:C_LQ1 + 64] = np.asarray(diff_lam_q1, f)[0][None, :]
        cst[:, C_LK1:C_LK1 + 64] = np.asarray(diff_lam_k1, f)[0][None, :]
        cst[:, C_LQ2:C_LQ2 + 64] = np.asarray(diff_lam_q2, f)[0][None, :]
        cst[:, C_LK2:C_LK2 + 64] = np.asarray(diff_lam_k2, f)[0][None, :]
        cst[:, C_EPSR] = RMS_EPS
        cst[:, C_EPSL] = LN_EPS
        cst[:, C_ONE] = 1.0
        in_maps.append({
            "x_all": x_all,
            "x_halo": np.ascontiguousarray(x_halo.reshape(D, NSLOT * NHALO * TT)),
            "rope_all": _rope_tables(pos_all),
            "rope_halo": _rope_tables(pos_halo.reshape(-1)),
            "w_in0": w_in0, "w_out0": w_out0, "w_in1": w_in1, "w_out1": w_out1,
            "cst": cst, "cmat": cm, "lngb": lngb, "wsT": wsT, "bsb": bsb,
        })
    return in_maps


def kernel(**inputs):
    f = np.float32
    in_maps = _prep(**inputs)
    if "nc" not in _NC_CACHE:
        _NC_CACHE["nc"] = build_program()
    nc = _NC_CACHE["nc"]
    res = run_bass_kernel_spmd(nc, in_maps, core_ids=list(range(NCORE)))
    out = np.zeros((S, D), f)
    for c in range(NCORE):
        oT = np.asarray(res.results[c]["outT"], f)
        for j in range(NSLOT):
            t = 8 * j + c
            out[t * TT:(t + 1) * TT, :] = oT[:, j * TT:(j + 1) * TT].T
    return out[None]
```

```python
import math
from contextlib import ExitStack

import ml_dtypes
import numpy as np

import concourse.bass as bass
import concourse.mybir as mybir
from concourse.bass_utils import run_bass_kernel_spmd

F32 = mybir.dt.float32
BF16 = mybir.dt.bfloat16
AF = mybir.ActivationFunctionType
ALU = mybir.AluOpType

S = 16384
D = 1024
TT = 512
NT = S // TT
NCORE = 8
NSLOT = NT // NCORE
NHALO = 4
NEG = -30000.0
RMS_EPS = 1e-6
LN_EPS = 1e-5
LAM_INIT0 = 0.8 - 0.6 * math.exp(-0.3 * 0)

C_ATTN = 0
C_SGUN = 8
C_DSWQ = 16
C_DSWK = 17
C_DIFQ = 18
C_DIFK = 19
C_SUBLN = 20
C_CB = 21
C_HB = 29
C_ZERO = 45
C_LQ1 = 46
C_LK1 = 110
C_LQ2 = 174
C_LK2 = 238
C_EPSR = 302
C_EPSL = 303
C_ONE = 304
NCST = 305
M_ONES = 0
M_BONES = 128
M_ROT = 256
M_TRI = 384
M_DSW = 512
NCM = 512 + 23 * 128


class Sched:
    def __init__(self, nc, es):
        self.nc = nc
        self.es = es
        self.E = {"pe": nc.tensor, "act": nc.scalar, "dve": nc.vector, "pool": nc.gpsimd, "sp": nc.sync}
        self.semh = {}
        self.cnt = {}
        for e in self.E:
            self.semh[e] = es.enter_context(nc.semaphore("s_" + e))
            self.cnt[e] = 0
        self.known = {e: {} for e in self.E}
        self.lastw = {}
        self.readers = {}
        self.nwait = 0

    def dsem(self, name):
        k = "d_" + name
        if k not in self.semh:
            self.semh[k] = self.es.enter_context(self.nc.semaphore(k))
            self.cnt[k] = 0
        return k

    def _wait(self, eng, deps):
        need = {}
        for (sk, val) in deps:
            if sk == eng and eng in ("pe", "sp"):
                continue
            if val > need.get(sk, 0):
                need[sk] = val
        for sk, val in need.items():
            if self.known[eng].get(sk, 0) >= val:
                continue
            self.E[eng].wait_ge(self.semh[sk], val)
            self.known[eng][sk] = val
            self.nwait += 1

    def _deps(self, reads, writes):
        deps = []
        for k in reads:
            if k in self.lastw:
                deps.append(self.lastw[k])
        for k in writes:
            if k in self.lastw:
                deps.append(self.lastw[k])
            deps.extend(self.readers.get(k, {}).items())
        return deps

    def _record(self, ev, reads, writes):
        for k in reads:
            r = self.readers.setdefault(k, {})
            if ev[1] > r.get(ev[0], 0):
                r[ev[0]] = ev[1]
        for k in writes:
            self.lastw[k] = ev
            self.readers[k] = {}

    def op(self, eng, fn, reads=(), writes=()):
        self._wait(eng, self._deps(reads, writes))
        ins = fn()
        self.cnt[eng] += 1
        ins.then_inc(self.semh[eng], 1)
        self._record((eng, self.cnt[eng]), reads, writes)

    def dma(self, q, sem, out, in_, reads=(), writes=()):
        sk = self.dsem(sem)
        self._wait(q, self._deps(reads, writes))
        ins = self.E[q].dma_start(out=out, in_=in_)
        self.cnt[sk] += 16
        ins.then_inc(self.semh[sk], 16)
        self._record((sk, self.cnt[sk]), reads, writes)

    def barrier(self):
        for e in self.E:
            for sk, c in self.cnt.items():
                if sk == e and e in ("pe", "sp"):
                    continue
                if c > 0 and self.known[e].get(sk, 0) < c:
                    self.E[e].wait_ge(self.semh[sk], c)
                    self.known[e][sk] = c
        self.lastw = {}
        self.readers = {}


def build_program(stop=None):
    nc = bass.Bass("TRN2", target_bir_lowering=False)
    x_all = nc.dram_tensor("x_all", [D, S], F32, kind="ExternalInput").ap()
    x_halo = nc.dram_tensor("x_halo", [D, NSLOT * NHALO * TT], F32, kind="ExternalInput").ap()
    rope_all = nc.dram_tensor("rope_all", [2, 128, S], F32, kind="ExternalInput").ap()
    rope_halo = nc.dram_tensor("rope_halo", [2, 128, NSLOT * NHALO * TT], F32, kind="ExternalInput").ap()
    w_in0 = nc.dram_tensor("w_in0", [D, 4096], F32, kind="ExternalInput").ap()
    w_out0 = nc.dram_tensor("w_out0", [D, D], F32, kind="ExternalInput").ap()
    w_in1 = nc.dram_tensor("w_in1", [D, 6144], F32, kind="ExternalInput").ap()
    w_out1 = nc.dram_tensor("w_out1", [2048, D], F32, kind="ExternalInput").ap()
    cst_d = nc.dram_tensor("cst", [128, NCST], F32, kind="ExternalInput").ap()
    cmat_d = nc.dram_tensor("cmat", [128, NCM], BF16, kind="ExternalInput").ap()
    lngb_d = nc.dram_tensor("lngb", [2, 2048], F32, kind="ExternalInput").ap()
    wsT_d = nc.dram_tensor("wsT", [128, 8 * 128], F32, kind="ExternalInput").ap()
    bsb_d = nc.dram_tensor("bsb", [1, 8 * 128], F32, kind="ExternalInput").ap()
    out_d = nc.dram_tensor("outT", [D, NSLOT * TT], F32, kind="ExternalOutput").ap()
    W0b = nc.dram_tensor("W0b", [D, 4096], BF16).ap()
    Wo0b = nc.dram_tensor("Wo0b", [D, D], BF16).ap()
    W1b = nc.dram_tensor("W1b", [D, 6144], BF16).ap()
    Wo1b = nc.dram_tensor("Wo1b", [2048, D], BF16).ap()
    KD = nc.dram_tensor("KD", [4, 128, S], BF16).ap()
    VD = nc.dram_tensor("VD", [S, 512], BF16).ap()
    NKA = NSLOT * (NHALO + 1) * TT
    KA = nc.dram_tensor("KA", [4, 128, NKA], BF16).ap()
    VA = nc.dram_tensor("VA", [NKA, 512], BF16).ap()
    QA = nc.dram_tensor("QA", [4, 128, NSLOT * TT], BF16).ap()
    QD = nc.dram_tensor("QD", [4, 128, NSLOT * TT], BF16).ap()
    GA = nc.dram_tensor("GA", [8, 64, NSLOT * TT], BF16).ap()
    GD = nc.dram_tensor("GD", [4, 128, NSLOT * TT], BF16).ap()

    with ExitStack() as es:
        es.enter_context(nc.allow_low_precision("bf16 matmul operands, fp32 accumulation"))
        es.enter_context(nc.allow_non_contiguous_dma("strided weight / scratch tiles"))
        sc = Sched(nc, es)

        uid = [0]

        def sb(name, shape, dt, stack=es):
            uid[0] += 1
            return stack.enter_context(nc.sbuf_tensor(f"{name}_{uid[0]}", shape, dt))

        cst = sb("cst_sb", [128, NCST], F32)
        cmat = sb("cmat_sb", [128, NCM], BF16)
        lamc = sb("lamc", [128, 8], F32)
        x1 = sb("x1", [128, 8, TT], F32)
        onesf = sb("onesf", [128, 128], F32)
        psum = [es.enter_context(nc.psum_tensor(f"ps{i}", [128, TT], F32)) for i in range(8)]
        ps_rr = [0]

        def ps_next(lo=0, hi=8):
            i = lo + (ps_rr[0] % (hi - lo))
            ps_rr[0] += 1
            return i

        ones = cmat[:, M_ONES:M_ONES + 128]
        bones = cmat[:, M_BONES:M_BONES + 128]
        rotm = cmat[:, M_ROT:M_ROT + 128]
        tri = cmat[:, M_TRI:M_TRI + 128]

        def ccol(c, n=128):
            return cst[0:n, c:c + 1]

        sc.dma("sp", "cst", cst[:], cst_d, writes=["cst"])
        sc.dma("sp", "cmat", cmat[:], cmat_d, writes=["cmat"])
        for (src, dst, rows, nm) in ((w_in0, W0b, D, "W0b"), (w_out0, Wo0b, D, "Wo0b"),
                                     (w_in1, W1b, D, "W1b"), (w_out1, Wo1b, 2048, "Wo1b")):
            step = 128
            for r0 in range(0, rows, step):
                sc.dma("pool", "wcast", dst[r0:r0 + step, :], src[r0:r0 + step, :], writes=[nm])
                done = sc.cnt["d_wcast"] - 32
                if done > 0:
                    nc.gpsimd.wait_ge(sc.semh["d_wcast"], done)
                    sc.known["pool"]["d_wcast"] = done
        sc.op("dve", lambda: nc.vector.memset(onesf[:], 1.0), writes=["onesf"])
        with ExitStack() as ls:
            lt = sb("lam_t", [128, 128], F32, ls)
            for i, (ca, cb) in enumerate(((C_LQ1, C_LK1), (C_LQ2, C_LK2))):
                sc.op("dve", lambda ca=ca, cb=cb, i=i: nc.vector.tensor_tensor(
                    out=lt[:, i * 64:(i + 1) * 64], in0=cst[:, ca:ca + 64], in1=cst[:, cb:cb + 64], op=ALU.mult),
                    reads=["cst"], writes=["lt"])
            for i in range(2):
                sc.op("dve", lambda i=i: nc.vector.reduce_sum(
                    out=lamc[:, 2 + i:3 + i], in_=lt[:, i * 64:(i + 1) * 64], axis=mybir.AxisListType.X),
                    reads=["lt"], writes=["lamc"])
            sc.op("act", lambda: nc.scalar.activation(out=lamc[:, 4:6], in_=lamc[:, 2:4], func=AF.Exp),
                  reads=["lamc"], writes=["lamc"])
            sc.op("dve", lambda: nc.vector.scalar_tensor_tensor(
                out=lamc[:, 0:1], in0=lamc[:, 5:6], scalar=-LAM_INIT0, in1=lamc[:, 4:5],
                op0=ALU.add, op1=ALU.subtract), reads=["lamc"], writes=["lamc"])
            sc.op("dve", lambda: nc.vector.tensor_scalar(
                out=lamc[:, 1:2], in0=cst[:, C_SUBLN:C_SUBLN + 1], scalar1=1.0 - LAM_INIT0, scalar2=None,
                op0=ALU.mult), reads=["cst", "lamc"], writes=["lamc"])
            sc.barrier()

        def rmsnorm_tile(xin, xk, gcol0, sq, sqk, hT, hk, rs, rsk):
            sc.op("act", lambda: nc.scalar.activation(out=sq[:], in_=xin[:], func=AF.Square),
                  reads=[xk], writes=[sqk])
            pi = ps_next()
            for k in range(8):
                sc.op("pe", lambda k=k: nc.tensor.matmul(psum[pi][:], lhsT=ones, rhs=sq[:, k, :],
                                                       start=(k == 0), stop=(k == 7)),
                      reads=[sqk, "cmat"], writes=[f"ps{pi}"])
            sc.op("act", lambda: nc.scalar.activation(out=rs[:], in_=psum[pi][:], func=AF.Ln,
                                                      scale=1.0 / D, bias=ccol(C_EPSR)),
                  reads=[f"ps{pi}", "cst"], writes=[rsk])
            sc.op("act", lambda: nc.scalar.activation(out=rs[:], in_=rs[:], func=AF.Exp, scale=-0.5),
                  reads=[rsk], writes=[rsk])
            for k in range(8):
                sc.op("dve", lambda k=k: nc.vector.scalar_tensor_tensor(
                    out=hT[:, k, :], in0=xin[:, k, :], scalar=cst[:, gcol0 + k:gcol0 + k + 1], in1=rs[:],
                    op0=ALU.mult, op1=ALU.mult), reads=[xk, rsk, "cst"], writes=[hk])

        def proj_fm(pi, hT, hk, w, wk, c0, m=128, prow=128):
            for k in range(8):
                sc.op("pe", lambda k=k: nc.tensor.matmul(psum[pi][0:m, :], lhsT=w[:, k, c0:c0 + m], rhs=hT[:, k, :],
                                                       start=(k == 0), stop=(k == 7)),
                      reads=[hk, wk], writes=[f"ps{pi}"])

        if stop == "setup":
            return nc
        with ExitStack() as p1:
            xin = [sb(f"xin{i}", [128, 8, TT], F32, p1) for i in range(2)]
            sq = [sb(f"sq{i}", [128, 8, TT], BF16, p1) for i in range(2)]
            hTs = [sb(f"hT{i}", [128, 8, TT], BF16, p1) for i in range(2)]
            rs = [sb(f"rs{i}", [128, TT], F32, p1) for i in range(2)]
            wkvd = sb("wkvd", [128, 8, 1024], BF16, p1)
            wkva = sb("wkva", [128, 8, 1024], BF16, p1)
            wst = [sb(f"wst{i}", [128, 8, 512], BF16, p1) for i in range(2)]
            rope = [sb(f"rope{i}", [128, 2, TT], F32, p1) for i in range(2)]
            NPB = 3
            sqc = [sb(f"sqc{i}", [128, TT], BF16, p1) for i in range(NPB)]
            qa_ = [sb(f"qA{i}", [128, TT], BF16, p1) for i in range(NPB)]
            rsc = [sb(f"rsc{i}", [128, TT], F32, p1) for i in range(NPB)]
            t1 = [sb(f"t1{i}", [128, TT], F32, p1) for i in range(NPB)]
            t2 = [sb(f"t2{i}", [128, TT], F32, p1) for i in range(NPB)]
            ko = [sb(f"ko{i}", [128, TT], BF16, p1) for i in range(NPB)]
            vst = [sb(f"vst{i}", [128, TT], BF16, p1) for i in range(2)]
            pb = [0]
            vb = [0]
            wsb = [0]

            W0v = W0b.rearrange("(k p) e -> p k e", p=128)
            sc.dma("sp", "wkvd", wkvd[:, :, 0:512], W0v[:, :, 2560:3072], reads=["W0b"], writes=["wkvd"])
            sc.dma("sp", "wkvd", wkvd[:, :, 512:1024], W0v[:, :, 3072:3584], reads=["W0b"], writes=["wkvd"])
            sc.dma("sp", "wkva", wkva[:, :, 0:512], W0v[:, :, 512:1024], reads=["W0b"], writes=["wkva"])
            sc.dma("sp", "wkva", wkva[:, :, 512:1024], W0v[:, :, 1024:1536], reads=["W0b"], writes=["wkva"])

            deferred = []

            def flush_deferred():
                while deferred:
                    deferred.pop(0)()

            def qk_post(pi, gcol, ropet, ropek, dst_ap, dkey):
                b = pb[0] % NPB
                pb[0] += 1
                sc.op("act", lambda: nc.scalar.activation(out=sqc[b][:], in_=psum[pi][:], func=AF.Square),
                      reads=[f"ps{pi}"], writes=[f"sqc{b}"])
                sc.op("act", lambda: nc.scalar.activation(out=qa_[b][:], in_=psum[pi][:], func=AF.Identity,
                                                          scale=ccol(gcol)),
                      reads=[f"ps{pi}", "cst"], writes=[f"qA{b}"])
                flush_deferred()
                deferred.append(lambda: qk_post_b(b, ropet, ropek, dst_ap, dkey))

            def qk_post_b(b, ropet, ropek, dst_ap, dkey):
                p2 = ps_next()
                sc.op("pe", lambda: nc.tensor.matmul(psum[p2][:], lhsT=bones, rhs=sqc[b][:], start=True, stop=True),
                      reads=[f"sqc{b}", "cmat"], writes=[f"ps{p2}"])
                p3 = ps_next()
                sc.op("pe", lambda: nc.tensor.matmul(psum[p3][:], lhsT=rotm, rhs=qa_[b][:], start=True, stop=True),
                      reads=[f"qA{b}", "cmat"], writes=[f"ps{p3}"])
                sc.op("act", lambda: nc.scalar.activation(out=rsc[b][:], in_=psum[p2][:], func=AF.Ln,
                                                          scale=1.0 / 64, bias=ccol(C_EPSR)),
                      reads=[f"ps{p2}", "cst"], writes=[f"rsc{b}"])
                sc.op("act", lambda: nc.scalar.activation(out=rsc[b][:], in_=rsc[b][:], func=AF.Exp, scale=-0.5),
                      reads=[f"rsc{b}"], writes=[f"rsc{b}"])
                sc.op("pool", lambda: nc.gpsimd.tensor_tensor(out=t1[b][:], in0=qa_[b][:], in1=ropet[:, 0, :],
                                                              op=ALU.mult),
                      reads=[f"qA{b}", ropek], writes=[f"t1{b}"])
                sc.op("dve", lambda: nc.vector.tensor_tensor(out=t2[b][:], in0=psum[p3][:], in1=ropet[:, 1, :],
                                                             op=ALU.mult),
                      reads=[f"ps{p3}", ropek], writes=[f"t2{b}"])
                sc.op("pool", lambda: nc.gpsimd.tensor_tensor(out=t1[b][:], in0=t1[b][:], in1=t2[b][:], op=ALU.add),
                      reads=[f"t1{b}", f"t2{b}"], writes=[f"t1{b}"])
                sc.op("dve", lambda: nc.vector.tensor_tensor(out=ko[b][:], in0=t1[b][:], in1=rsc[b][:], op=ALU.mult),
                      reads=[f"t1{b}", f"rsc{b}"], writes=[f"ko{b}"])
                sc.dma("sp", f"ko{b}", dst_ap, ko[b][:], reads=[f"ko{b}"], writes=[dkey])

            def v_tm(hT, hk, w, wk, c0, dst_rows_fn, dkey):
                for sub in range(4):
                    pi = ps_next()
                    for k in range(8):
                        sc.op("pe", lambda k=k: nc.tensor.matmul(
                            psum[pi][:], lhsT=hT[:, k, sub * 128:(sub + 1) * 128], rhs=w[:, k, c0:c0 + 512],
                            start=(k == 0), stop=(k == 7)), reads=[hk, wk], writes=[f"ps{pi}"])
                    flush_deferred()
                    b = vb[0] % 2
                    vb[0] += 1
                    sc.op("act", lambda: nc.scalar.activation(out=vst[b][:], in_=psum[pi][:], func=AF.Copy),
                          reads=[f"ps{pi}"], writes=[f"vst{b}"])
                    sc.dma("sp", f"vst{b}", dst_rows_fn(sub), vst[b][:], reads=[f"vst{b}"], writes=[dkey])

            def silu_fm(pi, m, dst_ap, dkey):
                flush_deferred()
                b = pb[0] % NPB
                pb[0] += 1
                sc.op("act", lambda: nc.scalar.activation(out=ko[b][0:m, :], in_=psum[pi][0:m, :], func=AF.Silu),
                      reads=[f"ps{pi}"], writes=[f"ko{b}"])
                sc.dma("sp", f"ko{b}", dst_ap, ko[b][0:m, :], reads=[f"ko{b}"], writes=[dkey])

            def load_wst(c0):
                b = wsb[0] % 2
                wsb[0] += 1
                sc.dma("sp", f"wst{b}", wst[b][:], W0v[:, :, c0:c0 + 512], reads=["W0b"], writes=[f"wst{b}"])
                return wst[b], f"wst{b}"

            tiles = [("all", pos) for pos in range(NT)] + [("halo", i) for i in range(NSLOT * NHALO)]
            if stop == "p1a":
                tiles = tiles[:8]
            xav = x_all.rearrange("(k p) t -> p k t", p=128)
            xhv = x_halo.rearrange("(k p) t -> p k t", p=128)

            def load_x(ti):
                kind, idx = tiles[ti]
                xb = ti % 2
                srcx = xav if kind == "all" else xhv
                sc.dma("sp", f"xin{xb}", xin[xb][:], srcx[:, :, idx * TT:(idx + 1) * TT], writes=[f"xin{xb}"])

            def load_rope(ti):
                kind, idx = tiles[ti]
                xb = ti % 2
                srcr = rope_all if kind == "all" else rope_halo
                sc.dma("sp", f"rope{xb}", rope[xb][:],
                       srcr.rearrange("c p t -> p c t")[:, :, idx * TT:(idx + 1) * TT], writes=[f"rope{xb}"])

            def front(ti):
                xb = ti % 2
                rmsnorm_tile(xin[xb], f"xin{xb}", C_ATTN, sq[xb], f"sq{xb}", hTs[xb], f"hT{xb}", rs[xb], f"rs{xb}")

            load_x(0)
            load_rope(0)
            front(0)
            if len(tiles) > 1:
                load_x(1)
            for ti, (kind, idx) in enumerate(tiles):
                xb = ti % 2
                xk = f"xin{xb}"
                rk = f"rope{xb}"
                hT, hk = hTs[xb], f"hT{xb}"
                if ti + 1 < len(tiles):
                    load_rope(ti + 1)
                    front(ti + 1)
                if ti + 2 < len(tiles):
                    load_x(ti + 2)
                if kind == "all":
                    pos = idx
                    for ch in range(4):
                        pi = ps_next()
                        proj_fm(pi, hT, hk, wkvd, "wkvd", ch * 128)
                        qk_post(pi, C_DIFK, rope[xb], rk, KD[ch, :, pos * TT:(pos + 1) * TT], "KD")
                    v_tm(hT, hk, wkvd, "wkvd", 512, lambda sub, pos=pos: VD[(pos * 4 + sub) * 128:(pos * 4 + sub + 1) * 128, :],
                         "VD")
                    if pos % 8 == 7:
                        j = pos // 8
                        kpos = j * (NHALO + 1) + NHALO
                        for ch in range(4):
                            pi = ps_next()
                            proj_fm(pi, hT, hk, wkva, "wkva", ch * 128)
                            qk_post(pi, C_DSWK, rope[xb], rk, KA[ch, :, kpos * TT:(kpos + 1) * TT], "KA")
                        v_tm(hT, hk, wkva, "wkva", 512,
                             lambda sub, kpos=kpos: VA[(kpos * 4 + sub) * 128:(kpos * 4 + sub + 1) * 128, :], "VA")
                        w, wk = load_wst(0)
                        for ch in range(4):
                            pi = ps_next()
                            proj_fm(pi, hT, hk, w, wk, ch * 128)
                            qk_post(pi, C_DSWQ, rope[xb], rk, QA[ch, :, j * TT:(j + 1) * TT], "QA")
                        w, wk = load_wst(1536)
                        for h in range(8):
                            pi = ps_next()
                            proj_fm(pi, hT, hk, w, wk, h * 64, m=64)
                            silu_fm(pi, 64, GA[h, :, j * TT:(j + 1) * TT], "GA")
                        w, wk = load_wst(2048)
                        for ch in range(4):
                            pi = ps_next()
                            proj_fm(pi, hT, hk, w, wk, ch * 128)
                            qk_post(pi, C_DIFQ, rope[xb], rk, QD[ch, :, j * TT:(j + 1) * TT], "QD")
                        w, wk = load_wst(3584)
                        for ch in range(4):
                            pi = ps_next()
                            proj_fm(pi, hT, hk, w, wk, ch * 128)
                            silu_fm(pi, 128, GD[ch, :, j * TT:(j + 1) * TT], "GD")
                else:
                    j, m = idx // NHALO, idx % NHALO
                    kpos = j * (NHALO + 1) + m
                    for ch in range(4):
                        pi = ps_next()
                        proj_fm(pi, hT, hk, wkva, "wkva", ch * 128)
                        qk_post(pi, C_DSWK, rope[xb], rk, KA[ch, :, kpos * TT:(kpos + 1) * TT], "KA")
                    v_tm(hT, hk, wkva, "wkva", 512,
                         lambda sub, kpos=kpos: VA[(kpos * 4 + sub) * 128:(kpos * 4 + sub + 1) * 128, :], "VA")
                flush_deferred()
            sc.barrier()

        if stop in ("p1", "p1a"):
            return nc
        Wo0v_a = Wo0b[0:512, :].rearrange("(h d) e -> d h e", d=64)
        Wo0v_d = Wo0b[512:1024, :].rearrange("(h d) e -> d h e", d=128)
        VDv = VD.rearrange("(n p) e -> p n e", p=128)
        VAv = VA.rearrange("(n p) e -> p n e", p=128)
        W1v = W1b.rearrange("(k p) e -> p k e", p=128)
        Wo1v = Wo1b.rearrange("(c p) e -> p c e", p=128)
        xav = x_all.rearrange("(k p) t -> p k t", p=128)
        for j in range(NSLOT):
            own = 8 * j + 7
            with ExitStack() as p2:
                qaT = sb("qaT", [128, 4, TT], BF16, p2)
                qdT = sb("qdT", [128, 4, TT], BF16, p2)
                gaT = sb("gaT", [64, 8, TT], BF16, p2)
                gdT = sb("gdT", [128, 4, TT], BF16, p2)
                yaT = sb("yaT", [64, 8, TT], BF16, p2)
                ydT = sb("ydT", [128, 4, TT], BF16, p2)
                woa = sb("woa", [64, 8, D], BF16, p2)
                wod = sb("wod", [128, 4, D], BF16, p2)
                NKB = 3
                kbuf = [sb(f"kbuf{i}", [128, TT], BF16, p2) for i in range(NKB)]
                vbuf = [sb(f"vbuf{i}", [128, 4, 128], BF16, p2) for i in range(NKB)]
                kab = [sb(f"kab{i}", [128, (NHALO + 1) * TT], BF16, p2) for i in range(2)]
                vab = [sb(f"vab{i}", [128, (NHALO + 1) * 4, 128], BF16, p2) for i in range(2)]
                LAG = 2
                NPT = 2 * (LAG + 2)
                pT = [sb(f"pT{i}", [128, TT], BF16, p2) for i in range(NPT)]
                acc = [sb(f"acc{i}", [128, TT], F32, p2) for i in range(4)]
                tf = [sb(f"tf{i}", [128, TT], F32, p2) for i in range(8)]
                sqo = sb("sqo", [128, TT], BF16, p2)
                ptc = [0]

                sc.dma("sp", "qaT", qaT[:], QA.rearrange("c p t -> p c t")[:, :, j * TT:(j + 1) * TT], writes=["qaT"])
                sc.dma("sp", "qdT", qdT[:], QD.rearrange("c p t -> p c t")[:, :, j * TT:(j + 1) * TT], writes=["qdT"])
                sc.dma("sp", "gaT", gaT[:], GA.rearrange("c p t -> p c t")[:, :, j * TT:(j + 1) * TT], writes=["gaT"])
                sc.dma("sp", "gdT", gdT[:], GD.rearrange("c p t -> p c t")[:, :, j * TT:(j + 1) * TT], writes=["gdT"])
                sc.dma("sp", "woa", woa[:], Wo0v_a, writes=["woa"])
                sc.dma("sp", "wod", wod[:], Wo0v_d, writes=["wod"])
                sc.dma("sp", "x1", x1[:], xav[:, :, own * TT:(own + 1) * TT], writes=["x1"])

                def attn_pipeline(steps, emit_front, emit_back):
                    pend = []
                    for st in steps:
                        pend.append(emit_front(st))
                        if len(pend) > LAG:
                            emit_back(pend.pop(0))
                    while pend:
                        emit_back(pend.pop(0))

                def recip_from_acc(accs, m, outs):
                    for a_, o_ in zip(accs, outs):
                        pi = ps_next(2, 8)
                        sc.op("pe", lambda: nc.tensor.matmul(psum[pi][0:m, :], lhsT=onesf[:, 0:m], rhs=acc[a_][:],
                                                             start=True, stop=True),
                              reads=[f"acc{a_}", "onesf"], writes=[f"ps{pi}"])
                        sc.op("act", lambda: nc.scalar.activation(out=tf[o_][0:m, :], in_=psum[pi][0:m, :],
                                                                  func=AF.Ln),
                              reads=[f"ps{pi}"], writes=[f"tf{o_}"])
                        sc.op("act", lambda: nc.scalar.activation(out=tf[o_][0:m, :], in_=tf[o_][0:m, :],
                                                                  func=AF.Exp, scale=-1.0),
                              reads=[f"tf{o_}"], writes=[f"tf{o_}"])

                nkb = (NHALO + 1) * 4
                for ci in range(4):
                    b2 = ci % 2
                    kk, vk = f"kab{b2}", f"vab{b2}"
                    sc.dma("sp", kk, kab[b2][:], KA[ci, :, j * nkb * 128:(j + 1) * nkb * 128], writes=[kk])
                    sc.dma("sp", vk, vab[b2][:], VAv[:, j * nkb:(j + 1) * nkb, ci * 128:(ci + 1) * 128], writes=[vk])
                    aset = (ci % 2) * 2

                    def front_a(kb):
                        ids = []
                        pis = []
                        for hh in range(2):
                            base = hh * 64
                            pi = ps_next(2, 8)
                            pis.append(pi)
                            sc.op("pe", lambda: nc.tensor.matmul(
                                psum[pi][:], lhsT=kab[b2][base:base + 64, kb * 128:(kb + 1) * 128],
                                rhs=qaT[base:base + 64, ci, :], start=True, stop=True),
                                reads=[kk, "qaT"], writes=[f"ps{pi}"])
                        bcol = (C_HB + 4 * j + kb // 4) if kb < NHALO * 4 else C_ZERO
                        m0 = M_DSW + (19 - kb) * 128
                        for hh in range(2):
                            pi = pis[hh]
                            pb_ = ptc[0] % NPT
                            ptc[0] += 1
                            ids.append(pb_)
                            sc.op("act", lambda: nc.scalar.activation(out=pT[pb_][:], in_=psum[pi][:], func=AF.Exp,
                                                                      scale=0.125, bias=ccol(bcol)),
                                  reads=[f"ps{pi}", "cst"], writes=[f"pT{pb_}"])
                            sc.op("pool", lambda: nc.gpsimd.tensor_tensor(out=pT[pb_][:], in0=pT[pb_][:],
                                                                          in1=cmat[:, m0:m0 + 512], op=ALU.mult),
                                  reads=[f"pT{pb_}", "cmat"], writes=[f"pT{pb_}"])
                            a_ = aset + hh
                            if kb == 0:
                                sc.op("dve", lambda: nc.vector.tensor_copy(out=acc[a_][:], in_=pT[pb_][:]),
                                      reads=[f"pT{pb_}"], writes=[f"acc{a_}"])
                            else:
                                sc.op("dve", lambda: nc.vector.tensor_tensor(out=acc[a_][:], in0=acc[a_][:],
                                                                             in1=pT[pb_][:], op=ALU.add),
                                      reads=[f"pT{pb_}", f"acc{a_}"], writes=[f"acc{a_}"])
                        return (kb, ids)

                    def back_a(item):
                        kb, ids = item
                        for hh in range(2):
                            pb_ = ids[hh]
                            sc.op("pe", lambda: nc.tensor.matmul(
                                psum[hh][0:64, :], lhsT=vab[b2][:, kb, hh * 64:(hh + 1) * 64], rhs=pT[pb_][:],
                                start=(kb == 0), stop=(kb == nkb - 1)),
                                reads=[vk, f"pT{pb_}"], writes=[f"ps{hh}"])

                    attn_pipeline(list(range(nkb)), front_a, back_a)
                    recip_from_acc([aset, aset + 1], 64, [0, 1])
                    for hh in range(2):
                        h = 2 * ci + hh
                        sc.op("pool", lambda: nc.gpsimd.tensor_tensor(out=tf[hh][0:64, :], in0=tf[hh][0:64, :],
                                                                      in1=gaT[:, h, :], op=ALU.mult),
                              reads=[f"tf{hh}", "gaT"], writes=[f"tf{hh}"])
                        sc.op("dve", lambda: nc.vector.tensor_tensor(out=yaT[:, h, :], in0=psum[hh][0:64, :],
                                                                     in1=tf[hh][0:64, :], op=ALU.mult),
                              reads=[f"ps{hh}", f"tf{hh}"], writes=["yaT"])

                npos = 8 * j + 8
                for h in range(4):
                    aset = (h % 2) * 2

                    def front_d(st):
                        pos, kb = st
                        kbi = (h * npos + pos) % NKB
                        kk, vk = f"kbuf{kbi}", f"vbuf{kbi}"
                        if kb == 0:
                            sc.dma("sp", kk, kbuf[kbi][:], KD[h, :, pos * TT:(pos + 1) * TT], writes=[kk])
                            sc.dma("sp", vk, vbuf[kbi][:], VDv[:, pos * 4:(pos + 1) * 4, h * 128:(h + 1) * 128],
                                   writes=[vk])
                        diag = (pos == own)
                        bcol = (C_CB + (pos - 8 * j)) if pos >= 8 * j else C_ZERO
                        q0 = kb * 128 if diag else 0
                        pis = []
                        ids = []
                        for comp in range(2):
                            base = comp * 64
                            pi = ps_next(2, 8)
                            pis.append(pi)
                            sc.op("pe", lambda: nc.tensor.matmul(
                                psum[pi][:, q0:TT], lhsT=kbuf[kbi][base:base + 64, kb * 128:(kb + 1) * 128],
                                rhs=qdT[base:base + 64, h, q0:TT], start=True, stop=True),
                                reads=[kk, "qdT"], writes=[f"ps{pi}"])
                        for comp in range(2):
                            pi = pis[comp]
                            pb_ = ptc[0] % NPT
                            ptc[0] += 1
                            ids.append(pb_)
                            sc.op("act", lambda: nc.scalar.activation(
                                out=pT[pb_][:, q0:TT], in_=psum[pi][:, q0:TT], func=AF.Exp, scale=0.125,
                                bias=ccol(bcol)), reads=[f"ps{pi}", "cst"], writes=[f"pT{pb_}"])
                            if diag:
                                sc.op("pool", lambda: nc.gpsimd.tensor_tensor(
                                    out=pT[pb_][:, q0:q0 + 128], in0=pT[pb_][:, q0:q0 + 128], in1=tri,
                                    op=ALU.mult), reads=[f"pT{pb_}", "cmat"], writes=[f"pT{pb_}"])
                            a_ = aset + comp
                            eng = "dve" if comp == 0 else "pool"
                            E_ = nc.vector if comp == 0 else nc.gpsimd
                            if pos == 0 and kb == 0:
                                sc.op(eng, lambda: E_.tensor_copy(out=acc[a_][:], in_=pT[pb_][:]),
                                      reads=[f"pT{pb_}"], writes=[f"acc{a_}"])
                            else:
                                sc.op(eng, lambda: E_.tensor_tensor(out=acc[a_][:, q0:TT], in0=acc[a_][:, q0:TT],
                                                                    in1=pT[pb_][:, q0:TT], op=ALU.add),
                                      reads=[f"pT{pb_}", f"acc{a_}"], writes=[f"acc{a_}"])
                        return (pos, kb, q0, kbi, ids)

                    def back_d(item):
                        pos, kb, q0, kbi, ids = item
                        vk = f"vbuf{kbi}"
                        st = (pos == 0 and kb == 0)
                        last = (pos == npos - 1 and kb == 3)
                        for comp in range(2):
                            pb_ = ids[comp]
                            sc.op("pe", lambda: nc.tensor.matmul(
                                psum[comp][:, q0:TT], lhsT=vbuf[kbi][:, kb, :], rhs=pT[pb_][:, q0:TT],
                                start=st, stop=last), reads=[vk, f"pT{pb_}"], writes=[f"ps{comp}"])

                    attn_pipeline([(pos, kb) for pos in range(npos) for kb in range(4)], front_d, back_d)
                    recip_from_acc([aset, aset + 1], 128, [0, 1])
                    for comp in range(2):
                        sc.op("dve", lambda comp=comp: nc.vector.tensor_tensor(
                            out=tf[2 + comp][:], in0=psum[comp][:], in1=tf[comp][:], op=ALU.mult),
                            reads=[f"ps{comp}", f"tf{comp}"], writes=[f"tf{2 + comp}"])
                    sc.op("dve", lambda: nc.vector.scalar_tensor_tensor(
                        out=tf[4][:], in0=tf[3][:], scalar=lamc[:, 0:1], in1=tf[2][:], op0=ALU.mult, op1=ALU.add),
                        reads=["tf3", "tf2", "lamc"], writes=["tf4"])
                    sc.op("act", lambda: nc.scalar.activation(out=sqo[:], in_=tf[4][:], func=AF.Square),
                          reads=["tf4"], writes=["sqo"])
                    pi = ps_next(2, 8)
                    sc.op("pe", lambda: nc.tensor.matmul(psum[pi][:], lhsT=ones, rhs=sqo[:], start=True, stop=True),
                          reads=["sqo", "cmat"], writes=[f"ps{pi}"])
                    sc.op("act", lambda: nc.scalar.activation(out=tf[5][:], in_=psum[pi][:], func=AF.Ln,
                                                              scale=1.0 / 128, bias=ccol(C_EPSR)),
                          reads=[f"ps{pi}", "cst"], writes=["tf5"])
                    sc.op("act", lambda: nc.scalar.activation(out=tf[5][:], in_=tf[5][:], func=AF.Exp, scale=-0.5),
                          reads=["tf5"], writes=["tf5"])
                    sc.op("pool", lambda: nc.gpsimd.tensor_tensor(out=tf[5][:], in0=tf[5][:], in1=gdT[:, h, :],
                                                                  op=ALU.mult),
                          reads=["tf5", "gdT"], writes=["tf5"])
                    sc.op("dve", lambda: nc.vector.scalar_tensor_tensor(
                        out=ydT[:, h, :], in0=tf[4][:], scalar=lamc[:, 1:2], in1=tf[5][:],
                        op0=ALU.mult, op1=ALU.mult), reads=["tf4", "tf5", "lamc"], writes=["ydT"])

                for dc in range(8):
                    pi = ps_next(2, 8)
                    for h in range(8):
                        sc.op("pe", lambda h=h: nc.tensor.matmul(
                            psum[pi][:], lhsT=woa[:, h, dc * 128:(dc + 1) * 128], rhs=yaT[:, h, :],
                            start=(h == 0), stop=False), reads=["woa", "yaT"], writes=[f"ps{pi}"])
                    for h in range(4):
                        sc.op("pe", lambda h=h: nc.tensor.matmul(
                            psum[pi][:], lhsT=wod[:, h, dc * 128:(dc + 1) * 128], rhs=ydT[:, h, :],
                            start=False, stop=(h == 3)), reads=["wod", "ydT"], writes=[f"ps{pi}"])
                    sc.op("dve", lambda: nc.vector.tensor_tensor(out=x1[:, dc, :], in0=psum[pi][:], in1=x1[:, dc, :],
                                                                 op=ALU.add),
                          reads=[f"ps{pi}", "x1"], writes=["x1"])
                sc.barrier()
            if stop == "p2":
                return nc

            with ExitStack() as p3:
                sq = sb("sq1", [128, 8, TT], BF16, p3)
                h1 = sb("h1", [128, 8, TT], BF16, p3)
                rs = sb("rs1", [128, TT], F32, p3)
                wst = [sb(f"w1st{i}", [128, 4096], BF16, p3) for i in range(2)]
                uT = sb("uT", [128, 16, TT], BF16, p3)
                gsT = sb("gsT", [128, 16, TT], BF16, p3)
                gv = sb("gv", [128, 4, 2048], BF16, p3)
                vtmp = sb("vtmp", [128, 2048], F32, p3)
                lng = sb("lng", [128, 2, 2048], F32, p3)
                wsf = sb("wsf", [128, 8, 128], BF16, p3)
                bsb = sb("bsb_sb", [128, 8, 128], F32, p3)
                stats = sb("stats", [128, 4, 6], F32, p3)
                mv = sb("mv", [128, 4], F32, p3)
                tf = [sb(f"tg{i}", [128, TT], F32, p3) for i in range(3)]
                wsb = [0]

                def load_w1(src_ap, shape3):
                    b = wsb[0] % 2
                    wsb[0] += 1
                    a, bb = shape3
                    view = wst[b][:, 0:a * bb].rearrange("p (a b) -> p a b", b=bb)
                    sc.dma("sp", f"w1st{b}", view, src_ap, writes=[f"w1st{b}"])
                    return view, f"w1st{b}"

                sc.dma("sp", "lng", lng[:, 0, :], lngb_d[0:1, :].partition_broadcast(128), writes=["lng"])
                sc.dma("sp", "lng", lng[:, 1, :], lngb_d[1:2, :].partition_broadcast(128), writes=["lng"])
                sc.dma("pool", "wsf", wsf[:].rearrange("p g t -> p (g t)"), wsT_d, writes=["wsf"])
                sc.dma("sp", "bsb", bsb[:].rearrange("p g t -> p (g t)"), bsb_d.partition_broadcast(128),
                       writes=["bsb"])
                for g in range(8):
                    sc.op("pool", lambda g=g: nc.gpsimd.tensor_tensor(out=wsf[:, g, :], in0=wsf[:, g, :], in1=tri,
                                                                      op=ALU.mult),
                          reads=["wsf", "cmat"], writes=["wsf"])

                rmsnorm_tile(x1, "x1", C_SGUN, sq, "sq1", h1, "h1", rs, "rs1")
                for wc in range(4):
                    w, wk = load_w1(W1v[:, :, 2048 + wc * 512:2048 + (wc + 1) * 512], (8, 512))
                    for sub in range(4):
                        pi = ps_next()
                        for k in range(8):
                            sc.op("pe", lambda k=k: nc.tensor.matmul(
                                psum[pi][:], lhsT=h1[:, k, sub * 128:(sub + 1) * 128], rhs=w[:, k, :],
                                start=(k == 0), stop=(k == 7)), reads=["h1", wk], writes=[f"ps{pi}"])
                        sc.op("act", lambda: nc.scalar.activation(out=gv[:, sub, wc * 512:(wc + 1) * 512],
                                                                  in_=psum[pi][:], func=AF.Gelu),
                              reads=[f"ps{pi}"], writes=[f"gv{sub}"])
                for sub in range(4):
                    for q in range(4):
                        sc.op("dve", lambda q=q: nc.vector.bn_stats(out=stats[:, q, :],
                                                                    in_=gv[:, sub, q * 512:(q + 1) * 512]),
                              reads=[f"gv{sub}"], writes=["stats"])
                    sc.op("dve", lambda: nc.vector.bn_aggr(out=mv[:, 0:2], in_=stats[:].rearrange("p a b -> p (a b)")),
                          reads=["stats"], writes=["mv"])
                    sc.op("act", lambda: nc.scalar.activation(out=mv[:, 2:3], in_=mv[:, 1:2], func=AF.Ln,
                                                              bias=ccol(C_EPSL)),
                          reads=["mv", "cst"], writes=["mv"])
                    sc.op("act", lambda: nc.scalar.activation(out=mv[:, 3:4], in_=mv[:, 2:3], func=AF.Exp, scale=-0.5),
                          reads=["mv"], writes=["mv"])
                    sc.op("dve", lambda: nc.vector.tensor_scalar(
                        out=vtmp[:], in0=gv[:, sub, :], scalar1=mv[:, 0:1], scalar2=mv[:, 3:4],
                        op0=ALU.subtract, op1=ALU.mult), reads=[f"gv{sub}", "mv"], writes=["vtmp"])
                    sc.op("pool", lambda: nc.gpsimd.tensor_tensor(out=vtmp[:], in0=vtmp[:], in1=lng[:, 0, :],
                                                                  op=ALU.mult),
                          reads=["vtmp", "lng"], writes=["vtmp"])
                    sc.op("pool", lambda: nc.gpsimd.tensor_tensor(out=gv[:, sub, :], in0=vtmp[:], in1=lng[:, 1, :],
                                                                  op=ALU.add),
                          reads=["vtmp", "lng"], writes=[f"gv{sub}"])
                for wc in range(4):
                    w, wk = load_w1(W1v[:, :, wc * 512:(wc + 1) * 512], (8, 512))
                    for s4 in range(4):
                        pi = ps_next()
                        proj_fm(pi, h1, "h1", w, wk, s4 * 128)
                        c = wc * 4 + s4
                        sc.op("act", lambda c=c: nc.scalar.activation(out=uT[:, c, :], in_=psum[pi][:], func=AF.Gelu),
                              reads=[f"ps{pi}"], writes=["uT"])
                for wc in range(4):
                    w, wk = load_w1(W1v[:, :, 4096 + wc * 512:4096 + (wc + 1) * 512], (8, 512))
                    for s4 in range(4):
                        pi = ps_next()
                        proj_fm(pi, h1, "h1", w, wk, s4 * 128)
                        c = wc * 4 + s4
                        sc.op("act", lambda c=c: nc.scalar.activation(out=gsT[:, c, :], in_=psum[pi][:], func=AF.Silu),
                              reads=[f"ps{pi}"], writes=["gsT"])
                for c in range(16):
                    g = c // 2
                    pi = ps_next()
                    for sub in range(4):
                        sc.op("pe", lambda sub=sub: nc.tensor.matmul(
                            psum[pi][:, sub * 128:(sub + 1) * 128], lhsT=gv[:, sub, c * 128:(c + 1) * 128],
                            rhs=wsf[:, g, :], start=True, stop=True),
                            reads=[f"gv{sub}", "wsf"], writes=[f"ps{pi}"])
                    tb = c % 3
                    for sub in range(4):
                        sc.op("dve", lambda sub=sub: nc.vector.tensor_tensor(
                            out=tf[tb][:, sub * 128:(sub + 1) * 128], in0=psum[pi][:, sub * 128:(sub + 1) * 128],
                            in1=bsb[:, g, :], op=ALU.add), reads=[f"ps{pi}", "bsb"], writes=[f"tg{tb}"])
                    sc.op("pool", lambda: nc.gpsimd.tensor_tensor(out=gsT[:, c, :], in0=gsT[:, c, :], in1=uT[:, c, :],
                                                                  op=ALU.mult),
                          reads=["gsT", "uT"], writes=["gsT"])
                    sc.op("dve", lambda: nc.vector.tensor_tensor(out=uT[:, c, :], in0=tf[tb][:], in1=gsT[:, c, :],
                                                                 op=ALU.mult),
                          reads=[f"tg{tb}", "gsT"], writes=["uT"])
                for q4 in range(4):
                    w, wk = load_w1(Wo1v[:, :, q4 * 256:(q4 + 1) * 256], (16, 256))
                    for d2 in range(2):
                        dc = q4 * 2 + d2
                        pi = ps_next()
                        for c in range(16):
                            sc.op("pe", lambda c=c: nc.tensor.matmul(
                                psum[pi][:], lhsT=w[:, c, d2 * 128:(d2 + 1) * 128], rhs=uT[:, c, :],
                                start=(c == 0), stop=(c == 15)), reads=[wk, "uT"], writes=[f"ps{pi}"])
                        tb = dc % 3
                        sc.op("dve", lambda: nc.vector.tensor_tensor(out=tf[tb][:], in0=psum[pi][:], in1=x1[:, dc, :],
                                                                     op=ALU.add),
                              reads=[f"ps{pi}", "x1"], writes=[f"tg{tb}"])
                        sc.dma("sp", f"tg{tb}", out_d[dc * 128:(dc + 1) * 128, j * TT:(j + 1) * TT], tf[tb][:],
                               reads=[f"tg{tb}"], writes=["out"])
                sc.barrier()
            if stop == "p3":
                return nc
    return nc


def _tile_order(c):
    order = []
    for j in range(NSLOT):
        order += [8 * j + i for i in range(c + 1, 8)] + [8 * j + i for i in range(0, c)] + [8 * j + c]
    return order


def _rope_tables(positions):
    rot = 16
    inv = 1.0 / (500000.0 ** (np.arange(0, rot, 2, dtype=np.float32) / rot))
    ang = positions.astype(np.float32)[:, None] * inv[None, :].astype(np.float32)
    cos, sin = np.cos(ang).astype(np.float32), np.sin(ang).astype(np.float32)
    T = positions.shape[0]
    cf = np.ones((128, T), np.float32)
    sf = np.zeros((128, T), np.float32)
    for hb in (0, 64):
        cf[hb:hb + 8] = cos.T
        cf[hb + 8:hb + 16] = cos.T
        sf[hb:hb + 8] = sin.T
        sf[hb + 8:hb + 16] = sin.T
    return np.stack([cf, sf], 0)


def _const_mats():
    cm = np.zeros((128, NCM), np.float32)
    cm[:, M_ONES:M_ONES + 128] = 1.0
    cm[0:64, M_BONES:M_BONES + 64] = 1.0
    cm[64:128, M_BONES + 64:M_BONES + 128] = 1.0
    for m in range(128):
        dl = m % 64
        if dl < 8:
            cm[m + 8, M_ROT + m] = -1.0
        elif dl < 16:
            cm[m - 8, M_ROT + m] = 1.0
    kk = np.arange(128)[:, None]
    qq = np.arange(128)[None, :]
    cm[:, M_TRI:M_TRI + 128] = (kk <= qq).astype(np.float32)
    for idx in range(23):
        delta = idx - 3
        d = 128 * delta + qq - kk
        m = ((d >= 0) & (d <= 128)).astype(np.float32)
        m += ((d % 4 == 0) & (d >= 0) & (d <= 512)).astype(np.float32)
        m += ((d % 16 == 0) & (d >= 0) & (d <= 2048)).astype(np.float32)
        cm[:, M_DSW + idx * 128:M_DSW + (idx + 1) * 128] = m
    return cm.astype(ml_dtypes.bfloat16)


_NC_CACHE = {}


def _prep(x, att_norm, att_w_in, dsw_q_norm, dsw_k_norm, diff_q_norm, diff_k_norm,
           diff_lam_q1, diff_lam_k1, diff_lam_q2, diff_lam_k2, diff_subln, att_w_out,
           sgu_norm, sgu_w_in, sgu_ln_g, sgu_ln_b, sgu_w_s, sgu_b_s, sgu_w_out):
    f = np.float32
    xT = np.ascontiguousarray(np.asarray(x, f)[0].T)
    xT_tiles = xT.reshape(D, NT, TT)
    cm = _const_mats()
    w_in0 = np.ascontiguousarray(np.asarray(att_w_in, f)[0])
    w_out0 = np.ascontiguousarray(np.asarray(att_w_out, f)[0])
    w_in1 = np.ascontiguousarray(np.asarray(sgu_w_in, f)[0])
    w_out1 = np.ascontiguousarray(np.asarray(sgu_w_out, f)[0])
    lngb = np.ascontiguousarray(np.stack([np.asarray(sgu_ln_g, f)[0], np.asarray(sgu_ln_b, f)[0]], 0))
    wsT = np.ascontiguousarray(np.asarray(sgu_w_s, f)[0].transpose(2, 0, 1).reshape(128, 8 * 128))
    bsb = np.ascontiguousarray(np.asarray(sgu_b_s, f)[0].reshape(1, 8 * 128))
    in_maps = []
    orders = []
    for c in range(NCORE):
        order = _tile_order(c)
        orders.append(order)
        x_all = np.ascontiguousarray(xT_tiles[:, order, :].reshape(D, S))
        pos_all = (np.asarray(order)[:, None] * TT + np.arange(TT)[None, :]).reshape(-1)
        halo_tiles = []
        for j in range(NSLOT):
            halo_tiles += [8 * j + c - NHALO + m for m in range(NHALO)]
        x_halo = np.zeros((D, NSLOT * NHALO, TT), f)
        pos_halo = np.zeros((NSLOT * NHALO, TT), np.int64)
        for i, t in enumerate(halo_tiles):
            if t >= 0:
                x_halo[:, i, :] = xT_tiles[:, t, :]
                pos_halo[i] = t * TT + np.arange(TT)
        cst = np.zeros((128, NCST), f)
        cst[:, C_ATTN:C_ATTN + 8] = np.asarray(att_norm, f)[0].reshape(8, 128).T
        cst[:, C_SGUN:C_SGUN + 8] = np.asarray(sgu_norm, f)[0].reshape(8, 128).T
        cst[:, C_DSWQ] = np.tile(np.asarray(dsw_q_norm, f)[0], 2)
        cst[:, C_DSWK] = np.tile(np.asarray(dsw_k_norm, f)[0], 2)
        cst[:, C_DIFQ] = np.tile(np.asarray(diff_q_norm, f)[0], 2)
        cst[:, C_DIFK] = np.tile(np.asarray(diff_k_norm, f)[0], 2)
        cst[:, C_SUBLN] = np.asarray(diff_subln, f)[0]
        for i in range(8):
            cst[:, C_CB + i] = 0.0 if (i >= 7 - c) else NEG
        for j in range(NSLOT):
            for m in range(NHALO):
                cst[:, C_HB + 4 * j + m] = 0.0 if (8 * j + c - NHALO + m) >= 0 else NEG
        cst[:, C_LQ1:C_LQ1 + 64] = np.asarray(diff_lam_q1, f)[0][None, :]
        cst[:, C_LK1:C_LK1 + 64] = np.asarray(diff_lam_k1, f)[0][None, :]
        cst[:, C_LQ2:C_LQ2 + 64] = np.asarray(diff_lam_q2, f)[0][None, :]
        cst[:, C_LK2:C_LK2 + 64] = np.asarray(diff_lam_k2, f)[0][None, :]
        cst[:, C_EPSR] = RMS_EPS
        cst[:, C_EPSL] = LN_EPS
        cst[:, C_ONE] = 1.0
        in_maps.append({
            "x_all": x_all,
            "x_halo": np.ascontiguousarray(x_halo.reshape(D, NSLOT * NHALO * TT)),
            "rope_all": _rope_tables(pos_all),
            "rope_halo": _rope_tables(pos_halo.reshape(-1)),
            "w_in0": w_in0, "w_out0": w_out0, "w_in1": w_in1, "w_out1": w_out1,
            "cst": cst, "cmat": cm, "lngb": lngb, "wsT": wsT, "bsb": bsb,
        })
    return in_maps


def kernel(**inputs):
    f = np.float32
    in_maps = _prep(**inputs)
    if "nc" not in _NC_CACHE:
        _NC_CACHE["nc"] = build_program()
    nc = _NC_CACHE["nc"]
    res = run_bass_kernel_spmd(nc, in_maps, core_ids=list(range(NCORE)))
    out = np.zeros((S, D), f)
    for c in range(NCORE):
        oT = np.asarray(res.results[c]["outT"], f)
        for j in range(NSLOT):
            t = 8 * j + c
            out[t * TT:(t + 1) * TT, :] = oT[:, j * TT:(j + 1) * TT].T
    return out[None]
```

```python
import math
from contextlib import ExitStack

import ml_dtypes
import numpy as np

import concourse.bass as bass
import concourse.mybir as mybir
from concourse.bass_utils import run_bass_kernel_spmd

F32 = mybir.dt.float32
BF16 = mybir.dt.bfloat16
AF = mybir.ActivationFunctionType
ALU = mybir.AluOpType

S = 16384
D = 1024
TT = 512
NT = S // TT
NCORE = 8
NSLOT = NT // NCORE
NHALO = 4
NEG = -30000.0
RMS_EPS = 1e-6
LN_EPS = 1e-5
LAM_INIT0 = 0.8 - 0.6 * math.exp(-0.3 * 0)

C_ATTN = 0
C_SGUN = 8
C_DSWQ = 16
C_DSWK = 17
C_DIFQ = 18
C_DIFK = 19
C_SUBLN = 20
C_CB = 21
C_HB = 29
C_ZERO = 45
C_LQ1 = 46
C_LK1 = 110
C_LQ2 = 174
C_LK2 = 238
C_EPSR = 302
C_EPSL = 303
C_ONE = 304
C_SHF = 305
NCST = 305 + 64
M_ONES = 0
M_BONES = 128
M_ROT = 256
M_TRI = 384
M_DSW = 512
NCM = 512 + 23 * 128


class Sched:
    def __init__(self, nc, es):
        self.nc = nc
        self.es = es
        self.E = {"pe": nc.tensor, "act": nc.scalar, "dve": nc.vector, "pool": nc.gpsimd, "sp": nc.sync}
        self.semh = {}
        self.cnt = {}
        for e in self.E:
            self.semh[e] = es.enter_context(nc.semaphore("s_" + e))
            self.cnt[e] = 0
        self.known = {e: {} for e in self.E}
        self.lastw = {}
        self.readers = {}
        self.nwait = 0

    def dsem(self, name):
        k = "d_" + name
        if k not in self.semh:
            self.semh[k] = self.es.enter_context(self.nc.semaphore(k))
            self.cnt[k] = 0
        return k

    def _wait(self, eng, deps):
        need = {}
        for (sk, val) in deps:
            if sk == eng and eng in ("pe", "sp"):
                continue
            if val > need.get(sk, 0):
                need[sk] = val
        for sk, val in need.items():
            if self.known[eng].get(sk, 0) >= val:
                continue
            self.E[eng].wait_ge(self.semh[sk], val)
            self.known[eng][sk] = val
            self.nwait += 1

    def _deps(self, reads, writes):
        deps = []
        for k in reads:
            if k in self.lastw:
                deps.append(self.lastw[k])
        for k in writes:
            if k in self.lastw:
                deps.append(self.lastw[k])
            deps.extend(self.readers.get(k, {}).items())
        return deps

    def _record(self, ev, reads, writes):
        for k in reads:
            r = self.readers.setdefault(k, {})
            if ev[1] > r.get(ev[0], 0):
                r[ev[0]] = ev[1]
        for k in writes:
            self.lastw[k] = ev
            self.readers[k] = {}

    def op(self, eng, fn, reads=(), writes=()):
        self._wait(eng, self._deps(reads, writes))
        ins = fn()
        self.cnt[eng] += 1
        ins.then_inc(self.semh[eng], 1)
        self._record((eng, self.cnt[eng]), reads, writes)

    def dma(self, q, sem, out, in_, reads=(), writes=()):
        sk = self.dsem(sem)
        self._wait(q, self._deps(reads, writes))
        ins = self.E[q].dma_start(out=out, in_=in_)
        self.cnt[sk] += 16
        ins.then_inc(self.semh[sk], 16)
        self._record((sk, self.cnt[sk]), reads, writes)

    def barrier(self):
        for e in self.E:
            for sk, c in self.cnt.items():
                if sk == e and e in ("pe", "sp"):
                    continue
                if c > 0 and self.known[e].get(sk, 0) < c:
                    self.E[e].wait_ge(self.semh[sk], c)
                    self.known[e][sk] = c
        self.lastw = {}
        self.readers = {}


def build_program(stop=None):
    nc = bass.Bass("TRN2", target_bir_lowering=False)
    x_all = nc.dram_tensor("x_all", [D, S], F32, kind="ExternalInput").ap()
    x_halo = nc.dram_tensor("x_halo", [D, NSLOT * NHALO * TT], F32, kind="ExternalInput").ap()
    rope_all = nc.dram_tensor("rope_all", [2, 128, S], F32, kind="ExternalInput").ap()
    rope_halo = nc.dram_tensor("rope_halo", [2, 128, NSLOT * NHALO * TT], F32, kind="ExternalInput").ap()
    w_in0 = nc.dram_tensor("w_in0", [D, 4096], F32, kind="ExternalInput").ap()
    w_out0 = nc.dram_tensor("w_out0", [D, D], F32, kind="ExternalInput").ap()
    w_in1 = nc.dram_tensor("w_in1", [D, 6144], F32, kind="ExternalInput").ap()
    w_out1 = nc.dram_tensor("w_out1", [2048, D], F32, kind="ExternalInput").ap()
    cst_d = nc.dram_tensor("cst", [128, NCST], F32, kind="ExternalInput").ap()
    cmat_d = nc.dram_tensor("cmat", [128, NCM], BF16, kind="ExternalInput").ap()
    lngb_d = nc.dram_tensor("lngb", [2, 2048], F32, kind="ExternalInput").ap()
    wsT_d = nc.dram_tensor("wsT", [128, 8 * 128], F32, kind="ExternalInput").ap()
    bsb_d = nc.dram_tensor("bsb", [1, 8 * 128], F32, kind="ExternalInput").ap()
    out_d = nc.dram_tensor("outT", [D, NSLOT * TT], F32, kind="ExternalOutput").ap()
    W0b = nc.dram_tensor("W0b", [D, 4096], BF16).ap()
    Wo0b = nc.dram_tensor("Wo0b", [D, D], BF16).ap()
    W1b = nc.dram_tensor("W1b", [D, 6144], BF16).ap()
    Wo1b = nc.dram_tensor("Wo1b", [2048, D], BF16).ap()
    KD = nc.dram_tensor("KD", [4, 128, S], BF16).ap()
    VD = nc.dram_tensor("VD", [S, 512], BF16).ap()
    NKA = NSLOT * (NHALO + 1) * TT
    KA = nc.dram_tensor("KA", [4, 128, NKA], BF16).ap()
    VA = nc.dram_tensor("VA", [NKA, 512], BF16).ap()
    QA = nc.dram_tensor("QA", [4, 128, NSLOT * TT], BF16).ap()
    QD = nc.dram_tensor("QD", [4, 128, NSLOT * TT], BF16).ap()
    GA = nc.dram_tensor("GA", [8, 64, NSLOT * TT], BF16).ap()
    GD = nc.dram_tensor("GD", [4, 128, NSLOT * TT], BF16).ap()

    with ExitStack() as es:
        es.enter_context(nc.allow_low_precision("bf16 matmul operands, fp32 accumulation"))
        es.enter_context(nc.allow_non_contiguous_dma("strided weight / scratch tiles"))
        sc = Sched(nc, es)

        uid = [0]

        def sb(name, shape, dt, stack=es):
            uid[0] += 1
            return stack.enter_context(nc.sbuf_tensor(f"{name}_{uid[0]}", shape, dt))

        cst = sb("cst_sb", [128, NCST], F32)
        cmat = sb("cmat_sb", [128, NCM], BF16)
        lamc = sb("lamc", [128, 8], F32)
        x1 = sb("x1", [128, 8, TT], F32)
        onesf = sb("onesf", [128, 128], F32)
        psum = [es.enter_context(nc.psum_tensor(f"ps{i}", [128, TT], F32)) for i in range(8)]
        ps_rr = [0]

        def ps_next(lo=0, hi=8):
            i = lo + (ps_rr[0] % (hi - lo))
            ps_rr[0] += 1
            return i

        ones = cmat[:, M_ONES:M_ONES + 128]
        bones = cmat[:, M_BONES:M_BONES + 128]
        rotm = cmat[:, M_ROT:M_ROT + 128]
        tri = cmat[:, M_TRI:M_TRI + 128]

        def ccol(c, n=128):
            return cst[0:n, c:c + 1]

        sc.dma("sp", "cst", cst[:], cst_d, writes=["cst"])
        sc.dma("sp", "cmat", cmat[:], cmat_d, writes=["cmat"])
        for (src, dst, rows, nm) in ((w_in0, W0b, D, "W0b"), (w_out0, Wo0b, D, "Wo0b"),
                                     (w_in1, W1b, D, "W1b"), (w_out1, Wo1b, 2048, "Wo1b")):
            step = 128
            for r0 in range(0, rows, step):
                sc.dma("pool", "wcast", dst[r0:r0 + step, :], src[r0:r0 + step, :], writes=[nm])
                done = sc.cnt["d_wcast"] - 32
                if done > 0:
                    nc.gpsimd.wait_ge(sc.semh["d_wcast"], done)
                    sc.known["pool"]["d_wcast"] = done
        sc.op("dve", lambda: nc.vector.memset(onesf[:], 1.0), writes=["onesf"])
        with ExitStack() as ls:
            lt = sb("lam_t", [128, 128], F32, ls)
            for i, (ca, cb) in enumerate(((C_LQ1, C_LK1), (C_LQ2, C_LK2))):
                sc.op("dve", lambda ca=ca, cb=cb, i=i: nc.vector.tensor_tensor(
                    out=lt[:, i * 64:(i + 1) * 64], in0=cst[:, ca:ca + 64], in1=cst[:, cb:cb + 64], op=ALU.mult),
                    reads=["cst"], writes=["lt"])
            for i in range(2):
                sc.op("dve", lambda i=i: nc.vector.reduce_sum(
                    out=lamc[:, 2 + i:3 + i], in_=lt[:, i * 64:(i + 1) * 64], axis=mybir.AxisListType.X),
                    reads=["lt"], writes=["lamc"])
            sc.op("act", lambda: nc.scalar.activation(out=lamc[:, 4:6], in_=lamc[:, 2:4], func=AF.Exp),
                  reads=["lamc"], writes=["lamc"])
            sc.op("dve", lambda: nc.vector.scalar_tensor_tensor(
                out=lamc[:, 0:1], in0=lamc[:, 5:6], scalar=-LAM_INIT0, in1=lamc[:, 4:5],
                op0=ALU.add, op1=ALU.subtract), reads=["lamc"], writes=["lamc"])
            sc.op("dve", lambda: nc.vector.tensor_scalar(
                out=lamc[:, 1:2], in0=cst[:, C_SUBLN:C_SUBLN + 1], scalar1=1.0 - LAM_INIT0, scalar2=None,
                op0=ALU.mult), reads=["cst", "lamc"], writes=["lamc"])
            sc.barrier()

        def rmsnorm_tile(xin, xk, gcol0, sq, sqk, hT, hk, rs, rsk, ssum, ssk):
            sc.op("act", lambda: nc.scalar.activation(out=sq[:], in_=xin[:], func=AF.Square),
                  reads=[xk], writes=[sqk])
            sc.op("dve", lambda: nc.vector.reduce_sum(out=ssum[:], in_=sq[:].rearrange("p k t -> p t k"),
                                                      axis=mybir.AxisListType.X),
                  reads=[sqk], writes=[ssk])
            pi = ps_next()
            sc.op("pe", lambda: nc.tensor.matmul(psum[pi][:], lhsT=ones, rhs=ssum[:], start=True, stop=True),
                  reads=[ssk, "cmat"], writes=[f"ps{pi}"])
            sc.op("act", lambda: nc.scalar.activation(out=rs[:], in_=psum[pi][:], func=AF.Ln,
                                                      scale=1.0 / D, bias=ccol(C_EPSR)),
                  reads=[f"ps{pi}", "cst"], writes=[rsk])
            sc.op("act", lambda: nc.scalar.activation(out=rs[:], in_=rs[:], func=AF.Exp, scale=-0.5),
                  reads=[rsk], writes=[rsk])
            for k in range(8):
                sc.op("dve", lambda k=k: nc.vector.scalar_tensor_tensor(
                    out=hT[:, k, :], in0=xin[:, k, :], scalar=cst[:, gcol0 + k:gcol0 + k + 1], in1=rs[:],
                    op0=ALU.mult, op1=ALU.mult), reads=[xk, rsk, "cst"], writes=[hk])

        def proj_fm(pi, hT, hk, w, wk, c0, m=128, prow=128):
            for k in range(8):
                sc.op("pe", lambda k=k: nc.tensor.matmul(psum[pi][0:m, :], lhsT=w[:, k, c0:c0 + m], rhs=hT[:, k, :],
                                                       start=(k == 0), stop=(k == 7)),
                      reads=[hk, wk], writes=[f"ps{pi}"])

        if stop == "setup":
            return nc
        with ExitStack() as p1:
            xin = [sb(f"xin{i}", [128, 8, TT], F32, p1) for i in range(2)]
            sq = [sb(f"sq{i}", [128, 8, TT], BF16, p1) for i in range(2)]
            hTs = [sb(f"hT{i}", [128, 8, TT], BF16, p1) for i in range(2)]
            rs = [sb(f"rs{i}", [128, TT], F32, p1) for i in range(2)]
            ssm = [sb(f"ssm{i}", [128, TT], BF16, p1) for i in range(2)]
            wkvd = sb("wkvd", [128, 8, 1024], BF16, p1)
            wkva = sb("wkva", [128, 8, 1024], BF16, p1)
            wst = [sb(f"wst{i}", [128, 8, 512], BF16, p1) for i in range(2)]
            rope = [sb(f"rope{i}", [128, 2, TT], F32, p1) for i in range(2)]
            NPB = 3
            sqc = [sb(f"sqc{i}", [128, TT], BF16, p1) for i in range(NPB)]
            qa_ = [sb(f"qA{i}", [128, TT], BF16, p1) for i in range(NPB)]
            rsc = [sb(f"rsc{i}", [128, TT], F32, p1) for i in range(NPB)]
            t1 = [sb(f"t1{i}", [128, TT], F32, p1) for i in range(NPB)]
            t2 = [sb(f"t2{i}", [128, TT], F32, p1) for i in range(NPB)]
            ko = [sb(f"ko{i}", [128, TT], BF16, p1) for i in range(NPB)]
            vst = [sb(f"vst{i}", [128, TT], BF16, p1) for i in range(2)]
            pb = [0]
            vb = [0]
            wsb = [0]

            W0v = W0b.rearrange("(k p) e -> p k e", p=128)
            sc.dma("sp", "wkvd", wkvd[:, :, 0:512], W0v[:, :, 2560:3072], reads=["W0b"], writes=["wkvd"])
            sc.dma("sp", "wkvd", wkvd[:, :, 512:1024], W0v[:, :, 3072:3584], reads=["W0b"], writes=["wkvd"])
            sc.dma("sp", "wkva", wkva[:, :, 0:512], W0v[:, :, 512:1024], reads=["W0b"], writes=["wkva"])
            sc.dma("sp", "wkva", wkva[:, :, 512:1024], W0v[:, :, 1024:1536], reads=["W0b"], writes=["wkva"])

            deferred = []

            def flush_deferred():
                while deferred:
                    deferred.pop(0)()

            def qk_post(pi, gcol, ropet, ropek, dst_ap, dkey):
                b = pb[0] % NPB
                pb[0] += 1
                sc.op("act", lambda: nc.scalar.activation(out=sqc[b][:], in_=psum[pi][:], func=AF.Square),
                      reads=[f"ps{pi}"], writes=[f"sqc{b}"])
                sc.op("act", lambda: nc.scalar.activation(out=qa_[b][:], in_=psum[pi][:], func=AF.Identity,
                                                          scale=ccol(gcol)),
                      reads=[f"ps{pi}", "cst"], writes=[f"qA{b}"])
                flush_deferred()
                deferred.append(lambda: qk_post_b(b, ropet, ropek, dst_ap, dkey))

            def qk_post_b(b, ropet, ropek, dst_ap, dkey):
                p2 = ps_next()
                sc.op("pe", lambda: nc.tensor.matmul(psum[p2][:], lhsT=bones, rhs=sqc[b][:], start=True, stop=True),
                      reads=[f"sqc{b}", "cmat"], writes=[f"ps{p2}"])
                p3 = ps_next()
                sc.op("pe", lambda: nc.tensor.matmul(psum[p3][:], lhsT=rotm, rhs=qa_[b][:], start=True, stop=True),
                      reads=[f"qA{b}", "cmat"], writes=[f"ps{p3}"])
                sc.op("act", lambda: nc.scalar.activation(out=rsc[b][:], in_=psum[p2][:], func=AF.Ln,
                                                          scale=1.0 / 64, bias=ccol(C_EPSR)),
                      reads=[f"ps{p2}", "cst"], writes=[f"rsc{b}"])
                sc.op("act", lambda: nc.scalar.activation(out=rsc[b][:], in_=rsc[b][:], func=AF.Exp, scale=-0.5),
                      reads=[f"rsc{b}"], writes=[f"rsc{b}"])
                sc.op("pool", lambda: nc.gpsimd.tensor_tensor(out=t1[b][:], in0=qa_[b][:], in1=ropet[:, 0, :],
                                                              op=ALU.mult),
                      reads=[f"qA{b}", ropek], writes=[f"t1{b}"])
                sc.op("dve", lambda: nc.vector.tensor_tensor(out=t2[b][:], in0=psum[p3][:], in1=ropet[:, 1, :],
                                                             op=ALU.mult),
                      reads=[f"ps{p3}", ropek], writes=[f"t2{b}"])
                sc.op("pool", lambda: nc.gpsimd.tensor_tensor(out=t1[b][:], in0=t1[b][:], in1=t2[b][:], op=ALU.add),
                      reads=[f"t1{b}", f"t2{b}"], writes=[f"t1{b}"])
                sc.op("dve", lambda: nc.vector.tensor_tensor(out=ko[b][:], in0=t1[b][:], in1=rsc[b][:], op=ALU.mult),
                      reads=[f"t1{b}", f"rsc{b}"], writes=[f"ko{b}"])
                sc.dma("sp", f"ko{b}", dst_ap, ko[b][:], reads=[f"ko{b}"], writes=[dkey])

            def v_tm(hT, hk, w, wk, c0, dst_rows_fn, dkey):
                for sub in range(4):
                    pi = ps_next()
                    for k in range(8):
                        sc.op("pe", lambda k=k: nc.tensor.matmul(
                            psum[pi][:], lhsT=hT[:, k, sub * 128:(sub + 1) * 128], rhs=w[:, k, c0:c0 + 512],
                            start=(k == 0), stop=(k == 7)), reads=[hk, wk], writes=[f"ps{pi}"])
                    flush_deferred()
                    b = vb[0] % 2
                    vb[0] += 1
                    sc.op("act", lambda: nc.scalar.activation(out=vst[b][:], in_=psum[pi][:], func=AF.Copy),
                          reads=[f"ps{pi}"], writes=[f"vst{b}"])
                    sc.dma("sp", f"vst{b}", dst_rows_fn(sub), vst[b][:], reads=[f"vst{b}"], writes=[dkey])

            def silu_fm(pi, m, dst_ap, dkey):
                flush_deferred()
                b = pb[0] % NPB
                pb[0] += 1
                sc.op("act", lambda: nc.scalar.activation(out=ko[b][0:m, :], in_=psum[pi][0:m, :], func=AF.Silu),
                      reads=[f"ps{pi}"], writes=[f"ko{b}"])
                sc.dma("sp", f"ko{b}", dst_ap, ko[b][0:m, :], reads=[f"ko{b}"], writes=[dkey])

            def load_wst(c0):
                b = wsb[0] % 2
                wsb[0] += 1
                sc.dma("sp", f"wst{b}", wst[b][:], W0v[:, :, c0:c0 + 512], reads=["W0b"], writes=[f"wst{b}"])
                return wst[b], f"wst{b}"

            tiles = [("all", pos) for pos in range(NT)] + [("halo", i) for i in range(NSLOT * NHALO)]
            if stop == "p1a":
                tiles = tiles[:8]
            xav = x_all.rearrange("(k p) t -> p k t", p=128)
            xhv = x_halo.rearrange("(k p) t -> p k t", p=128)

            def load_x(ti):
                kind, idx = tiles[ti]
                xb = ti % 2
                srcx = xav if kind == "all" else xhv
                sc.dma("sp", f"xin{xb}", xin[xb][:], srcx[:, :, idx * TT:(idx + 1) * TT], writes=[f"xin{xb}"])

            def load_rope(ti):
                kind, idx = tiles[ti]
                xb = ti % 2
                srcr = rope_all if kind == "all" else rope_halo
                sc.dma("sp", f"rope{xb}", rope[xb][:],
                       srcr.rearrange("c p t -> p c t")[:, :, idx * TT:(idx + 1) * TT], writes=[f"rope{xb}"])

            def front(ti):
                xb = ti % 2
                rmsnorm_tile(xin[xb], f"xin{xb}", C_ATTN, sq[xb], f"sq{xb}", hTs[xb], f"hT{xb}", rs[xb], f"rs{xb}",
                             ssm[xb], f"ssm{xb}")

            load_x(0)
            load_rope(0)
            front(0)
            if len(tiles) > 1:
                load_x(1)
            for ti, (kind, idx) in enumerate(tiles):
                xb = ti % 2
                xk = f"xin{xb}"
                rk = f"rope{xb}"
                hT, hk = hTs[xb], f"hT{xb}"
                if ti + 1 < len(tiles):
                    load_rope(ti + 1)
                    front(ti + 1)
                if ti + 2 < len(tiles):
                    load_x(ti + 2)
                if kind == "all":
                    pos = idx
                    for ch in range(4):
                        pi = ps_next()
                        proj_fm(pi, hT, hk, wkvd, "wkvd", ch * 128)
                        qk_post(pi, C_DIFK, rope[xb], rk, KD[ch, :, pos * TT:(pos + 1) * TT], "KD")
                    v_tm(hT, hk, wkvd, "wkvd", 512, lambda sub, pos=pos: VD[(pos * 4 + sub) * 128:(pos * 4 + sub + 1) * 128, :],
                         "VD")
                    if pos % 8 == 7:
                        j = pos // 8
                        kpos = j * (NHALO + 1) + NHALO
                        for ch in range(4):
                            pi = ps_next()
                            proj_fm(pi, hT, hk, wkva, "wkva", ch * 128)
                            qk_post(pi, C_DSWK, rope[xb], rk, KA[ch, :, kpos * TT:(kpos + 1) * TT], "KA")
                        v_tm(hT, hk, wkva, "wkva", 512,
                             lambda sub, kpos=kpos: VA[(kpos * 4 + sub) * 128:(kpos * 4 + sub + 1) * 128, :], "VA")
                        w, wk = load_wst(0)
                        for ch in range(4):
                            pi = ps_next()
                            proj_fm(pi, hT, hk, w, wk, ch * 128)
                            qk_post(pi, C_DSWQ, rope[xb], rk, QA[ch, :, j * TT:(j + 1) * TT], "QA")
                        w, wk = load_wst(1536)
                        for h in range(8):
                            pi = ps_next()
                            proj_fm(pi, hT, hk, w, wk, h * 64, m=64)
                            silu_fm(pi, 64, GA[h, :, j * TT:(j + 1) * TT], "GA")
                        w, wk = load_wst(2048)
                        for ch in range(4):
                            pi = ps_next()
                            proj_fm(pi, hT, hk, w, wk, ch * 128)
                            qk_post(pi, C_DIFQ, rope[xb], rk, QD[ch, :, j * TT:(j + 1) * TT], "QD")
                        w, wk = load_wst(3584)
                        for ch in range(4):
                            pi = ps_next()
                            proj_fm(pi, hT, hk, w, wk, ch * 128)
                            silu_fm(pi, 128, GD[ch, :, j * TT:(j + 1) * TT], "GD")
                else:
                    j, m = idx // NHALO, idx % NHALO
                    kpos = j * (NHALO + 1) + m
                    for ch in range(4):
                        pi = ps_next()
                        proj_fm(pi, hT, hk, wkva, "wkva", ch * 128)
                        qk_post(pi, C_DSWK, rope[xb], rk, KA[ch, :, kpos * TT:(kpos + 1) * TT], "KA")
                    v_tm(hT, hk, wkva, "wkva", 512,
                         lambda sub, kpos=kpos: VA[(kpos * 4 + sub) * 128:(kpos * 4 + sub + 1) * 128, :], "VA")
                flush_deferred()
            sc.barrier()

        if stop in ("p1", "p1a"):
            return nc
        Wo0v_a = Wo0b[0:512, :].rearrange("(h d) e -> d h e", d=64)
        Wo0v_d = Wo0b[512:1024, :].rearrange("(h d) e -> d h e", d=128)
        VDv = VD.rearrange("(n p) e -> p n e", p=128)
        VAv = VA.rearrange("(n p) e -> p n e", p=128)
        W1v = W1b.rearrange("(k p) e -> p k e", p=128)
        Wo1v = Wo1b.rearrange("(c p) e -> p c e", p=128)
        xav = x_all.rearrange("(k p) t -> p k t", p=128)
        for j in range(NSLOT):
            own = 8 * j + 7
            with ExitStack() as p2:
                qaT = sb("qaT", [128, 4, TT], BF16, p2)
                qdT = sb("qdT", [128, 4, TT], BF16, p2)
                gaT = sb("gaT", [64, 8, TT], BF16, p2)
                gdT = sb("gdT", [128, 4, TT], BF16, p2)
                yaT = sb("yaT", [64, 8, TT], BF16, p2)
                ydT = sb("ydT", [128, 4, TT], BF16, p2)
                woa = sb("woa", [64, 8, D], BF16, p2)
                wod = sb("wod", [128, 4, D], BF16, p2)
                NKB = 3
                kbuf = [sb(f"kbuf{i}", [128, TT], BF16, p2) for i in range(NKB)]
                vbuf = [sb(f"vbuf{i}", [128, 4, 128], BF16, p2) for i in range(NKB)]
                kab = [sb(f"kab{i}", [128, (NHALO + 1) * TT], BF16, p2) for i in range(2)]
                vab = [sb(f"vab{i}", [128, (NHALO + 1) * 4, 2, 128], BF16, p2) for i in range(2)]
                Rt = [sb(f"Rt{i}", [128, TT], F32, p2) for i in range(2)]
                for i in range(2):
                    sc.op("dve", lambda i=i: nc.vector.memset(Rt[i][:], 0.0), writes=[f"Rt{i}"])
                    sc.op("pool", lambda i=i: nc.gpsimd.memset(vab[i][:, :, :, 64:128], 1.0), writes=[f"vab{i}"])
                LAG = 2
                NPT = 2 * (LAG + 2)
                pT = [sb(f"pT{i}", [128, TT], BF16, p2) for i in range(NPT)]
                acc = [sb(f"acc{i}", [128, TT], F32, p2) for i in range(4)]
                tf = [sb(f"tf{i}", [128, TT], F32, p2) for i in range(8)]
                sqo = sb("sqo", [128, TT], BF16, p2)
                ptc = [0]

                sc.dma("sp", "qaT", qaT[:], QA.rearrange("c p t -> p c t")[:, :, j * TT:(j + 1) * TT], writes=["qaT"])
                sc.dma("sp", "qdT", qdT[:], QD.rearrange("c p t -> p c t")[:, :, j * TT:(j + 1) * TT], writes=["qdT"])
                sc.dma("sp", "gaT", gaT[:], GA.rearrange("c p t -> p c t")[:, :, j * TT:(j + 1) * TT], writes=["gaT"])
                sc.dma("sp", "gdT", gdT[:], GD.rearrange("c p t -> p c t")[:, :, j * TT:(j + 1) * TT], writes=["gdT"])
                sc.dma("sp", "woa", woa[:], Wo0v_a, writes=["woa"])
                sc.dma("sp", "wod", wod[:], Wo0v_d, writes=["wod"])
                sc.dma("sp", "x1", x1[:], xav[:, :, own * TT:(own + 1) * TT], writes=["x1"])

                def attn_pipeline(steps, emit_front, emit_back):
                    pend = []
                    for st in steps:
                        pend.append(emit_front(st))
                        if len(pend) > LAG:
                            emit_back(pend.pop(0))
                    while pend:
                        emit_back(pend.pop(0))

                def recip_from_acc(accs, m, outs):
                    for a_, o_ in zip(accs, outs):
                        pi = ps_next(2, 8)
                        sc.op("pe", lambda: nc.tensor.matmul(psum[pi][0:m, :], lhsT=onesf[:, 0:m], rhs=acc[a_][:],
                                                             start=True, stop=True),
                              reads=[f"acc{a_}", "onesf"], writes=[f"ps{pi}"])
                        sc.op("act", lambda: nc.scalar.activation(out=tf[o_][0:m, :], in_=psum[pi][0:m, :],
                                                                  func=AF.Ln),
                              reads=[f"ps{pi}"], writes=[f"tf{o_}"])
                        sc.op("act", lambda: nc.scalar.activation(out=tf[o_][0:m, :], in_=tf[o_][0:m, :],
                                                                  func=AF.Exp, scale=-1.0),
                              reads=[f"tf{o_}"], writes=[f"tf{o_}"])

                nkb = (NHALO + 1) * 4
                for ci in range(4):
                    b2 = ci % 2
                    kk, vk = f"kab{b2}", f"vab{b2}"
                    sc.dma("sp", kk, kab[b2][:], KA[ci, :, j * nkb * 128:(j + 1) * nkb * 128], writes=[kk])
                    for hh in range(2):
                        hcol = (2 * ci + hh) * 64
                        sc.dma("sp", vk, vab[b2][:, :, hh, 0:64], VAv[:, j * nkb:(j + 1) * nkb, hcol:hcol + 64],
                               writes=[vk])

                    def front_a(kb):
                        ids = []
                        pis = []
                        for hh in range(2):
                            base = hh * 64
                            pi = ps_next(2, 8)
                            pis.append(pi)
                            sc.op("pe", lambda: nc.tensor.matmul(
                                psum[pi][:], lhsT=kab[b2][base:base + 64, kb * 128:(kb + 1) * 128],
                                rhs=qaT[base:base + 64, ci, :], start=True, stop=True),
                                reads=[kk, "qaT"], writes=[f"ps{pi}"])
                        bcol = (C_HB + 4 * j + kb // 4) if kb < NHALO * 4 else C_ZERO
                        m0 = M_DSW + (19 - kb) * 128
                        for hh in range(2):
                            pi = pis[hh]
                            pb_ = ptc[0] % NPT
                            ptc[0] += 1
                            ids.append(pb_)
                            sc.op("act", lambda: nc.scalar.activation(out=pT[pb_][:], in_=psum[pi][:], func=AF.Exp,
                                                                      scale=0.125, bias=ccol(bcol)),
                                  reads=[f"ps{pi}", "cst"], writes=[f"pT{pb_}"])
                            eng = "pool" if hh == 0 else "dve"
                            E_ = nc.gpsimd if hh == 0 else nc.vector
                            sc.op(eng, lambda: E_.tensor_tensor(out=pT[pb_][:], in0=pT[pb_][:],
                                                                in1=cmat[:, m0:m0 + 512], op=ALU.mult),
                                  reads=[f"pT{pb_}", "cmat"], writes=[f"pT{pb_}"])
                        return (kb, ids)

                    def back_a(item):
                        kb, ids = item
                        for hh in range(2):
                            pb_ = ids[hh]
                            sc.op("pe", lambda: nc.tensor.matmul(
                                psum[hh][:], lhsT=vab[b2][:, kb, hh, :], rhs=pT[pb_][:],
                                start=(kb == 0), stop=(kb == nkb - 1)),
                                reads=[vk, f"pT{pb_}"], writes=[f"ps{hh}"])

                    attn_pipeline(list(range(nkb)), front_a, back_a)
                    for hh in range(2):
                        h = 2 * ci + hh
                        sc.op("act", lambda: nc.scalar.activation(out=Rt[hh][64:128, :], in_=psum[hh][64:128, :],
                                                                  func=AF.Ln),
                              reads=[f"ps{hh}"], writes=[f"Rt{hh}"])
                        sc.op("act", lambda: nc.scalar.activation(out=Rt[hh][64:128, :], in_=Rt[hh][64:128, :],
                                                                  func=AF.Exp, scale=-1.0),
                              reads=[f"Rt{hh}"], writes=[f"Rt{hh}"])
                        pi = ps_next(2, 8)
                        sc.op("pe", lambda: nc.tensor.matmul(psum[pi][0:64, :], lhsT=cst[:, C_SHF:C_SHF + 64],
                                                             rhs=Rt[hh][:], start=True, stop=True),
                              reads=[f"Rt{hh}", "cst"], writes=[f"ps{pi}"])
                        sc.op("dve", lambda: nc.vector.tensor_tensor(out=tf[hh][0:64, :], in0=psum[pi][0:64, :],
                                                                     in1=gaT[:, h, :], op=ALU.mult),
                              reads=[f"ps{pi}", "gaT"], writes=[f"tf{hh}"])
                        sc.op("dve", lambda: nc.vector.tensor_tensor(out=yaT[:, h, :], in0=psum[hh][0:64, :],
                                                                     in1=tf[hh][0:64, :], op=ALU.mult),
                              reads=[f"ps{hh}", f"tf{hh}"], writes=["yaT"])

                npos = 8 * j + 8
                for h in range(4):
                    aset = (h % 2) * 2

                    def front_d(st):
                        pos, kb = st
                        kbi = (h * npos + pos) % NKB
                        kk, vk = f"kbuf{kbi}", f"vbuf{kbi}"
                        if kb == 0:
                            sc.dma("sp", kk, kbuf[kbi][:], KD[h, :, pos * TT:(pos + 1) * TT], writes=[kk])
                            sc.dma("sp", vk, vbuf[kbi][:], VDv[:, pos * 4:(pos + 1) * 4, h * 128:(h + 1) * 128],
                                   writes=[vk])
                        diag = (pos == own)
                        bcol = (C_CB + (pos - 8 * j)) if pos >= 8 * j else C_ZERO
                        q0 = kb * 128 if diag else 0
                        pis = []
                        ids = []
                        for comp in range(2):
                            base = comp * 64
                            pi = ps_next(2, 8)
                            pis.append(pi)
                            sc.op("pe", lambda: nc.tensor.matmul(
                                psum[pi][:, q0:TT], lhsT=kbuf[kbi][base:base + 64, kb * 128:(kb + 1) * 128],
                                rhs=qdT[base:base + 64, h, q0:TT], start=True, stop=True),
                                reads=[kk, "qdT"], writes=[f"ps{pi}"])
                        for comp in range(2):
                            pi = pis[comp]
                            pb_ = ptc[0] % NPT
                            ptc[0] += 1
                            ids.append(pb_)
                            sc.op("act", lambda: nc.scalar.activation(
                                out=pT[pb_][:, q0:TT], in_=psum[pi][:, q0:TT], func=AF.Exp, scale=0.125,
                                bias=ccol(bcol)), reads=[f"ps{pi}", "cst"], writes=[f"pT{pb_}"])
                            if diag:
                                sc.op("pool", lambda: nc.gpsimd.tensor_tensor(
                                    out=pT[pb_][:, q0:q0 + 128], in0=pT[pb_][:, q0:q0 + 128], in1=tri,
                                    op=ALU.mult), reads=[f"pT{pb_}", "cmat"], writes=[f"pT{pb_}"])
                            a_ = aset + comp
                            eng = "dve" if comp == 0 else "pool"
                            E_ = nc.vector if comp == 0 else nc.gpsimd
                            if pos == 0 and kb == 0:
                                sc.op(eng, lambda: E_.tensor_copy(out=acc[a_][:], in_=pT[pb_][:]),
                                      reads=[f"pT{pb_}"], writes=[f"acc{a_}"])
                            else:
                                sc.op(eng, lambda: E_.tensor_tensor(out=acc[a_][:, q0:TT], in0=acc[a_][:, q0:TT],
                                                                    in1=pT[pb_][:, q0:TT], op=ALU.add),
                                      reads=[f"pT{pb_}", f"acc{a_}"], writes=[f"acc{a_}"])
                        return (pos, kb, q0, kbi, ids)

                    def back_d(item):
                        pos, kb, q0, kbi, ids = item
                        vk = f"vbuf{kbi}"
                        st = (pos == 0 and kb == 0)
                        last = (pos == npos - 1 and kb == 3)
                        for comp in range(2):
                            pb_ = ids[comp]
                            sc.op("pe", lambda: nc.tensor.matmul(
                                psum[comp][:, q0:TT], lhsT=vbuf[kbi][:, kb, :], rhs=pT[pb_][:, q0:TT],
                                start=st, stop=last), reads=[vk, f"pT{pb_}"], writes=[f"ps{comp}"])

                    attn_pipeline([(pos, kb) for pos in range(npos) for kb in range(4)], front_d, back_d)
                    recip_from_acc([aset, aset + 1], 128, [0, 1])
                    for comp in range(2):
                        sc.op("dve", lambda comp=comp: nc.vector.tensor_tensor(
                            out=tf[2 + comp][:], in0=psum[comp][:], in1=tf[comp][:], op=ALU.mult),
                            reads=[f"ps{comp}", f"tf{comp}"], writes=[f"tf{2 + comp}"])
                    sc.op("dve", lambda: nc.vector.scalar_tensor_tensor(
                        out=tf[4][:], in0=tf[3][:], scalar=lamc[:, 0:1], in1=tf[2][:], op0=ALU.mult, op1=ALU.add),
                        reads=["tf3", "tf2", "lamc"], writes=["tf4"])
                    sc.op("act", lambda: nc.scalar.activation(out=sqo[:], in_=tf[4][:], func=AF.Square),
                          reads=["tf4"], writes=["sqo"])
                    pi = ps_next(2, 8)
                    sc.op("pe", lambda: nc.tensor.matmul(psum[pi][:], lhsT=ones, rhs=sqo[:], start=True, stop=True),
                          reads=["sqo", "cmat"], writes=[f"ps{pi}"])
                    sc.op("act", lambda: nc.scalar.activation(out=tf[5][:], in_=psum[pi][:], func=AF.Ln,
                                                              scale=1.0 / 128, bias=ccol(C_EPSR)),
                          reads=[f"ps{pi}", "cst"], writes=["tf5"])
                    sc.op("act", lambda: nc.scalar.activation(out=tf[5][:], in_=tf[5][:], func=AF.Exp, scale=-0.5),
                          reads=["tf5"], writes=["tf5"])
                    sc.op("pool", lambda: nc.gpsimd.tensor_tensor(out=tf[5][:], in0=tf[5][:], in1=gdT[:, h, :],
                                                                  op=ALU.mult),
                          reads=["tf5", "gdT"], writes=["tf5"])
                    sc.op("dve", lambda: nc.vector.scalar_tensor_tensor(
                        out=ydT[:, h, :], in0=tf[4][:], scalar=lamc[:, 1:2], in1=tf[5][:],
                        op0=ALU.mult, op1=ALU.mult), reads=["tf4", "tf5", "lamc"], writes=["ydT"])

                for dc in range(8):
                    pi = ps_next(2, 8)
                    for h in range(8):
                        sc.op("pe", lambda h=h: nc.tensor.matmul(
                            psum[pi][:], lhsT=woa[:, h, dc * 128:(dc + 1) * 128], rhs=yaT[:, h, :],
                            start=(h == 0), stop=False), reads=["woa", "yaT"], writes=[f"ps{pi}"])
                    for h in range(4):
                        sc.op("pe", lambda h=h: nc.tensor.matmul(
                            psum[pi][:], lhsT=wod[:, h, dc * 128:(dc + 1) * 128], rhs=ydT[:, h, :],
                            start=False, stop=(h == 3)), reads=["wod", "ydT"], writes=[f"ps{pi}"])
                    sc.op("dve", lambda: nc.vector.tensor_tensor(out=x1[:, dc, :], in0=psum[pi][:], in1=x1[:, dc, :],
                                                                 op=ALU.add),
                          reads=[f"ps{pi}", "x1"], writes=["x1"])
                sc.barrier()
            if stop == "p2":
                return nc

            with ExitStack() as p3:
                sq = sb("sq1", [128, 8, TT], BF16, p3)
                h1 = sb("h1", [128, 8, TT], BF16, p3)
                rs = sb("rs1", [128, TT], F32, p3)
                ssm1 = sb("ssm1", [128, TT], BF16, p3)
                wst = [sb(f"w1st{i}", [128, 4096], BF16, p3) for i in range(2)]
                uT = sb("uT", [128, 16, TT], BF16, p3)
                gsT = sb("gsT", [128, 16, TT], BF16, p3)
                gv = sb("gv", [128, 4, 2048], BF16, p3)
                vtmp = sb("vtmp", [128, 2048], F32, p3)
                lng = sb("lng", [128, 2, 2048], F32, p3)
                wsf = sb("wsf", [128, 8, 128], BF16, p3)
                bsb = sb("bsb_sb", [128, 8, 128], F32, p3)
                stats = sb("stats", [128, 4, 6], F32, p3)
                mv = sb("mv", [128, 4], F32, p3)
                tf = [sb(f"tg{i}", [128, TT], F32, p3) for i in range(3)]
                wsb = [0]

                def load_w1(src_ap, shape3):
                    b = wsb[0] % 2
                    wsb[0] += 1
                    a, bb = shape3
                    view = wst[b][:, 0:a * bb].rearrange("p (a b) -> p a b", b=bb)
                    sc.dma("sp", f"w1st{b}", view, src_ap, writes=[f"w1st{b}"])
                    return view, f"w1st{b}"

                sc.dma("sp", "lng", lng[:, 0, :], lngb_d[0:1, :].partition_broadcast(128), writes=["lng"])
                sc.dma("sp", "lng", lng[:, 1, :], lngb_d[1:2, :].partition_broadcast(128), writes=["lng"])
                sc.dma("pool", "wsf", wsf[:].rearrange("p g t -> p (g t)"), wsT_d, writes=["wsf"])
                sc.dma("sp", "bsb", bsb[:].rearrange("p g t -> p (g t)"), bsb_d.partition_broadcast(128),
                       writes=["bsb"])
                for g in range(8):
                    sc.op("pool", lambda g=g: nc.gpsimd.tensor_tensor(out=wsf[:, g, :], in0=wsf[:, g, :], in1=tri,
                                                                      op=ALU.mult),
                          reads=["wsf", "cmat"], writes=["wsf"])

                rmsnorm_tile(x1, "x1", C_SGUN, sq, "sq1", h1, "h1", rs, "rs1", ssm1, "ssm1")
                for wc in range(4):
                    w, wk = load_w1(W1v[:, :, 2048 + wc * 512:2048 + (wc + 1) * 512], (8, 512))
                    for sub in range(4):
                        pi = ps_next()
                        for k in range(8):
                            sc.op("pe", lambda k=k: nc.tensor.matmul(
                                psum[pi][:], lhsT=h1[:, k, sub * 128:(sub + 1) * 128], rhs=w[:, k, :],
                                start=(k == 0), stop=(k == 7)), reads=["h1", wk], writes=[f"ps{pi}"])
                        sc.op("act", lambda: nc.scalar.activation(out=gv[:, sub, wc * 512:(wc + 1) * 512],
                                                                  in_=psum[pi][:], func=AF.Gelu),
                              reads=[f"ps{pi}"], writes=[f"gv{sub}"])
                for sub in range(4):
                    for q in range(4):
                        sc.op("dve", lambda q=q: nc.vector.bn_stats(out=stats[:, q, :],
                                                                    in_=gv[:, sub, q * 512:(q + 1) * 512]),
                              reads=[f"gv{sub}"], writes=["stats"])
                    sc.op("dve", lambda: nc.vector.bn_aggr(out=mv[:, 0:2], in_=stats[:].rearrange("p a b -> p (a b)")),
                          reads=["stats"], writes=["mv"])
                    sc.op("act", lambda: nc.scalar.activation(out=mv[:, 2:3], in_=mv[:, 1:2], func=AF.Ln,
                                                              bias=ccol(C_EPSL)),
                          reads=["mv", "cst"], writes=["mv"])
                    sc.op("act", lambda: nc.scalar.activation(out=mv[:, 3:4], in_=mv[:, 2:3], func=AF.Exp, scale=-0.5),
                          reads=["mv"], writes=["mv"])
                    sc.op("dve", lambda: nc.vector.tensor_scalar(
                        out=vtmp[:], in0=gv[:, sub, :], scalar1=mv[:, 0:1], scalar2=mv[:, 3:4],
                        op0=ALU.subtract, op1=ALU.mult), reads=[f"gv{sub}", "mv"], writes=["vtmp"])
                    sc.op("pool", lambda: nc.gpsimd.tensor_tensor(out=vtmp[:], in0=vtmp[:], in1=lng[:, 0, :],
                                                                  op=ALU.mult),
                          reads=["vtmp", "lng"], writes=["vtmp"])
                    sc.op("pool", lambda: nc.gpsimd.tensor_tensor(out=gv[:, sub, :], in0=vtmp[:], in1=lng[:, 1, :],
                                                                  op=ALU.add),
                          reads=["vtmp", "lng"], writes=[f"gv{sub}"])
                for wc in range(4):
                    w, wk = load_w1(W1v[:, :, wc * 512:(wc + 1) * 512], (8, 512))
                    for s4 in range(4):
                        pi = ps_next()
                        proj_fm(pi, h1, "h1", w, wk, s4 * 128)
                        c = wc * 4 + s4
                        sc.op("act", lambda c=c: nc.scalar.activation(out=uT[:, c, :], in_=psum[pi][:], func=AF.Gelu),
                              reads=[f"ps{pi}"], writes=["uT"])
                for wc in range(4):
                    w, wk = load_w1(W1v[:, :, 4096 + wc * 512:4096 + (wc + 1) * 512], (8, 512))
                    for s4 in range(4):
                        pi = ps_next()
                        proj_fm(pi, h1, "h1", w, wk, s4 * 128)
                        c = wc * 4 + s4
                        sc.op("act", lambda c=c: nc.scalar.activation(out=gsT[:, c, :], in_=psum[pi][:], func=AF.Silu),
                              reads=[f"ps{pi}"], writes=["gsT"])
                for c in range(16):
                    g = c // 2
                    pi = ps_next()
                    for sub in range(4):
                        sc.op("pe", lambda sub=sub: nc.tensor.matmul(
                            psum[pi][:, sub * 128:(sub + 1) * 128], lhsT=gv[:, sub, c * 128:(c + 1) * 128],
                            rhs=wsf[:, g, :], start=True, stop=True),
                            reads=[f"gv{sub}", "wsf"], writes=[f"ps{pi}"])
                    tb = c % 3
                    for sub in range(4):
                        sc.op("dve", lambda sub=sub: nc.vector.tensor_tensor(
                            out=tf[tb][:, sub * 128:(sub + 1) * 128], in0=psum[pi][:, sub * 128:(sub + 1) * 128],
                            in1=bsb[:, g, :], op=ALU.add), reads=[f"ps{pi}", "bsb"], writes=[f"tg{tb}"])
                    sc.op("pool", lambda: nc.gpsimd.tensor_tensor(out=gsT[:, c, :], in0=gsT[:, c, :], in1=uT[:, c, :],
                                                                  op=ALU.mult),
                          reads=["gsT", "uT"], writes=["gsT"])
                    sc.op("dve", lambda: nc.vector.tensor_tensor(out=uT[:, c, :], in0=tf[tb][:], in1=gsT[:, c, :],
                                                                 op=ALU.mult),
                          reads=[f"tg{tb}", "gsT"], writes=["uT"])
                for q4 in range(4):
                    w, wk = load_w1(Wo1v[:, :, q4 * 256:(q4 + 1) * 256], (16, 256))
                    for d2 in range(2):
                        dc = q4 * 2 + d2
                        pi = ps_next()
                        for c in range(16):
                            sc.op("pe", lambda c=c: nc.tensor.matmul(
                                psum[pi][:], lhsT=w[:, c, d2 * 128:(d2 + 1) * 128], rhs=uT[:, c, :],
                                start=(c == 0), stop=(c == 15)), reads=[wk, "uT"], writes=[f"ps{pi}"])
                        tb = dc % 3
                        sc.op("dve", lambda: nc.vector.tensor_tensor(out=tf[tb][:], in0=psum[pi][:], in1=x1[:, dc, :],
                                                                     op=ALU.add),
                              reads=[f"ps{pi}", "x1"], writes=[f"tg{tb}"])
                        sc.dma("sp", f"tg{tb}", out_d[dc * 128:(dc + 1) * 128, j * TT:(j + 1) * TT], tf[tb][:],
                               reads=[f"tg{tb}"], writes=["out"])
                sc.barrier()
            if stop == "p3":
                return nc
    return nc


def _tile_order(c):
    order = []
    for j in range(NSLOT):
        order += [8 * j + i for i in range(c + 1, 8)] + [8 * j + i for i in range(0, c)] + [8 * j + c]
    return order


def _rope_tables(positions):
    rot = 16
    inv = 1.0 / (500000.0 ** (np.arange(0, rot, 2, dtype=np.float32) / rot))
    ang = positions.astype(np.float32)[:, None] * inv[None, :].astype(np.float32)
    cos, sin = np.cos(ang).astype(np.float32), np.sin(ang).astype(np.float32)
    T = positions.shape[0]
    cf = np.ones((128, T), np.float32)
    sf = np.zeros((128, T), np.float32)
    for hb in (0, 64):
        cf[hb:hb + 8] = cos.T
        cf[hb + 8:hb + 16] = cos.T
        sf[hb:hb + 8] = sin.T
        sf[hb + 8:hb + 16] = sin.T
    return np.stack([cf, sf], 0)


def _const_mats():
    cm = np.zeros((128, NCM), np.float32)
    cm[:, M_ONES:M_ONES + 128] = 1.0
    cm[0:64, M_BONES:M_BONES + 64] = 1.0
    cm[64:128, M_BONES + 64:M_BONES + 128] = 1.0
    for m in range(128):
        dl = m % 64
        if dl < 8:
            cm[m + 8, M_ROT + m] = -1.0
        elif dl < 16:
            cm[m - 8, M_ROT + m] = 1.0
    kk = np.arange(128)[:, None]
    qq = np.arange(128)[None, :]
    cm[:, M_TRI:M_TRI + 128] = (kk <= qq).astype(np.float32)
    for idx in range(23):
        delta = idx - 3
        d = 128 * delta + qq - kk
        m = ((d >= 0) & (d <= 128)).astype(np.float32)
        m += ((d % 4 == 0) & (d >= 0) & (d <= 512)).astype(np.float32)
        m += ((d % 16 == 0) & (d >= 0) & (d <= 2048)).astype(np.float32)
        cm[:, M_DSW + idx * 128:M_DSW + (idx + 1) * 128] = m
    return cm.astype(ml_dtypes.bfloat16)


_NC_CACHE = {}


def _prep(x, att_norm, att_w_in, dsw_q_norm, dsw_k_norm, diff_q_norm, diff_k_norm,
           diff_lam_q1, diff_lam_k1, diff_lam_q2, diff_lam_k2, diff_subln, att_w_out,
           sgu_norm, sgu_w_in, sgu_ln_g, sgu_ln_b, sgu_w_s, sgu_b_s, sgu_w_out):
    f = np.float32
    xT = np.ascontiguousarray(np.asarray(x, f)[0].T)
    xT_tiles = xT.reshape(D, NT, TT)
    cm = _const_mats()
    w_in0 = np.ascontiguousarray(np.asarray(att_w_in, f)[0])
    w_out0 = np.ascontiguousarray(np.asarray(att_w_out, f)[0])
    w_in1 = np.ascontiguousarray(np.asarray(sgu_w_in, f)[0])
    w_out1 = np.ascontiguousarray(np.asarray(sgu_w_out, f)[0])
    lngb = np.ascontiguousarray(np.stack([np.asarray(sgu_ln_g, f)[0], np.asarray(sgu_ln_b, f)[0]], 0))
    wsT = np.ascontiguousarray(np.asarray(sgu_w_s, f)[0].transpose(2, 0, 1).reshape(128, 8 * 128))
    bsb = np.ascontiguousarray(np.asarray(sgu_b_s, f)[0].reshape(1, 8 * 128))
    in_maps = []
    orders = []
    for c in range(NCORE):
        order = _tile_order(c)
        orders.append(order)
        x_all = np.ascontiguousarray(xT_tiles[:, order, :].reshape(D, S))
        pos_all = (np.asarray(order)[:, None] * TT + np.arange(TT)[None, :]).reshape(-1)
        halo_tiles = []
        for j in range(NSLOT):
            halo_tiles += [8 * j + c - NHALO + m for m in range(NHALO)]
        x_halo = np.zeros((D, NSLOT * NHALO, TT), f)
        pos_halo = np.zeros((NSLOT * NHALO, TT), np.int64)
        for i, t in enumerate(halo_tiles):
            if t >= 0:
                x_halo[:, i, :] = xT_tiles[:, t, :]
                pos_halo[i] = t * TT + np.arange(TT)
        cst = np.zeros((128, NCST), f)
        cst[:, C_ATTN:C_ATTN + 8] = np.asarray(att_norm, f)[0].reshape(8, 128).T
        cst[:, C_SGUN:C_SGUN + 8] = np.asarray(sgu_norm, f)[0].reshape(8, 128).T
        cst[:, C_DSWQ] = np.tile(np.asarray(dsw_q_norm, f)[0], 2)
        cst[:, C_DSWK] = np.tile(np.asarray(dsw_k_norm, f)[0], 2)
        cst[:, C_DIFQ] = np.tile(np.asarray(diff_q_norm, f)[0], 2)
        cst[:, C_DIFK] = np.tile(np.asarray(diff_k_norm, f)[0], 2)
        cst[:, C_SUBLN] = np.asarray(diff_subln, f)[0]
        for i in range(8):
            cst[:, C_CB + i] = 0.0 if (i >= 7 - c) else NEG
        for j in range(NSLOT):
            for m in range(NHALO):
                cst[:, C_HB + 4 * j + m] = 0.0 if (8 * j + c - NHALO + m) >= 0 else NEG
        cst[:, C_LQ1:C_LQ1 + 64] = np.asarray(diff_lam_q1, f)[0][None, :]
        cst[:, C_LK1:C_LK1 + 64] = np.asarray(diff_lam_k1, f)[0][None, :]
        cst[:, C_LQ2:C_LQ2 + 64] = np.asarray(diff_lam_q2, f)[0][None, :]
        cst[:, C_LK2:C_LK2 + 64] = np.asarray(diff_lam_k2, f)[0][None, :]
        cst[:, C_EPSR] = RMS_EPS
        cst[:, C_EPSL] = LN_EPS
        cst[:, C_ONE] = 1.0
        for m in range(64):
            cst[m + 64, C_SHF + m] = 1.0
        in_maps.append({
            "x_all": x_all,
            "x_halo": np.ascontiguousarray(x_halo.reshape(D, NSLOT * NHALO * TT)),
            "rope_all": _rope_tables(pos_all),
            "rope_halo": _rope_tables(pos_halo.reshape(-1)),
            "w_in0": w_in0, "w_out0": w_out0, "w_in1": w_in1, "w_out1": w_out1,
            "cst": cst, "cmat": cm, "lngb": lngb, "wsT": wsT, "bsb": bsb,
        })
    return in_maps


def kernel(**inputs):
    f = np.float32
    in_maps = _prep(**inputs)
    if "nc" not in _NC_CACHE:
        _NC_CACHE["nc"] = build_program()
    nc = _NC_CACHE["nc"]
    res = run_bass_kernel_spmd(nc, in_maps, core_ids=list(range(NCORE)))
    out = np.zeros((S, D), f)
    for c in range(NCORE):
        oT = np.asarray(res.results[c]["outT"], f)
        for j in range(NSLOT):
            t = 8 * j + c
            out[t * TT:(t + 1) * TT, :] = oT[:, j * TT:(j + 1) * TT].T
    return out[None]
```

```python
import math
from contextlib import ExitStack

import ml_dtypes
import numpy as np

import concourse.bass as bass
import concourse.mybir as mybir
from concourse.bass_utils import run_bass_kernel_spmd

F32 = mybir.dt.float32
BF16 = mybir.dt.bfloat16
AF = mybir.ActivationFunctionType
ALU = mybir.AluOpType

S = 16384
D = 1024
TT = 512
NT = S // TT
NCORE = 8
NSLOT = NT // NCORE
NHALO = 4
NEG = -30000.0
RMS_EPS = 1e-6
LN_EPS = 1e-5
LAM_INIT0 = 0.8 - 0.6 * math.exp(-0.3 * 0)

C_ATTN = 0
C_SGUN = 8
C_DSWQ = 16
C_DSWK = 17
C_DIFQ = 18
C_DIFK = 19
C_SUBLN = 20
C_CB = 21
C_HB = 29
C_ZERO = 45
C_LQ1 = 46
C_LK1 = 110
C_LQ2 = 174
C_LK2 = 238
C_EPSR = 302
C_EPSL = 303
C_ONE = 304
C_SHF = 305
NCST = 305 + 64
M_ONES = 0
M_BONES = 128
M_ROT = 256
M_TRI = 384
M_DSW = 512
NCM = 512 + 23 * 128


class Sched:
    def __init__(self, nc, es):
        self.nc = nc
        self.es = es
        self.E = {"pe": nc.tensor, "act": nc.scalar, "dve": nc.vector, "pool": nc.gpsimd, "sp": nc.sync}
        self.semh = {}
        self.cnt = {}
        for e in self.E:
            self.semh[e] = es.enter_context(nc.semaphore("s_" + e))
            self.cnt[e] = 0
        self.known = {e: {} for e in self.E}
        self.lastw = {}
        self.readers = {}
        self.nwait = 0

    def dsem(self, name):
        k = "d_" + name
        if k not in self.semh:
            self.semh[k] = self.es.enter_context(self.nc.semaphore(k))
            self.cnt[k] = 0
        return k

    def _wait(self, eng, deps):
        need = {}
        for (sk, val) in deps:
            if sk == eng and eng in ("pe", "sp"):
                continue
            if val > need.get(sk, 0):
                need[sk] = val
        for sk, val in need.items():
            if self.known[eng].get(sk, 0) >= val:
                continue
            self.E[eng].wait_ge(self.semh[sk], val)
            self.known[eng][sk] = val
            self.nwait += 1

    def _deps(self, reads, writes):
        deps = []
        for k in reads:
            if k in self.lastw:
                deps.append(self.lastw[k])
        for k in writes:
            if k in self.lastw:
                deps.append(self.lastw[k])
            deps.extend(self.readers.get(k, {}).items())
        return deps

    def _record(self, ev, reads, writes):
        for k in reads:
            r = self.readers.setdefault(k, {})
            if ev[1] > r.get(ev[0], 0):
                r[ev[0]] = ev[1]
        for k in writes:
            self.lastw[k] = ev
            self.readers[k] = {}

    def op(self, eng, fn, reads=(), writes=()):
        self._wait(eng, self._deps(reads, writes))
        ins = fn()
        self.cnt[eng] += 1
        ins.then_inc(self.semh[eng], 1)
        self._record((eng, self.cnt[eng]), reads, writes)

    def dma(self, q, sem, out, in_, reads=(), writes=()):
        sk = self.dsem(sem)
        self._wait(q, self._deps(reads, writes))
        ins = self.E[q].dma_start(out=out, in_=in_)
        self.cnt[sk] += 16
        ins.then_inc(self.semh[sk], 16)
        self._record((sk, self.cnt[sk]), reads, writes)

    def barrier(self):
        for e in self.E:
            for sk, c in self.cnt.items():
                if sk == e and e in ("pe", "sp"):
                    continue
                if c > 0 and self.known[e].get(sk, 0) < c:
                    self.E[e].wait_ge(self.semh[sk], c)
                    self.known[e][sk] = c
        self.lastw = {}
        self.readers = {}


def build_program(stop=None):
    nc = bass.Bass("TRN2", target_bir_lowering=False)
    x_all = nc.dram_tensor("x_all", [D, S], F32, kind="ExternalInput").ap()
    x_halo = nc.dram_tensor("x_halo", [D, NSLOT * NHALO * TT], F32, kind="ExternalInput").ap()
    rope_all = nc.dram_tensor("rope_all", [2, 128, S], F32, kind="ExternalInput").ap()
    rope_halo = nc.dram_tensor("rope_halo", [2, 128, NSLOT * NHALO * TT], F32, kind="ExternalInput").ap()
    w_in0 = nc.dram_tensor("w_in0", [D, 4096], F32, kind="ExternalInput").ap()
    w_out0 = nc.dram_tensor("w_out0", [D, D], F32, kind="ExternalInput").ap()
    w_in1 = nc.dram_tensor("w_in1", [D, 6144], F32, kind="ExternalInput").ap()
    w_out1 = nc.dram_tensor("w_out1", [2048, D], F32, kind="ExternalInput").ap()
    cst_d = nc.dram_tensor("cst", [128, NCST], F32, kind="ExternalInput").ap()
    cmat_d = nc.dram_tensor("cmat", [128, NCM], BF16, kind="ExternalInput").ap()
    lngb_d = nc.dram_tensor("lngb", [2, 2048], F32, kind="ExternalInput").ap()
    wsT_d = nc.dram_tensor("wsT", [128, 8 * 128], F32, kind="ExternalInput").ap()
    bsb_d = nc.dram_tensor("bsb", [1, 8 * 128], F32, kind="ExternalInput").ap()
    out_d = nc.dram_tensor("outT", [D, NSLOT * TT], F32, kind="ExternalOutput").ap()
    W0b = nc.dram_tensor("W0b", [D, 4096], BF16).ap()
    Wo0b = nc.dram_tensor("Wo0b", [D, D], BF16).ap()
    W1b = nc.dram_tensor("W1b", [D, 6144], BF16).ap()
    Wo1b = nc.dram_tensor("Wo1b", [2048, D], BF16).ap()
    KD = nc.dram_tensor("KD", [4, 128, S], BF16).ap()
    VD = nc.dram_tensor("VD", [S, 512], BF16).ap()
    NKA = NSLOT * (NHALO + 1) * TT
    KA = nc.dram_tensor("KA", [4, 128, NKA], BF16).ap()
    VA = nc.dram_tensor("VA", [NKA, 512], BF16).ap()
    QA = nc.dram_tensor("QA", [4, 128, NSLOT * TT], BF16).ap()
    QD = nc.dram_tensor("QD", [4, 128, NSLOT * TT], BF16).ap()
    GA = nc.dram_tensor("GA", [8, 64, NSLOT * TT], BF16).ap()
    GD = nc.dram_tensor("GD", [4, 128, NSLOT * TT], BF16).ap()

    with ExitStack() as es:
        es.enter_context(nc.allow_low_precision("bf16 matmul operands, fp32 accumulation"))
        es.enter_context(nc.allow_non_contiguous_dma("strided weight / scratch tiles"))
        sc = Sched(nc, es)

        uid = [0]

        def sb(name, shape, dt, stack=es):
            uid[0] += 1
            return stack.enter_context(nc.sbuf_tensor(f"{name}_{uid[0]}", shape, dt))

        cst = sb("cst_sb", [128, NCST], F32)
        cmat = sb("cmat_sb", [128, NCM], BF16)
        lamc = sb("lamc", [128, 8], F32)
        x1 = sb("x1", [128, 8, TT], F32)
        onesf = sb("onesf", [128, 128], F32)
        psum = [es.enter_context(nc.psum_tensor(f"ps{i}", [128, TT], F32)) for i in range(8)]
        ps_rr = [0]

        def ps_next(lo=0, hi=8):
            i = lo + (ps_rr[0] % (hi - lo))
            ps_rr[0] += 1
            return i

        ones = cmat[:, M_ONES:M_ONES + 128]
        bones = cmat[:, M_BONES:M_BONES + 128]
        rotm = cmat[:, M_ROT:M_ROT + 128]
        tri = cmat[:, M_TRI:M_TRI + 128]

        def ccol(c, n=128):
            return cst[0:n, c:c + 1]

        sc.dma("sp", "cst", cst[:], cst_d, writes=["cst"])
        sc.dma("sp", "cmat", cmat[:], cmat_d, writes=["cmat"])
        for (src, dst, rows, nm) in ((w_in0, W0b, D, "W0b"), (w_out0, Wo0b, D, "Wo0b"),
                                     (w_in1, W1b, D, "W1b"), (w_out1, Wo1b, 2048, "Wo1b")):
            step = 128
            for r0 in range(0, rows, step):
                sc.dma("pool", "wcast", dst[r0:r0 + step, :], src[r0:r0 + step, :], writes=[nm])
                done = sc.cnt["d_wcast"] - 32
                if done > 0:
                    nc.gpsimd.wait_ge(sc.semh["d_wcast"], done)
                    sc.known["pool"]["d_wcast"] = done
        sc.op("dve", lambda: nc.vector.memset(onesf[:], 1.0), writes=["onesf"])
        with ExitStack() as ls:
            lt = sb("lam_t", [128, 128], F32, ls)
            for i, (ca, cb) in enumerate(((C_LQ1, C_LK1), (C_LQ2, C_LK2))):
                sc.op("dve", lambda ca=ca, cb=cb, i=i: nc.vector.tensor_tensor(
                    out=lt[:, i * 64:(i + 1) * 64], in0=cst[:, ca:ca + 64], in1=cst[:, cb:cb + 64], op=ALU.mult),
                    reads=["cst"], writes=["lt"])
            for i in range(2):
                sc.op("dve", lambda i=i: nc.vector.reduce_sum(
                    out=lamc[:, 2 + i:3 + i], in_=lt[:, i * 64:(i + 1) * 64], axis=mybir.AxisListType.X),
                    reads=["lt"], writes=["lamc"])
            sc.op("act", lambda: nc.scalar.activation(out=lamc[:, 4:6], in_=lamc[:, 2:4], func=AF.Exp),
                  reads=["lamc"], writes=["lamc"])
            sc.op("dve", lambda: nc.vector.scalar_tensor_tensor(
                out=lamc[:, 0:1], in0=lamc[:, 5:6], scalar=-LAM_INIT0, in1=lamc[:, 4:5],
                op0=ALU.add, op1=ALU.subtract), reads=["lamc"], writes=["lamc"])
            sc.op("dve", lambda: nc.vector.tensor_scalar(
                out=lamc[:, 1:2], in0=cst[:, C_SUBLN:C_SUBLN + 1], scalar1=1.0 - LAM_INIT0, scalar2=None,
                op0=ALU.mult), reads=["cst", "lamc"], writes=["lamc"])
            sc.barrier()

        def rmsnorm_tile(xin, xk, gcol0, sq, sqk, hT, hk, rs, rsk, ssum, ssk, pieces=None):
            sc.op("act", lambda: nc.scalar.activation(out=sq[:], in_=xin[:], func=AF.Square),
                  reads=[xk], writes=[sqk])
            sc.op("dve", lambda: nc.vector.reduce_sum(out=ssum[:], in_=sq[:].rearrange("p k t -> p t k"),
                                                      axis=mybir.AxisListType.X),
                  reads=[sqk], writes=[ssk])
            pi = ps_next()
            sc.op("pe", lambda: nc.tensor.matmul(psum[pi][:], lhsT=ones, rhs=ssum[:], start=True, stop=True),
                  reads=[ssk, "cmat"], writes=[f"ps{pi}"])
            sc.op("act", lambda: nc.scalar.activation(out=rs[:], in_=psum[pi][:], func=AF.Ln,
                                                      scale=1.0 / D, bias=ccol(C_EPSR)),
                  reads=[f"ps{pi}", "cst"], writes=[rsk])
            sc.op("act", lambda: nc.scalar.activation(out=rs[:], in_=rs[:], func=AF.Exp, scale=-0.5),
                  reads=[rsk], writes=[rsk])
            def piece(k):
                sc.op("dve", lambda: nc.vector.scalar_tensor_tensor(
                    out=hT[:, k, :], in0=xin[:, k, :], scalar=cst[:, gcol0 + k:gcol0 + k + 1], in1=rs[:],
                    op0=ALU.mult, op1=ALU.mult), reads=[xk, rsk, "cst"], writes=[hk])
            if pieces is None:
                for k in range(8):
                    piece(k)
            else:
                for k in range(8):
                    pieces.append(lambda k=k: piece(k))

        def proj_fm(pi, hT, hk, w, wk, c0, m=128, prow=128):
            for k in range(8):
                sc.op("pe", lambda k=k: nc.tensor.matmul(psum[pi][0:m, :], lhsT=w[:, k, c0:c0 + m], rhs=hT[:, k, :],
                                                       start=(k == 0), stop=(k == 7)),
                      reads=[hk, wk], writes=[f"ps{pi}"])

        if stop == "setup":
            return nc
        with ExitStack() as p1:
            xin = [sb(f"xin{i}", [128, 8, TT], F32, p1) for i in range(2)]
            sq = [sb(f"sq{i}", [128, 8, TT], BF16, p1) for i in range(2)]
            hTs = [sb(f"hT{i}", [128, 8, TT], BF16, p1) for i in range(2)]
            rs = [sb(f"rs{i}", [128, TT], F32, p1) for i in range(2)]
            ssm = [sb(f"ssm{i}", [128, TT], BF16, p1) for i in range(2)]
            wkvd = sb("wkvd", [128, 8, 1024], BF16, p1)
            wkva = sb("wkva", [128, 8, 1024], BF16, p1)
            wst = [sb(f"wst{i}", [128, 8, 512], BF16, p1) for i in range(2)]
            rope = [sb(f"rope{i}", [128, 2, TT], F32, p1) for i in range(2)]
            NPB = 3
            sqc = [sb(f"sqc{i}", [128, TT], BF16, p1) for i in range(NPB)]
            qa_ = [sb(f"qA{i}", [128, TT], BF16, p1) for i in range(NPB)]
            rsc = [sb(f"rsc{i}", [128, TT], F32, p1) for i in range(NPB)]
            t1 = [sb(f"t1{i}", [128, TT], F32, p1) for i in range(NPB)]
            t2 = [sb(f"t2{i}", [128, TT], F32, p1) for i in range(NPB)]
            ko = [sb(f"ko{i}", [128, TT], BF16, p1) for i in range(NPB)]
            vst = [sb(f"vst{i}", [128, TT], BF16, p1) for i in range(2)]
            pb = [0]
            vb = [0]
            wsb = [0]

            W0v = W0b.rearrange("(k p) e -> p k e", p=128)
            sc.dma("sp", "wkvd", wkvd[:, :, 0:512], W0v[:, :, 2560:3072], reads=["W0b"], writes=["wkvd"])
            sc.dma("sp", "wkvd", wkvd[:, :, 512:1024], W0v[:, :, 3072:3584], reads=["W0b"], writes=["wkvd"])
            sc.dma("sp", "wkva", wkva[:, :, 0:512], W0v[:, :, 512:1024], reads=["W0b"], writes=["wkva"])
            sc.dma("sp", "wkva", wkva[:, :, 512:1024], W0v[:, :, 1024:1536], reads=["W0b"], writes=["wkva"])

            deferred = []

            def flush_deferred():
                while deferred:
                    deferred.pop(0)()

            def qk_post(pi, gcol, ropet, ropek, dst_ap, dkey):
                b = pb[0] % NPB
                pb[0] += 1
                sc.op("act", lambda: nc.scalar.activation(out=sqc[b][:], in_=psum[pi][:], func=AF.Square),
                      reads=[f"ps{pi}"], writes=[f"sqc{b}"])
                sc.op("act", lambda: nc.scalar.activation(out=qa_[b][:], in_=psum[pi][:], func=AF.Identity,
                                                          scale=ccol(gcol)),
                      reads=[f"ps{pi}", "cst"], writes=[f"qA{b}"])
                flush_deferred()
                front_piece()
                deferred.append(lambda: qk_post_b(b, ropet, ropek, dst_ap, dkey))

            def qk_post_b(b, ropet, ropek, dst_ap, dkey):
                p2 = ps_next()
                sc.op("pe", lambda: nc.tensor.matmul(psum[p2][:], lhsT=bones, rhs=sqc[b][:], start=True, stop=True),
                      reads=[f"sqc{b}", "cmat"], writes=[f"ps{p2}"])
                p3 = ps_next()
                sc.op("pe", lambda: nc.tensor.matmul(psum[p3][:], lhsT=rotm, rhs=qa_[b][:], start=True, stop=True),
                      reads=[f"qA{b}", "cmat"], writes=[f"ps{p3}"])
                sc.op("act", lambda: nc.scalar.activation(out=rsc[b][:], in_=psum[p2][:], func=AF.Ln,
                                                          scale=1.0 / 64, bias=ccol(C_EPSR)),
                      reads=[f"ps{p2}", "cst"], writes=[f"rsc{b}"])
                sc.op("act", lambda: nc.scalar.activation(out=rsc[b][:], in_=rsc[b][:], func=AF.Exp, scale=-0.5),
                      reads=[f"rsc{b}"], writes=[f"rsc{b}"])
                sc.op("pool", lambda: nc.gpsimd.tensor_tensor(out=t1[b][:], in0=qa_[b][:], in1=ropet[:, 0, :],
                                                              op=ALU.mult),
                      reads=[f"qA{b}", ropek], writes=[f"t1{b}"])
                sc.op("dve", lambda: nc.vector.tensor_tensor(out=t2[b][:], in0=psum[p3][:], in1=ropet[:, 1, :],
                                                             op=ALU.mult),
                      reads=[f"ps{p3}", ropek], writes=[f"t2{b}"])
                sc.op("pool", lambda: nc.gpsimd.tensor_tensor(out=t1[b][:], in0=t1[b][:], in1=t2[b][:], op=ALU.add),
                      reads=[f"t1{b}", f"t2{b}"], writes=[f"t1{b}"])
                sc.op("dve", lambda: nc.vector.tensor_tensor(out=ko[b][:], in0=t1[b][:], in1=rsc[b][:], op=ALU.mult),
                      reads=[f"t1{b}", f"rsc{b}"], writes=[f"ko{b}"])
                sc.dma("sp", f"ko{b}", dst_ap, ko[b][:], reads=[f"ko{b}"], writes=[dkey])

            def v_tm(hT, hk, w, wk, c0, dst_rows_fn, dkey):
                for sub in range(4):
                    pi = ps_next()
                    for k in range(8):
                        sc.op("pe", lambda k=k: nc.tensor.matmul(
                            psum[pi][:], lhsT=hT[:, k, sub * 128:(sub + 1) * 128], rhs=w[:, k, c0:c0 + 512],
                            start=(k == 0), stop=(k == 7)), reads=[hk, wk], writes=[f"ps{pi}"])
                    flush_deferred()
                    front_piece()
                    b = vb[0] % 2
                    vb[0] += 1
                    sc.op("act", lambda: nc.scalar.activation(out=vst[b][:], in_=psum[pi][:], func=AF.Copy),
                          reads=[f"ps{pi}"], writes=[f"vst{b}"])
                    sc.dma("sp", f"vst{b}", dst_rows_fn(sub), vst[b][:], reads=[f"vst{b}"], writes=[dkey])

            def silu_fm(pi, m, dst_ap, dkey):
                flush_deferred()
                b = pb[0] % NPB
                pb[0] += 1
                sc.op("act", lambda: nc.scalar.activation(out=ko[b][0:m, :], in_=psum[pi][0:m, :], func=AF.Silu),
                      reads=[f"ps{pi}"], writes=[f"ko{b}"])
                sc.dma("sp", f"ko{b}", dst_ap, ko[b][0:m, :], reads=[f"ko{b}"], writes=[dkey])

            def load_wst(c0):
                b = wsb[0] % 2
                wsb[0] += 1
                sc.dma("sp", f"wst{b}", wst[b][:], W0v[:, :, c0:c0 + 512], reads=["W0b"], writes=[f"wst{b}"])
                return wst[b], f"wst{b}"

            tiles = [("all", pos) for pos in range(NT)] + [("halo", i) for i in range(NSLOT * NHALO)]
            if stop == "p1a":
                tiles = tiles[:8]
            xav = x_all.rearrange("(k p) t -> p k t", p=128)
            xhv = x_halo.rearrange("(k p) t -> p k t", p=128)

            def load_x(ti):
                kind, idx = tiles[ti]
                xb = ti % 2
                srcx = xav if kind == "all" else xhv
                sc.dma("sp", f"xin{xb}", xin[xb][:], srcx[:, :, idx * TT:(idx + 1) * TT], writes=[f"xin{xb}"])

            def load_rope(ti):
                kind, idx = tiles[ti]
                xb = ti % 2
                srcr = rope_all if kind == "all" else rope_halo
                sc.dma("sp", f"rope{xb}", rope[xb][:],
                       srcr.rearrange("c p t -> p c t")[:, :, idx * TT:(idx + 1) * TT], writes=[f"rope{xb}"])

            fpieces = []

            def front_piece():
                if fpieces:
                    fpieces.pop(0)()

            def front(ti, spread=True):
                xb = ti % 2
                while fpieces:
                    fpieces.pop(0)()
                rmsnorm_tile(xin[xb], f"xin{xb}", C_ATTN, sq[xb], f"sq{xb}", hTs[xb], f"hT{xb}", rs[xb], f"rs{xb}",
                             ssm[xb], f"ssm{xb}", pieces=(fpieces if spread else None))

            load_x(0)
            load_rope(0)
            front(0, spread=False)
            if len(tiles) > 1:
                load_x(1)
            for ti, (kind, idx) in enumerate(tiles):
                xb = ti % 2
                xk = f"xin{xb}"
                rk = f"rope{xb}"
                hT, hk = hTs[xb], f"hT{xb}"
                if ti + 1 < len(tiles):
                    load_rope(ti + 1)
                    front(ti + 1)
                if ti + 2 < len(tiles):
                    load_x(ti + 2)
                if kind == "all":
                    pos = idx
                    for ch in range(4):
                        pi = ps_next()
                        proj_fm(pi, hT, hk, wkvd, "wkvd", ch * 128)
                        qk_post(pi, C_DIFK, rope[xb], rk, KD[ch, :, pos * TT:(pos + 1) * TT], "KD")
                    v_tm(hT, hk, wkvd, "wkvd", 512, lambda sub, pos=pos: VD[(pos * 4 + sub) * 128:(pos * 4 + sub + 1) * 128, :],
                         "VD")
                    if pos % 8 == 7:
                        j = pos // 8
                        kpos = j * (NHALO + 1) + NHALO
                        for ch in range(4):
                            pi = ps_next()
                            proj_fm(pi, hT, hk, wkva, "wkva", ch * 128)
                            qk_post(pi, C_DSWK, rope[xb], rk, KA[ch, :, kpos * TT:(kpos + 1) * TT], "KA")
                        v_tm(hT, hk, wkva, "wkva", 512,
                             lambda sub, kpos=kpos: VA[(kpos * 4 + sub) * 128:(kpos * 4 + sub + 1) * 128, :], "VA")
                        w, wk = load_wst(0)
                        for ch in range(4):
                            pi = ps_next()
                            proj_fm(pi, hT, hk, w, wk, ch * 128)
                            qk_post(pi, C_DSWQ, rope[xb], rk, QA[ch, :, j * TT:(j + 1) * TT], "QA")
                        w, wk = load_wst(1536)
                        for h in range(8):
                            pi = ps_next()
                            proj_fm(pi, hT, hk, w, wk, h * 64, m=64)
                            silu_fm(pi, 64, GA[h, :, j * TT:(j + 1) * TT], "GA")
                        w, wk = load_wst(2048)
                        for ch in range(4):
                            pi = ps_next()
                            proj_fm(pi, hT, hk, w, wk, ch * 128)
                            qk_post(pi, C_DIFQ, rope[xb], rk, QD[ch, :, j * TT:(j + 1) * TT], "QD")
                        w, wk = load_wst(3584)
                        for ch in range(4):
                            pi = ps_next()
                            proj_fm(pi, hT, hk, w, wk, ch * 128)
                            silu_fm(pi, 128, GD[ch, :, j * TT:(j + 1) * TT], "GD")
                else:
                    j, m = idx // NHALO, idx % NHALO
                    kpos = j * (NHALO + 1) + m
                    for ch in range(4):
                        pi = ps_next()
                        proj_fm(pi, hT, hk, wkva, "wkva", ch * 128)
                        qk_post(pi, C_DSWK, rope[xb], rk, KA[ch, :, kpos * TT:(kpos + 1) * TT], "KA")
                    v_tm(hT, hk, wkva, "wkva", 512,
                         lambda sub, kpos=kpos: VA[(kpos * 4 + sub) * 128:(kpos * 4 + sub + 1) * 128, :], "VA")
                flush_deferred()
                while fpieces:
                    fpieces.pop(0)()
            sc.barrier()

        if stop in ("p1", "p1a"):
            return nc
        Wo0v_a = Wo0b[0:512, :].rearrange("(h d) e -> d h e", d=64)
        Wo0v_d = Wo0b[512:1024, :].rearrange("(h d) e -> d h e", d=128)
        VDv = VD.rearrange("(n p) e -> p n e", p=128)
        VAv = VA.rearrange("(n p) e -> p n e", p=128)
        W1v = W1b.rearrange("(k p) e -> p k e", p=128)
        Wo1v = Wo1b.rearrange("(c p) e -> p c e", p=128)
        xav = x_all.rearrange("(k p) t -> p k t", p=128)
        for j in range(NSLOT):
            own = 8 * j + 7
            with ExitStack() as p2:
                qaT = sb("qaT", [128, 4, TT], BF16, p2)
                qdT = sb("qdT", [128, 4, TT], BF16, p2)
                gaT = sb("gaT", [64, 8, TT], BF16, p2)
                gdT = sb("gdT", [128, 4, TT], BF16, p2)
                yaT = sb("yaT", [64, 8, TT], BF16, p2)
                ydT = sb("ydT", [128, 4, TT], BF16, p2)
                woa = sb("woa", [64, 8, D], BF16, p2)
                wod = sb("wod", [128, 4, D], BF16, p2)
                NKB = 3
                kbuf = [sb(f"kbuf{i}", [128, TT], BF16, p2) for i in range(NKB)]
                vbuf = [sb(f"vbuf{i}", [128, 4, 128], BF16, p2) for i in range(NKB)]
                kab = [sb(f"kab{i}", [128, (NHALO + 1) * TT], BF16, p2) for i in range(2)]
                vab = [sb(f"vab{i}", [128, (NHALO + 1) * 4, 2, 128], BF16, p2) for i in range(2)]
                Rt = [sb(f"Rt{i}", [128, TT], F32, p2) for i in range(2)]
                for i in range(2):
                    sc.op("dve", lambda i=i: nc.vector.memset(Rt[i][:], 0.0), writes=[f"Rt{i}"])
                    sc.op("pool", lambda i=i: nc.gpsimd.memset(vab[i][:, :, :, 64:128], 1.0), writes=[f"vab{i}"])
                LAG = 2
                NPT = 2 * (LAG + 2)
                pT = [sb(f"pT{i}", [128, TT], BF16, p2) for i in range(NPT)]
                acc = [sb(f"acc{i}", [128, TT], F32, p2) for i in range(4)]
                tf = [sb(f"tf{i}", [128, TT], F32, p2) for i in range(8)]
                sqo = sb("sqo", [128, TT], BF16, p2)
                ptc = [0]

                sc.dma("sp", "qaT", qaT[:], QA.rearrange("c p t -> p c t")[:, :, j * TT:(j + 1) * TT], writes=["qaT"])
                sc.dma("sp", "qdT", qdT[:], QD.rearrange("c p t -> p c t")[:, :, j * TT:(j + 1) * TT], writes=["qdT"])
                sc.dma("sp", "gaT", gaT[:], GA.rearrange("c p t -> p c t")[:, :, j * TT:(j + 1) * TT], writes=["gaT"])
                sc.dma("sp", "gdT", gdT[:], GD.rearrange("c p t -> p c t")[:, :, j * TT:(j + 1) * TT], writes=["gdT"])
                sc.dma("sp", "woa", woa[:], Wo0v_a, writes=["woa"])
                sc.dma("sp", "wod", wod[:], Wo0v_d, writes=["wod"])
                sc.dma("sp", "x1", x1[:], xav[:, :, own * TT:(own + 1) * TT], writes=["x1"])

                def attn_pipeline(steps, emit_front, emit_back):
                    pend = []
                    for st in steps:
                        pend.append(emit_front(st))
                        if len(pend) > LAG:
                            emit_back(pend.pop(0))
                    while pend:
                        emit_back(pend.pop(0))

                def recip_from_acc(accs, m, outs):
                    for a_, o_ in zip(accs, outs):
                        pi = ps_next(2, 8)
                        sc.op("pe", lambda: nc.tensor.matmul(psum[pi][0:m, :], lhsT=onesf[:, 0:m], rhs=acc[a_][:],
                                                             start=True, stop=True),
                              reads=[f"acc{a_}", "onesf"], writes=[f"ps{pi}"])
                        sc.op("act", lambda: nc.scalar.activation(out=tf[o_][0:m, :], in_=psum[pi][0:m, :],
                                                                  func=AF.Ln),
                              reads=[f"ps{pi}"], writes=[f"tf{o_}"])
                        sc.op("act", lambda: nc.scalar.activation(out=tf[o_][0:m, :], in_=tf[o_][0:m, :],
                                                                  func=AF.Exp, scale=-1.0),
                              reads=[f"tf{o_}"], writes=[f"tf{o_}"])

                nkb = (NHALO + 1) * 4
                for ci in range(4):
                    b2 = ci % 2
                    kk, vk = f"kab{b2}", f"vab{b2}"
                    sc.dma("sp", kk, kab[b2][:], KA[ci, :, j * nkb * 128:(j + 1) * nkb * 128], writes=[kk])
                    for hh in range(2):
                        hcol = (2 * ci + hh) * 64
                        sc.dma("sp", vk, vab[b2][:, :, hh, 0:64], VAv[:, j * nkb:(j + 1) * nkb, hcol:hcol + 64],
                               writes=[vk])

                    def front_a(kb):
                        ids = []
                        pis = []
                        for hh in range(2):
                            base = hh * 64
                            pi = ps_next(2, 8)
                            pis.append(pi)
                            sc.op("pe", lambda: nc.tensor.matmul(
                                psum[pi][:], lhsT=kab[b2][base:base + 64, kb * 128:(kb + 1) * 128],
                                rhs=qaT[base:base + 64, ci, :], start=True, stop=True),
                                reads=[kk, "qaT"], writes=[f"ps{pi}"])
                        bcol = (C_HB + 4 * j + kb // 4) if kb < NHALO * 4 else C_ZERO
                        m0 = M_DSW + (19 - kb) * 128
                        for hh in range(2):
                            pi = pis[hh]
                            pb_ = ptc[0] % NPT
                            ptc[0] += 1
                            ids.append(pb_)
                            sc.op("act", lambda: nc.scalar.activation(out=pT[pb_][:], in_=psum[pi][:], func=AF.Exp,
                                                                      scale=0.125, bias=ccol(bcol)),
                                  reads=[f"ps{pi}", "cst"], writes=[f"pT{pb_}"])
                            eng = "pool" if hh == 0 else "dve"
                            E_ = nc.gpsimd if hh == 0 else nc.vector
                            sc.op(eng, lambda: E_.tensor_tensor(out=pT[pb_][:], in0=pT[pb_][:],
                                                                in1=cmat[:, m0:m0 + 512], op=ALU.mult),
                                  reads=[f"pT{pb_}", "cmat"], writes=[f"pT{pb_}"])
                        return (kb, ids)

                    def back_a(item):
                        kb, ids = item
                        for hh in range(2):
                            pb_ = ids[hh]
                            sc.op("pe", lambda: nc.tensor.matmul(
                                psum[hh][:], lhsT=vab[b2][:, kb, hh, :], rhs=pT[pb_][:],
                                start=(kb == 0), stop=(kb == nkb - 1)),
                                reads=[vk, f"pT{pb_}"], writes=[f"ps{hh}"])

                    attn_pipeline(list(range(nkb)), front_a, back_a)
                    for hh in range(2):
                        h = 2 * ci + hh
                        sc.op("act", lambda: nc.scalar.activation(out=Rt[hh][64:128, :], in_=psum[hh][64:128, :],
                                                                  func=AF.Ln),
                              reads=[f"ps{hh}"], writes=[f"Rt{hh}"])
                        sc.op("act", lambda: nc.scalar.activation(out=Rt[hh][64:128, :], in_=Rt[hh][64:128, :],
                                                                  func=AF.Exp, scale=-1.0),
                              reads=[f"Rt{hh}"], writes=[f"Rt{hh}"])
                        pi = ps_next(2, 8)
                        sc.op("pe", lambda: nc.tensor.matmul(psum[pi][0:64, :], lhsT=cst[:, C_SHF:C_SHF + 64],
                                                             rhs=Rt[hh][:], start=True, stop=True),
                              reads=[f"Rt{hh}", "cst"], writes=[f"ps{pi}"])
                        sc.op("dve", lambda: nc.vector.tensor_tensor(out=tf[hh][0:64, :], in0=psum[pi][0:64, :],
                                                                     in1=gaT[:, h, :], op=ALU.mult),
                              reads=[f"ps{pi}", "gaT"], writes=[f"tf{hh}"])
                        sc.op("dve", lambda: nc.vector.tensor_tensor(out=yaT[:, h, :], in0=psum[hh][0:64, :],
                                                                     in1=tf[hh][0:64, :], op=ALU.mult),
                              reads=[f"ps{hh}", f"tf{hh}"], writes=["yaT"])

                npos = 8 * j + 8
                for h in range(4):
                    aset = (h % 2) * 2

                    def front_d(st):
                        pos, kb = st
                        kbi = (h * npos + pos) % NKB
                        kk, vk = f"kbuf{kbi}", f"vbuf{kbi}"
                        if kb == 0:
                            sc.dma("sp", kk, kbuf[kbi][:], KD[h, :, pos * TT:(pos + 1) * TT], writes=[kk])
                            sc.dma("sp", vk, vbuf[kbi][:], VDv[:, pos * 4:(pos + 1) * 4, h * 128:(h + 1) * 128],
                                   writes=[vk])
                        diag = (pos == own)
                        bcol = (C_CB + (pos - 8 * j)) if pos >= 8 * j else C_ZERO
                        q0 = kb * 128 if diag else 0
                        pis = []
                        ids = []
                        for comp in range(2):
                            base = comp * 64
                            pi = ps_next(2, 8)
                            pis.append(pi)
                            sc.op("pe", lambda: nc.tensor.matmul(
                                psum[pi][:, q0:TT], lhsT=kbuf[kbi][base:base + 64, kb * 128:(kb + 1) * 128],
                                rhs=qdT[base:base + 64, h, q0:TT], start=True, stop=True),
                                reads=[kk, "qdT"], writes=[f"ps{pi}"])
                        for comp in range(2):
                            pi = pis[comp]
                            pb_ = ptc[0] % NPT
                            ptc[0] += 1
                            ids.append(pb_)
                            sc.op("act", lambda: nc.scalar.activation(
                                out=pT[pb_][:, q0:TT], in_=psum[pi][:, q0:TT], func=AF.Exp, scale=0.125,
                                bias=ccol(bcol)), reads=[f"ps{pi}", "cst"], writes=[f"pT{pb_}"])
                            if diag:
                                sc.op("pool", lambda: nc.gpsimd.tensor_tensor(
                                    out=pT[pb_][:, q0:q0 + 128], in0=pT[pb_][:, q0:q0 + 128], in1=tri,
                                    op=ALU.mult), reads=[f"pT{pb_}", "cmat"], writes=[f"pT{pb_}"])
                            a_ = aset + comp
                            eng = "dve" if comp == 0 else "pool"
                            E_ = nc.vector if comp == 0 else nc.gpsimd
                            if pos == 0 and kb == 0:
                                sc.op(eng, lambda: E_.tensor_copy(out=acc[a_][:], in_=pT[pb_][:]),
                                      reads=[f"pT{pb_}"], writes=[f"acc{a_}"])
                            else:
                                sc.op(eng, lambda: E_.tensor_tensor(out=acc[a_][:, q0:TT], in0=acc[a_][:, q0:TT],
                                                                    in1=pT[pb_][:, q0:TT], op=ALU.add),
                                      reads=[f"pT{pb_}", f"acc{a_}"], writes=[f"acc{a_}"])
                        return (pos, kb, q0, kbi, ids)

                    def back_d(item):
                        pos, kb, q0, kbi, ids = item
                        vk = f"vbuf{kbi}"
                        st = (pos == 0 and kb == 0)
                        last = (pos == npos - 1 and kb == 3)
                        for comp in range(2):
                            pb_ = ids[comp]
                            sc.op("pe", lambda: nc.tensor.matmul(
                                psum[comp][:, q0:TT], lhsT=vbuf[kbi][:, kb, :], rhs=pT[pb_][:, q0:TT],
                                start=st, stop=last), reads=[vk, f"pT{pb_}"], writes=[f"ps{comp}"])

                    attn_pipeline([(pos, kb) for pos in range(npos) for kb in range(4)], front_d, back_d)
                    recip_from_acc([aset, aset + 1], 128, [0, 1])
                    for comp in range(2):
                        sc.op("dve", lambda comp=comp: nc.vector.tensor_tensor(
                            out=tf[2 + comp][:], in0=psum[comp][:], in1=tf[comp][:], op=ALU.mult),
                            reads=[f"ps{comp}", f"tf{comp}"], writes=[f"tf{2 + comp}"])
                    sc.op("dve", lambda: nc.vector.scalar_tensor_tensor(
                        out=tf[4][:], in0=tf[3][:], scalar=lamc[:, 0:1], in1=tf[2][:], op0=ALU.mult, op1=ALU.add),
                        reads=["tf3", "tf2", "lamc"], writes=["tf4"])
                    sc.op("act", lambda: nc.scalar.activation(out=sqo[:], in_=tf[4][:], func=AF.Square),
                          reads=["tf4"], writes=["sqo"])
                    pi = ps_next(2, 8)
                    sc.op("pe", lambda: nc.tensor.matmul(psum[pi][:], lhsT=ones, rhs=sqo[:], start=True, stop=True),
                          reads=["sqo", "cmat"], writes=[f"ps{pi}"])
                    sc.op("act", lambda: nc.scalar.activation(out=tf[5][:], in_=psum[pi][:], func=AF.Ln,
                                                              scale=1.0 / 128, bias=ccol(C_EPSR)),
                          reads=[f"ps{pi}", "cst"], writes=["tf5"])
                    sc.op("act", lambda: nc.scalar.activation(out=tf[5][:], in_=tf[5][:], func=AF.Exp, scale=-0.5),
                          reads=["tf5"], writes=["tf5"])
                    sc.op("pool", lambda: nc.gpsimd.tensor_tensor(out=tf[5][:], in0=tf[5][:], in1=gdT[:, h, :],
                                                                  op=ALU.mult),
                          reads=["tf5", "gdT"], writes=["tf5"])
                    sc.op("dve", lambda: nc.vector.scalar_tensor_tensor(
                        out=ydT[:, h, :], in0=tf[4][:], scalar=lamc[:, 1:2], in1=tf[5][:],
                        op0=ALU.mult, op1=ALU.mult), reads=["tf4", "tf5", "lamc"], writes=["ydT"])

                for dc in range(8):
                    pi = ps_next(2, 8)
                    for h in range(8):
                        sc.op("pe", lambda h=h: nc.tensor.matmul(
                            psum[pi][:], lhsT=woa[:, h, dc * 128:(dc + 1) * 128], rhs=yaT[:, h, :],
                            start=(h == 0), stop=False), reads=["woa", "yaT"], writes=[f"ps{pi}"])
                    for h in range(4):
                        sc.op("pe", lambda h=h: nc.tensor.matmul(
                            psum[pi][:], lhsT=wod[:, h, dc * 128:(dc + 1) * 128], rhs=ydT[:, h, :],
                            start=False, stop=(h == 3)), reads=["wod", "ydT"], writes=[f"ps{pi}"])
                    sc.op("dve", lambda: nc.vector.tensor_tensor(out=x1[:, dc, :], in0=psum[pi][:], in1=x1[:, dc, :],
                                                                 op=ALU.add),
                          reads=[f"ps{pi}", "x1"], writes=["x1"])
                sc.barrier()
            if stop == "p2":
                return nc

            with ExitStack() as p3:
                sq = sb("sq1", [128, 8, TT], BF16, p3)
                h1 = sb("h1", [128, 8, TT], BF16, p3)
                rs = sb("rs1", [128, TT], F32, p3)
                ssm1 = sb("ssm1", [128, TT], BF16, p3)
                wst = [sb(f"w1st{i}", [128, 4096], BF16, p3) for i in range(2)]
                uT = sb("uT", [128, 16, TT], BF16, p3)
                gsT = sb("gsT", [128, 16, TT], BF16, p3)
                gv = sb("gv", [128, 4, 2048], BF16, p3)
                vtmp = sb("vtmp", [128, 2048], F32, p3)
                lng = sb("lng", [128, 2, 2048], F32, p3)
                wsf = sb("wsf", [128, 8, 128], BF16, p3)
                bsb = sb("bsb_sb", [128, 8, 128], F32, p3)
                stats = sb("stats", [128, 4, 6], F32, p3)
                mv = sb("mv", [128, 4], F32, p3)
                tf = [sb(f"tg{i}", [128, TT], F32, p3) for i in range(3)]
                wsb = [0]

                def load_w1(src_ap, shape3):
                    b = wsb[0] % 2
                    wsb[0] += 1
                    a, bb = shape3
                    view = wst[b][:, 0:a * bb].rearrange("p (a b) -> p a b", b=bb)
                    sc.dma("sp", f"w1st{b}", view, src_ap, writes=[f"w1st{b}"])
                    return view, f"w1st{b}"

                sc.dma("sp", "lng", lng[:, 0, :], lngb_d[0:1, :].partition_broadcast(128), writes=["lng"])
                sc.dma("sp", "lng", lng[:, 1, :], lngb_d[1:2, :].partition_broadcast(128), writes=["lng"])
                sc.dma("pool", "wsf", wsf[:].rearrange("p g t -> p (g t)"), wsT_d, writes=["wsf"])
                sc.dma("sp", "bsb", bsb[:].rearrange("p g t -> p (g t)"), bsb_d.partition_broadcast(128),
                       writes=["bsb"])
                for g in range(8):
                    sc.op("pool", lambda g=g: nc.gpsimd.tensor_tensor(out=wsf[:, g, :], in0=wsf[:, g, :], in1=tri,
                                                                      op=ALU.mult),
                          reads=["wsf", "cmat"], writes=["wsf"])

                rmsnorm_tile(x1, "x1", C_SGUN, sq, "sq1", h1, "h1", rs, "rs1", ssm1, "ssm1")
                for wc in range(4):
                    w, wk = load_w1(W1v[:, :, 2048 + wc * 512:2048 + (wc + 1) * 512], (8, 512))
                    for sub in range(4):
                        pi = ps_next()
                        for k in range(8):
                            sc.op("pe", lambda k=k: nc.tensor.matmul(
                                psum[pi][:], lhsT=h1[:, k, sub * 128:(sub + 1) * 128], rhs=w[:, k, :],
                                start=(k == 0), stop=(k == 7)), reads=["h1", wk], writes=[f"ps{pi}"])
                        sc.op("act", lambda: nc.scalar.activation(out=gv[:, sub, wc * 512:(wc + 1) * 512],
                                                                  in_=psum[pi][:], func=AF.Gelu),
                              reads=[f"ps{pi}"], writes=[f"gv{sub}"])
                for sub in range(4):
                    for q in range(4):
                        sc.op("dve", lambda q=q: nc.vector.bn_stats(out=stats[:, q, :],
                                                                    in_=gv[:, sub, q * 512:(q + 1) * 512]),
                              reads=[f"gv{sub}"], writes=["stats"])
                    sc.op("dve", lambda: nc.vector.bn_aggr(out=mv[:, 0:2], in_=stats[:].rearrange("p a b -> p (a b)")),
                          reads=["stats"], writes=["mv"])
                    sc.op("act", lambda: nc.scalar.activation(out=mv[:, 2:3], in_=mv[:, 1:2], func=AF.Ln,
                                                              bias=ccol(C_EPSL)),
                          reads=["mv", "cst"], writes=["mv"])
                    sc.op("act", lambda: nc.scalar.activation(out=mv[:, 3:4], in_=mv[:, 2:3], func=AF.Exp, scale=-0.5),
                          reads=["mv"], writes=["mv"])
                    sc.op("dve", lambda: nc.vector.tensor_scalar(
                        out=vtmp[:], in0=gv[:, sub, :], scalar1=mv[:, 0:1], scalar2=mv[:, 3:4],
                        op0=ALU.subtract, op1=ALU.mult), reads=[f"gv{sub}", "mv"], writes=["vtmp"])
                    sc.op("pool", lambda: nc.gpsimd.tensor_tensor(out=vtmp[:], in0=vtmp[:], in1=lng[:, 0, :],
                                                                  op=ALU.mult),
                          reads=["vtmp", "lng"], writes=["vtmp"])
                    sc.op("pool", lambda: nc.gpsimd.tensor_tensor(out=gv[:, sub, :], in0=vtmp[:], in1=lng[:, 1, :],
                                                                  op=ALU.add),
                          reads=["vtmp", "lng"], writes=[f"gv{sub}"])
                for wc in range(4):
                    w, wk = load_w1(W1v[:, :, wc * 512:(wc + 1) * 512], (8, 512))
                    for s4 in range(4):
                        pi = ps_next()
                        proj_fm(pi, h1, "h1", w, wk, s4 * 128)
                        c = wc * 4 + s4
                        sc.op("act", lambda c=c: nc.scalar.activation(out=uT[:, c, :], in_=psum[pi][:], func=AF.Gelu),
                              reads=[f"ps{pi}"], writes=["uT"])
                for wc in range(4):
                    w, wk = load_w1(W1v[:, :, 4096 + wc * 512:4096 + (wc + 1) * 512], (8, 512))
                    for s4 in range(4):
                        pi = ps_next()
                        proj_fm(pi, h1, "h1", w, wk, s4 * 128)
                        c = wc * 4 + s4
                        sc.op("act", lambda c=c: nc.scalar.activation(out=gsT[:, c, :], in_=psum[pi][:], func=AF.Silu),
                              reads=[f"ps{pi}"], writes=["gsT"])
                for c in range(16):
                    g = c // 2
                    pi = ps_next()
                    for sub in range(4):
                        sc.op("pe", lambda sub=sub: nc.tensor.matmul(
                            psum[pi][:, sub * 128:(sub + 1) * 128], lhsT=gv[:, sub, c * 128:(c + 1) * 128],
                            rhs=wsf[:, g, :], start=True, stop=True),
                            reads=[f"gv{sub}", "wsf"], writes=[f"ps{pi}"])
                    tb = c % 3
                    for sub in range(4):
                        sc.op("dve", lambda sub=sub: nc.vector.tensor_tensor(
                            out=tf[tb][:, sub * 128:(sub + 1) * 128], in0=psum[pi][:, sub * 128:(sub + 1) * 128],
                            in1=bsb[:, g, :], op=ALU.add), reads=[f"ps{pi}", "bsb"], writes=[f"tg{tb}"])
                    sc.op("pool", lambda: nc.gpsimd.tensor_tensor(out=gsT[:, c, :], in0=gsT[:, c, :], in1=uT[:, c, :],
                                                                  op=ALU.mult),
                          reads=["gsT", "uT"], writes=["gsT"])
                    sc.op("dve", lambda: nc.vector.tensor_tensor(out=uT[:, c, :], in0=tf[tb][:], in1=gsT[:, c, :],
                                                                 op=ALU.mult),
                          reads=[f"tg{tb}", "gsT"], writes=["uT"])
                for q4 in range(4):
                    w, wk = load_w1(Wo1v[:, :, q4 * 256:(q4 + 1) * 256], (16, 256))
                    for d2 in range(2):
                        dc = q4 * 2 + d2
                        pi = ps_next()
                        for c in range(16):
                            sc.op("pe", lambda c=c: nc.tensor.matmul(
                                psum[pi][:], lhsT=w[:, c, d2 * 128:(d2 + 1) * 128], rhs=uT[:, c, :],
                                start=(c == 0), stop=(c == 15)), reads=[wk, "uT"], writes=[f"ps{pi}"])
                        tb = dc % 3
                        sc.op("dve", lambda: nc.vector.tensor_tensor(out=tf[tb][:], in0=psum[pi][:], in1=x1[:, dc, :],
                                                                     op=ALU.add),
                              reads=[f"ps{pi}", "x1"], writes=[f"tg{tb}"])
                        sc.dma("sp", f"tg{tb}", out_d[dc * 128:(dc + 1) * 128, j * TT:(j + 1) * TT], tf[tb][:],
                               reads=[f"tg{tb}"], writes=["out"])
                sc.barrier()
            if stop == "p3":
                return nc
    return nc


def _tile_order(c):
    order = []
    for j in range(NSLOT):
        order += [8 * j + i for i in range(c + 1, 8)] + [8 * j + i for i in range(0, c)] + [8 * j + c]
    return order


def _rope_tables(positions):
    rot = 16
    inv = 1.0 / (500000.0 ** (np.arange(0, rot, 2, dtype=np.float32) / rot))
    ang = positions.astype(np.float32)[:, None] * inv[None, :].astype(np.float32)
    cos, sin = np.cos(ang).astype(np.float32), np.sin(ang).astype(np.float32)
    T = positions.shape[0]
    cf = np.ones((128, T), np.float32)
    sf = np.zeros((128, T), np.float32)
    for hb in (0, 64):
        cf[hb:hb + 8] = cos.T
        cf[hb + 8:hb + 16] = cos.T
        sf[hb:hb + 8] = sin.T
        sf[hb + 8:hb + 16] = sin.T
    return np.stack([cf, sf], 0)


def _const_mats():
    cm = np.zeros((128, NCM), np.float32)
    cm[:, M_ONES:M_ONES + 128] = 1.0
    cm[0:64, M_BONES:M_BONES + 64] = 1.0
    cm[64:128, M_BONES + 64:M_BONES + 128] = 1.0
    for m in range(128):
        dl = m % 64
        if dl < 8:
            cm[m + 8, M_ROT + m] = -1.0
        elif dl < 16:
            cm[m - 8, M_ROT + m] = 1.0
    kk = np.arange(128)[:, None]
    qq = np.arange(128)[None, :]
    cm[:, M_TRI:M_TRI + 128] = (kk <= qq).astype(np.float32)
    for idx in range(23):
        delta = idx - 3
        d = 128 * delta + qq - kk
        m = ((d >= 0) & (d <= 128)).astype(np.float32)
        m += ((d % 4 == 0) & (d >= 0) & (d <= 512)).astype(np.float32)
        m += ((d % 16 == 0) & (d >= 0) & (d <= 2048)).astype(np.float32)
        cm[:, M_DSW + idx * 128:M_DSW + (idx + 1) * 128] = m
    return cm.astype(ml_dtypes.bfloat16)


_NC_CACHE = {}


def _prep(x, att_norm, att_w_in, dsw_q_norm, dsw_k_norm, diff_q_norm, diff_k_norm,
           diff_lam_q1, diff_lam_k1, diff_lam_q2, diff_lam_k2, diff_subln, att_w_out,
           sgu_norm, sgu_w_in, sgu_ln_g, sgu_ln_b, sgu_w_s, sgu_b_s, sgu_w_out):
    f = np.float32
    xT = np.ascontiguousarray(np.asarray(x, f)[0].T)
    xT_tiles = xT.reshape(D, NT, TT)
    cm = _const_mats()
    w_in0 = np.ascontiguousarray(np.asarray(att_w_in, f)[0])
    w_out0 = np.ascontiguousarray(np.asarray(att_w_out, f)[0])
    w_in1 = np.ascontiguousarray(np.asarray(sgu_w_in, f)[0])
    w_out1 = np.ascontiguousarray(np.asarray(sgu_w_out, f)[0])
    lngb = np.ascontiguousarray(np.stack([np.asarray(sgu_ln_g, f)[0], np.asarray(sgu_ln_b, f)[0]], 0))
    wsT = np.ascontiguousarray(np.asarray(sgu_w_s, f)[0].transpose(2, 0, 1).reshape(128, 8 * 128))
    bsb = np.ascontiguousarray(np.asarray(sgu_b_s, f)[0].reshape(1, 8 * 128))
    in_maps = []
    orders = []
    for c in range(NCORE):
        order = _tile_order(c)
        orders.append(order)
        x_all = np.ascontiguousarray(xT_tiles[:, order, :].reshape(D, S))
        pos_all = (np.asarray(order)[:, None] * TT + np.arange(TT)[None, :]).reshape(-1)
        halo_tiles = []
        for j in range(NSLOT):
            halo_tiles += [8 * j + c - NHALO + m for m in range(NHALO)]
        x_halo = np.zeros((D, NSLOT * NHALO, TT), f)
        pos_halo = np.zeros((NSLOT * NHALO, TT), np.int64)
        for i, t in enumerate(halo_tiles):
            if t >= 0:
                x_halo[:, i, :] = xT_tiles[:, t, :]
                pos_halo[i] = t * TT + np.arange(TT)
        cst = np.zeros((128, NCST), f)
        cst[:, C_ATTN:C_ATTN + 8] = np.asarray(att_norm, f)[0].reshape(8, 128).T
        cst[:, C_SGUN:C_SGUN + 8] = np.asarray(sgu_norm, f)[0].reshape(8, 128).T
        cst[:, C_DSWQ] = np.tile(np.asarray(dsw_q_norm, f)[0], 2)
        cst[:, C_DSWK] = np.tile(np.asarray(dsw_k_norm, f)[0], 2)
        cst[:, C_DIFQ] = np.tile(np.asarray(diff_q_norm, f)[0], 2)
        cst[:, C_DIFK] = np.tile(np.asarray(diff_k_norm, f)[0], 2)
        cst[:, C_SUBLN] = np.asarray(diff_subln, f)[0]
        for i in range(8):
            cst[:, C_CB + i] = 0.0 if (i >= 7 - c) else NEG
        for j in range(NSLOT):
            for m in range(NHALO):
                cst[:, C_HB + 4 * j + m] = 0.0 if (8 * j + c - NHALO + m) >= 0 else NEG
        cst[:, C_LQ1:C_LQ1 + 64] = np.asarray(diff_lam_q1, f)[0][None, :]
        cst[:, C_LK1:C_LK1 + 64] = np.asarray(diff_lam_k1, f)[0][None, :]
        cst[:, C_LQ2:C_LQ2 + 64] = np.asarray(diff_lam_q2, f)[0][None, :]
        cst[:, C_LK2:C_LK2 + 64] = np.asarray(diff_lam_k2, f)[0][None, :]
        cst[:, C_EPSR] = RMS_EPS
        cst[:, C_EPSL] = LN_EPS
        cst[:, C_ONE] = 1.0
        for m in range(64):
            cst[m + 64, C_SHF + m] = 1.0
        in_maps.append({
            "x_all": x_all,
            "x_halo": np.ascontiguousarray(x_halo.reshape(D, NSLOT * NHALO * TT)),
            "rope_all": _rope_tables(pos_all),
            "rope_halo": _rope_tables(pos_halo.reshape(-1)),
            "w_in0": w_in0, "w_out0": w_out0, "w_in1": w_in1, "w_out1": w_out1,
            "cst": cst, "cmat": cm, "lngb": lngb, "wsT": wsT, "bsb": bsb,
        })
    return in_maps


def kernel(**inputs):
    f = np.float32
    in_maps = _prep(**inputs)
    if "nc" not in _NC_CACHE:
        _NC_CACHE["nc"] = build_program()
    nc = _NC_CACHE["nc"]
    res = run_bass_kernel_spmd(nc, in_maps, core_ids=list(range(NCORE)))
    out = np.zeros((S, D), f)
    for c in range(NCORE):
        oT = np.asarray(res.results[c]["outT"], f)
        for j in range(NSLOT):
            t = 8 * j + c
            out[t * TT:(t + 1) * TT, :] = oT[:, j * TT:(j + 1) * TT].T
    return out[None]
```

```python
import math
from contextlib import ExitStack

import ml_dtypes
import numpy as np

import concourse.bass as bass
import concourse.mybir as mybir
from concourse.bass_utils import run_bass_kernel_spmd

F32 = mybir.dt.float32
BF16 = mybir.dt.bfloat16
AF = mybir.ActivationFunctionType
ALU = mybir.AluOpType

S = 16384
D = 1024
TT = 512
NT = S // TT
NCORE = 8
NSLOT = NT // NCORE
NHALO = 4
NEG = -30000.0
RMS_EPS = 1e-6
LN_EPS = 1e-5
LAM_INIT0 = 0.8 - 0.6 * math.exp(-0.3 * 0)

C_ATTN = 0
C_SGUN = 8
C_DSWQ = 16
C_DSWK = 17
C_DIFQ = 18
C_DIFK = 19
C_SUBLN = 20
C_CB = 21
C_HB = 29
C_ZERO = 45
C_LQ1 = 46
C_LK1 = 110
C_LQ2 = 174
C_LK2 = 238
C_EPSR = 302
C_EPSL = 303
C_ONE = 304
C_SHF = 305
NCST = 305 + 64
M_ONES = 0
M_BONES = 128
M_ROT = 256
M_TRI = 384
M_DSW = 512
NCM = 512 + 23 * 128


class Sched:
    def __init__(self, nc, es):
        self.nc = nc
        self.es = es
        self.E = {"pe": nc.tensor, "act": nc.scalar, "dve": nc.vector, "pool": nc.gpsimd, "sp": nc.sync}
        self.semh = {}
        self.cnt = {}
        for e in self.E:
            self.semh[e] = es.enter_context(nc.semaphore("s_" + e))
            self.cnt[e] = 0
        self.known = {e: {} for e in self.E}
        self.lastw = {}
        self.readers = {}
        self.nwait = 0

    def dsem(self, name):
        k = "d_" + name
        if k not in self.semh:
            self.semh[k] = self.es.enter_context(self.nc.semaphore(k))
            self.cnt[k] = 0
        return k

    def _wait(self, eng, deps):
        need = {}
        for (sk, val) in deps:
            if sk == eng and eng in ("pe", "sp"):
                continue
            if val > need.get(sk, 0):
                need[sk] = val
        for sk, val in need.items():
            if self.known[eng].get(sk, 0) >= val:
                continue
            self.E[eng].wait_ge(self.semh[sk], val)
            self.known[eng][sk] = val
            self.nwait += 1

    def _deps(self, reads, writes):
        deps = []
        for k in reads:
            if k in self.lastw:
                deps.append(self.lastw[k])
        for k in writes:
            if k in self.lastw:
                deps.append(self.lastw[k])
            deps.extend(self.readers.get(k, {}).items())
        return deps

    def _record(self, ev, reads, writes):
        for k in reads:
            r = self.readers.setdefault(k, {})
            if ev[1] > r.get(ev[0], 0):
                r[ev[0]] = ev[1]
        for k in writes:
            self.lastw[k] = ev
            self.readers[k] = {}

    def op(self, eng, fn, reads=(), writes=()):
        self._wait(eng, self._deps(reads, writes))
        ins = fn()
        self.cnt[eng] += 1
        ins.then_inc(self.semh[eng], 1)
        self._record((eng, self.cnt[eng]), reads, writes)

    def dma(self, q, sem, out, in_, reads=(), writes=()):
        sk = self.dsem(sem)
        self._wait(q, self._deps(reads, writes))
        ins = self.E[q].dma_start(out=out, in_=in_)
        self.cnt[sk] += 16
        ins.then_inc(self.semh[sk], 16)
        self._record((sk, self.cnt[sk]), reads, writes)

    def collective_allgather(self, in_ap, out_ap, wait_sems, writes):
        sk = self.dsem("cc")
        for ws in wait_sems:
            k = "d_" + ws
            c = self.cnt.get(k, 0)
            if c > 0 and self.known["pool"].get(k, 0) < c:
                self.nc.gpsimd.wait_ge(self.semh[k], c)
                self.known["pool"][k] = c
        ins = self.nc.gpsimd.collective_compute("AllGather", ALU.bypass, replica_groups=[list(range(NCORE))],
                                                ins=[in_ap], outs=[out_ap])
        self.cnt[sk] += 1
        ins.then_inc(self.semh[sk])
        self._record((sk, self.cnt[sk]), (), writes)

    def barrier(self):
        for e in self.E:
            for sk, c in self.cnt.items():
                if sk == e and e in ("pe", "sp"):
                    continue
                if c > 0 and self.known[e].get(sk, 0) < c:
                    self.E[e].wait_ge(self.semh[sk], c)
                    self.known[e][sk] = c
        self.lastw = {}
        self.readers = {}


def build_program(stop=None):
    nc = bass.Bass("TRN2", target_bir_lowering=False)
    x_own = nc.dram_tensor("x_own", [D, NSLOT * TT], F32, kind="ExternalInput").ap()
    x_halo = nc.dram_tensor("x_halo", [D, NSLOT * NHALO * TT], F32, kind="ExternalInput").ap()
    rope_own = nc.dram_tensor("rope_own", [2, 128, NSLOT * TT], F32, kind="ExternalInput").ap()
    rope_halo = nc.dram_tensor("rope_halo", [2, 128, NSLOT * NHALO * TT], F32, kind="ExternalInput").ap()
    w_in0 = nc.dram_tensor("w_in0", [D, 4096], F32, kind="ExternalInput").ap()
    w_out0 = nc.dram_tensor("w_out0", [D, D], F32, kind="ExternalInput").ap()
    w_in1 = nc.dram_tensor("w_in1", [D, 6144], F32, kind="ExternalInput").ap()
    w_out1 = nc.dram_tensor("w_out1", [2048, D], F32, kind="ExternalInput").ap()
    cst_d = nc.dram_tensor("cst", [128, NCST], F32, kind="ExternalInput").ap()
    cmat_d = nc.dram_tensor("cmat", [128, NCM], BF16, kind="ExternalInput").ap()
    lngb_d = nc.dram_tensor("lngb", [2, 2048], F32, kind="ExternalInput").ap()
    wsT_d = nc.dram_tensor("wsT", [128, 8 * 128], F32, kind="ExternalInput").ap()
    bsb_d = nc.dram_tensor("bsb", [1, 8 * 128], F32, kind="ExternalInput").ap()
    out_d = nc.dram_tensor("outT", [D, NSLOT * TT], F32, kind="ExternalOutput").ap()
    W0b = nc.dram_tensor("W0b", [D, 4096], BF16).ap()
    Wo0b = nc.dram_tensor("Wo0b", [D, D], BF16).ap()
    W1b = nc.dram_tensor("W1b", [D, 6144], BF16).ap()
    Wo1b = nc.dram_tensor("Wo1b", [2048, D], BF16).ap()
    KVl = nc.dram_tensor("KVl", [1024, NSLOT * TT], BF16).ap()
    KVg = nc.dram_tensor("KVg", [NCORE * 1024, NSLOT * TT], BF16).ap()
    KV2 = [nc.dram_tensor(f"KV2_{r}", [1024, NSLOT * TT], BF16).ap() for r in range(NCORE)]
    NKA = NSLOT * (NHALO + 1) * TT
    KA = nc.dram_tensor("KA", [4, 128, NKA], BF16).ap()
    VA = nc.dram_tensor("VA", [NKA, 512], BF16).ap()
    QA = nc.dram_tensor("QA", [4, 128, NSLOT * TT], BF16).ap()
    QD = nc.dram_tensor("QD", [4, 128, NSLOT * TT], BF16).ap()
    GA = nc.dram_tensor("GA", [8, 64, NSLOT * TT], BF16).ap()
    GD = nc.dram_tensor("GD", [4, 128, NSLOT * TT], BF16).ap()

    with ExitStack() as es:
        es.enter_context(nc.allow_low_precision("bf16 matmul operands, fp32 accumulation"))
        es.enter_context(nc.allow_non_contiguous_dma("strided weight / scratch tiles"))
        sc = Sched(nc, es)

        uid = [0]

        def sb(name, shape, dt, stack=es):
            uid[0] += 1
            return stack.enter_context(nc.sbuf_tensor(f"{name}_{uid[0]}", shape, dt))

        cst = sb("cst_sb", [128, NCST], F32)
        cmat = sb("cmat_sb", [128, NCM], BF16)
        lamc = sb("lamc", [128, 8], F32)
        x1 = sb("x1", [128, 8, TT], F32)
        onesf = sb("onesf", [128, 128], F32)
        psum = [es.enter_context(nc.psum_tensor(f"ps{i}", [128, TT], F32)) for i in range(8)]
        ps_rr = [0]

        def ps_next(lo=0, hi=8):
            i = lo + (ps_rr[0] % (hi - lo))
            ps_rr[0] += 1
            return i

        ones = cmat[:, M_ONES:M_ONES + 128]
        bones = cmat[:, M_BONES:M_BONES + 128]
        rotm = cmat[:, M_ROT:M_ROT + 128]
        tri = cmat[:, M_TRI:M_TRI + 128]

        def ccol(c, n=128):
            return cst[0:n, c:c + 1]

        sc.dma("sp", "cst", cst[:], cst_d, writes=["cst"])
        sc.dma("sp", "cmat", cmat[:], cmat_d, writes=["cmat"])
        for (src, dst, rows, nm) in ((w_in0, W0b, D, "W0b"), (w_out0, Wo0b, D, "Wo0b"),
                                     (w_in1, W1b, D, "W1b"), (w_out1, Wo1b, 2048, "Wo1b")):
            step = 128
            for r0 in range(0, rows, step):
                sc.dma("pool", "wcast", dst[r0:r0 + step, :], src[r0:r0 + step, :], writes=[nm])
                done = sc.cnt["d_wcast"] - 32
                if done > 0:
                    nc.gpsimd.wait_ge(sc.semh["d_wcast"], done)
                    sc.known["pool"]["d_wcast"] = done
        sc.op("dve", lambda: nc.vector.memset(onesf[:], 1.0), writes=["onesf"])
        with ExitStack() as ls:
            lt = sb("lam_t", [128, 128], F32, ls)
            for i, (ca, cb) in enumerate(((C_LQ1, C_LK1), (C_LQ2, C_LK2))):
                sc.op("dve", lambda ca=ca, cb=cb, i=i: nc.vector.tensor_tensor(
                    out=lt[:, i * 64:(i + 1) * 64], in0=cst[:, ca:ca + 64], in1=cst[:, cb:cb + 64], op=ALU.mult),
                    reads=["cst"], writes=["lt"])
            for i in range(2):
                sc.op("dve", lambda i=i: nc.vector.reduce_sum(
                    out=lamc[:, 2 + i:3 + i], in_=lt[:, i * 64:(i + 1) * 64], axis=mybir.AxisListType.X),
                    reads=["lt"], writes=["lamc"])
            sc.op("act", lambda: nc.scalar.activation(out=lamc[:, 4:6], in_=lamc[:, 2:4], func=AF.Exp),
                  reads=["lamc"], writes=["lamc"])
            sc.op("dve", lambda: nc.vector.scalar_tensor_tensor(
                out=lamc[:, 0:1], in0=lamc[:, 5:6], scalar=-LAM_INIT0, in1=lamc[:, 4:5],
                op0=ALU.add, op1=ALU.subtract), reads=["lamc"], writes=["lamc"])
            sc.op("dve", lambda: nc.vector.tensor_scalar(
                out=lamc[:, 1:2], in0=cst[:, C_SUBLN:C_SUBLN + 1], scalar1=1.0 - LAM_INIT0, scalar2=None,
                op0=ALU.mult), reads=["cst", "lamc"], writes=["lamc"])
            sc.barrier()

        def rmsnorm_tile(xin, xk, gcol0, sq, sqk, hT, hk, rs, rsk, ssum, ssk, pieces=None):
            sc.op("act", lambda: nc.scalar.activation(out=sq[:], in_=xin[:], func=AF.Square),
                  reads=[xk], writes=[sqk])
            sc.op("dve", lambda: nc.vector.reduce_sum(out=ssum[:], in_=sq[:].rearrange("p k t -> p t k"),
                                                      axis=mybir.AxisListType.X),
                  reads=[sqk], writes=[ssk])
            pi = ps_next()
            sc.op("pe", lambda: nc.tensor.matmul(psum[pi][:], lhsT=ones, rhs=ssum[:], start=True, stop=True),
                  reads=[ssk, "cmat"], writes=[f"ps{pi}"])
            sc.op("act", lambda: nc.scalar.activation(out=rs[:], in_=psum[pi][:], func=AF.Ln,
                                                      scale=1.0 / D, bias=ccol(C_EPSR)),
                  reads=[f"ps{pi}", "cst"], writes=[rsk])
            sc.op("act", lambda: nc.scalar.activation(out=rs[:], in_=rs[:], func=AF.Exp, scale=-0.5),
                  reads=[rsk], writes=[rsk])
            def piece(k):
                sc.op("dve", lambda: nc.vector.scalar_tensor_tensor(
                    out=hT[:, k, :], in0=xin[:, k, :], scalar=cst[:, gcol0 + k:gcol0 + k + 1], in1=rs[:],
                    op0=ALU.mult, op1=ALU.mult), reads=[xk, rsk, "cst"], writes=[hk])
            if pieces is None:
                for k in range(8):
                    piece(k)
            else:
                for k in range(8):
                    pieces.append(lambda k=k: piece(k))

        def proj_fm(pi, hT, hk, w, wk, c0, m=128, prow=128):
            for k in range(8):
                sc.op("pe", lambda k=k: nc.tensor.matmul(psum[pi][0:m, :], lhsT=w[:, k, c0:c0 + m], rhs=hT[:, k, :],
                                                       start=(k == 0), stop=(k == 7)),
                      reads=[hk, wk], writes=[f"ps{pi}"])

        if stop == "setup":
            return nc
        with ExitStack() as p1:
            xin = [sb(f"xin{i}", [128, 8, TT], F32, p1) for i in range(2)]
            sq = [sb(f"sq{i}", [128, 8, TT], BF16, p1) for i in range(2)]
            hTs = [sb(f"hT{i}", [128, 8, TT], BF16, p1) for i in range(2)]
            rs = [sb(f"rs{i}", [128, TT], F32, p1) for i in range(2)]
            ssm = [sb(f"ssm{i}", [128, TT], BF16, p1) for i in range(2)]
            wkvd = sb("wkvd", [128, 8, 1024], BF16, p1)
            wkva = sb("wkva", [128, 8, 1024], BF16, p1)
            wst = [sb(f"wst{i}", [128, 8, 512], BF16, p1) for i in range(2)]
            rope = [sb(f"rope{i}", [128, 2, TT], F32, p1) for i in range(2)]
            NPB = 3
            sqc = [sb(f"sqc{i}", [128, TT], BF16, p1) for i in range(NPB)]
            qa_ = [sb(f"qA{i}", [128, TT], BF16, p1) for i in range(NPB)]
            rsc = [sb(f"rsc{i}", [128, TT], F32, p1) for i in range(NPB)]
            t1 = [sb(f"t1{i}", [128, TT], F32, p1) for i in range(NPB)]
            t2 = [sb(f"t2{i}", [128, TT], F32, p1) for i in range(NPB)]
            ko = [sb(f"ko{i}", [128, TT], BF16, p1) for i in range(NPB)]
            vst = [sb(f"vst{i}", [128, TT], BF16, p1) for i in range(2)]
            pb = [0]
            vb = [0]
            wsb = [0]

            W0v = W0b.rearrange("(k p) e -> p k e", p=128)
            sc.dma("sp", "wkvd", wkvd[:, :, 0:512], W0v[:, :, 2560:3072], reads=["W0b"], writes=["wkvd"])
            sc.dma("sp", "wkvd", wkvd[:, :, 512:1024], W0v[:, :, 3072:3584], reads=["W0b"], writes=["wkvd"])
            sc.dma("sp", "wkva", wkva[:, :, 0:512], W0v[:, :, 512:1024], reads=["W0b"], writes=["wkva"])
            sc.dma("sp", "wkva", wkva[:, :, 512:1024], W0v[:, :, 1024:1536], reads=["W0b"], writes=["wkva"])

            deferred = []

            def flush_deferred():
                while deferred:
                    deferred.pop(0)()

            def qk_post(pi, gcol, ropet, ropek, dst_ap, dkey):
                b = pb[0] % NPB
                pb[0] += 1
                sc.op("act", lambda: nc.scalar.activation(out=sqc[b][:], in_=psum[pi][:], func=AF.Square),
                      reads=[f"ps{pi}"], writes=[f"sqc{b}"])
                sc.op("act", lambda: nc.scalar.activation(out=qa_[b][:], in_=psum[pi][:], func=AF.Identity,
                                                          scale=ccol(gcol)),
                      reads=[f"ps{pi}", "cst"], writes=[f"qA{b}"])
                flush_deferred()
                front_piece()
                deferred.append(lambda: qk_post_b(b, ropet, ropek, dst_ap, dkey))

            def qk_post_b(b, ropet, ropek, dst_ap, dkey):
                p2 = ps_next()
                sc.op("pe", lambda: nc.tensor.matmul(psum[p2][:], lhsT=bones, rhs=sqc[b][:], start=True, stop=True),
                      reads=[f"sqc{b}", "cmat"], writes=[f"ps{p2}"])
                p3 = ps_next()
                sc.op("pe", lambda: nc.tensor.matmul(psum[p3][:], lhsT=rotm, rhs=qa_[b][:], start=True, stop=True),
                      reads=[f"qA{b}", "cmat"], writes=[f"ps{p3}"])
                sc.op("act", lambda: nc.scalar.activation(out=rsc[b][:], in_=psum[p2][:], func=AF.Ln,
                                                          scale=1.0 / 64, bias=ccol(C_EPSR)),
                      reads=[f"ps{p2}", "cst"], writes=[f"rsc{b}"])
                sc.op("act", lambda: nc.scalar.activation(out=rsc[b][:], in_=rsc[b][:], func=AF.Exp, scale=-0.5),
                      reads=[f"rsc{b}"], writes=[f"rsc{b}"])
                sc.op("pool", lambda: nc.gpsimd.tensor_tensor(out=t1[b][:], in0=qa_[b][:], in1=ropet[:, 0, :],
                                                              op=ALU.mult),
                      reads=[f"qA{b}", ropek], writes=[f"t1{b}"])
                sc.op("dve", lambda: nc.vector.tensor_tensor(out=t2[b][:], in0=psum[p3][:], in1=ropet[:, 1, :],
                                                             op=ALU.mult),
                      reads=[f"ps{p3}", ropek], writes=[f"t2{b}"])
                sc.op("pool", lambda: nc.gpsimd.tensor_tensor(out=t1[b][:], in0=t1[b][:], in1=t2[b][:], op=ALU.add),
                      reads=[f"t1{b}", f"t2{b}"], writes=[f"t1{b}"])
                sc.op("dve", lambda: nc.vector.tensor_tensor(out=ko[b][:], in0=t1[b][:], in1=rsc[b][:], op=ALU.mult),
                      reads=[f"t1{b}", f"rsc{b}"], writes=[f"ko{b}"])
                sc.dma("sp", f"ko{b}", dst_ap, ko[b][:], reads=[f"ko{b}"], writes=[dkey])

            def v_tm(hT, hk, w, wk, c0, dst_rows_fn, dkey):
                for sub in range(4):
                    pi = ps_next()
                    for k in range(8):
                        sc.op("pe", lambda k=k: nc.tensor.matmul(
                            psum[pi][:], lhsT=hT[:, k, sub * 128:(sub + 1) * 128], rhs=w[:, k, c0:c0 + 512],
                            start=(k == 0), stop=(k == 7)), reads=[hk, wk], writes=[f"ps{pi}"])
                    flush_deferred()
                    front_piece()
                    b = vb[0] % 2
                    vb[0] += 1
                    sc.op("act", lambda: nc.scalar.activation(out=vst[b][:], in_=psum[pi][:], func=AF.Copy),
                          reads=[f"ps{pi}"], writes=[f"vst{b}"])
                    sc.dma("sp", f"vst{b}", dst_rows_fn(sub), vst[b][:], reads=[f"vst{b}"], writes=[dkey])

            def silu_fm(pi, m, dst_ap, dkey):
                flush_deferred()
                b = pb[0] % NPB
                pb[0] += 1
                sc.op("act", lambda: nc.scalar.activation(out=ko[b][0:m, :], in_=psum[pi][0:m, :], func=AF.Silu),
                      reads=[f"ps{pi}"], writes=[f"ko{b}"])
                sc.dma("sp", f"ko{b}", dst_ap, ko[b][0:m, :], reads=[f"ko{b}"], writes=[dkey])

            def load_wst(c0):
                b = wsb[0] % 2
                wsb[0] += 1
                sc.dma("sp", f"wst{b}", wst[b][:], W0v[:, :, c0:c0 + 512], reads=["W0b"], writes=[f"wst{b}"])
                return wst[b], f"wst{b}"

            tiles = [("all", j) for j in range(NSLOT)] + [("halo", i) for i in range(NSLOT * NHALO)]
            xav = x_own.rearrange("(k p) t -> p k t", p=128)
            VDl = KVl[512:1024, :].rearrange("a (b e) -> (a b) e", e=512)
            xhv = x_halo.rearrange("(k p) t -> p k t", p=128)

            def load_x(ti):
                kind, idx = tiles[ti]
                xb = ti % 2
                srcx = xav if kind == "all" else xhv
                sc.dma("sp", f"xin{xb}", xin[xb][:], srcx[:, :, idx * TT:(idx + 1) * TT], writes=[f"xin{xb}"])

            def load_rope(ti):
                kind, idx = tiles[ti]
                xb = ti % 2
                srcr = rope_own if kind == "all" else rope_halo
                sc.dma("sp", f"rope{xb}", rope[xb][:],
                       srcr.rearrange("c p t -> p c t")[:, :, idx * TT:(idx + 1) * TT], writes=[f"rope{xb}"])

            fpieces = []

            def front_piece():
                if fpieces:
                    fpieces.pop(0)()

            def front(ti, spread=True):
                xb = ti % 2
                while fpieces:
                    fpieces.pop(0)()
                rmsnorm_tile(xin[xb], f"xin{xb}", C_ATTN, sq[xb], f"sq{xb}", hTs[xb], f"hT{xb}", rs[xb], f"rs{xb}",
                             ssm[xb], f"ssm{xb}", pieces=(fpieces if spread else None))

            load_x(0)
            load_rope(0)
            front(0, spread=False)
            if len(tiles) > 1:
                load_x(1)
            for ti, (kind, idx) in enumerate(tiles):
                xb = ti % 2
                xk = f"xin{xb}"
                rk = f"rope{xb}"
                hT, hk = hTs[xb], f"hT{xb}"
                if ti == NSLOT:
                    flush_deferred()
                    sc.collective_allgather(KVl, KVg, [f"ko{i}" for i in range(NPB)] + ["vst0", "vst1"], ["KVg"])
                    for r in range(NCORE):
                        sc.dma("sp", "kvcopy", KV2[r], KVg[r * 1024:(r + 1) * 1024, :], reads=["KVg"], writes=["KV2"])
                if ti + 1 < len(tiles):
                    load_rope(ti + 1)
                    front(ti + 1)
                if ti + 2 < len(tiles):
                    load_x(ti + 2)
                if kind == "all":
                    pos = idx
                    for ch in range(4):
                        pi = ps_next()
                        proj_fm(pi, hT, hk, wkvd, "wkvd", ch * 128)
                        qk_post(pi, C_DIFK, rope[xb], rk, KVl[ch * 128:(ch + 1) * 128, pos * TT:(pos + 1) * TT], "KD")
                    v_tm(hT, hk, wkvd, "wkvd", 512,
                         lambda sub, pos=pos: VDl[(pos * 4 + sub) * 128:(pos * 4 + sub + 1) * 128, :], "VD")
                    if True:
                        j = pos
                        kpos = j * (NHALO + 1) + NHALO
                        for ch in range(4):
                            pi = ps_next()
                            proj_fm(pi, hT, hk, wkva, "wkva", ch * 128)
                            qk_post(pi, C_DSWK, rope[xb], rk, KA[ch, :, kpos * TT:(kpos + 1) * TT], "KA")
                        v_tm(hT, hk, wkva, "wkva", 512,
                             lambda sub, kpos=kpos: VA[(kpos * 4 + sub) * 128:(kpos * 4 + sub + 1) * 128, :], "VA")
                        w, wk = load_wst(0)
                        for ch in range(4):
                            pi = ps_next()
                            proj_fm(pi, hT, hk, w, wk, ch * 128)
                            qk_post(pi, C_DSWQ, rope[xb], rk, QA[ch, :, j * TT:(j + 1) * TT], "QA")
                        w, wk = load_wst(1536)
                        for h in range(8):
                            pi = ps_next()
                            proj_fm(pi, hT, hk, w, wk, h * 64, m=64)
                            silu_fm(pi, 64, GA[h, :, j * TT:(j + 1) * TT], "GA")
                        w, wk = load_wst(2048)
                        for ch in range(4):
                            pi = ps_next()
                            proj_fm(pi, hT, hk, w, wk, ch * 128)
                            qk_post(pi, C_DIFQ, rope[xb], rk, QD[ch, :, j * TT:(j + 1) * TT], "QD")
                        w, wk = load_wst(3584)
                        for ch in range(4):
                            pi = ps_next()
                            proj_fm(pi, hT, hk, w, wk, ch * 128)
                            silu_fm(pi, 128, GD[ch, :, j * TT:(j + 1) * TT], "GD")
                else:
                    j, m = idx // NHALO, idx % NHALO
                    kpos = j * (NHALO + 1) + m
                    for ch in range(4):
                        pi = ps_next()
                        proj_fm(pi, hT, hk, wkva, "wkva", ch * 128)
                        qk_post(pi, C_DSWK, rope[xb], rk, KA[ch, :, kpos * TT:(kpos + 1) * TT], "KA")
                    v_tm(hT, hk, wkva, "wkva", 512,
                         lambda sub, kpos=kpos: VA[(kpos * 4 + sub) * 128:(kpos * 4 + sub + 1) * 128, :], "VA")
                flush_deferred()
                while fpieces:
                    fpieces.pop(0)()
            sc.barrier()

        if stop in ("p1", "p1a"):
            return nc
        Wo0v_a = Wo0b[0:512, :].rearrange("(h d) e -> d h e", d=64)
        Wo0v_d = Wo0b[512:1024, :].rearrange("(h d) e -> d h e", d=128)
        def kd_tile(r, jj, h):
            src = KVl if r is None else KV2[r]
            return src[h * 128:(h + 1) * 128, jj * TT:(jj + 1) * TT]

        def vd_tile(r, jj, h):
            src = KVl if r is None else KV2[r]
            v = src[512:1024, :].rearrange("a (b e) -> (a b) e", e=512).rearrange("(n p) e -> p n e", p=128)
            return v[:, jj * 4:(jj + 1) * 4, h * 128:(h + 1) * 128]
        VAv = VA.rearrange("(n p) e -> p n e", p=128)
        W1v = W1b.rearrange("(k p) e -> p k e", p=128)
        Wo1v = Wo1b.rearrange("(c p) e -> p c e", p=128)
        xav = x_own.rearrange("(k p) t -> p k t", p=128)
        for j in range(NSLOT):
            own = 8 * j + 7
            with ExitStack() as p2:
                qaT = sb("qaT", [128, 4, TT], BF16, p2)
                qdT = sb("qdT", [128, 4, TT], BF16, p2)
                gaT = sb("gaT", [64, 8, TT], BF16, p2)
                gdT = sb("gdT", [128, 4, TT], BF16, p2)
                yaT = sb("yaT", [64, 8, TT], BF16, p2)
                ydT = sb("ydT", [128, 4, TT], BF16, p2)
                woa = sb("woa", [64, 8, D], BF16, p2)
                wod = sb("wod", [128, 4, D], BF16, p2)
                NKB = 3
                kbuf = [sb(f"kbuf{i}", [128, TT], BF16, p2) for i in range(NKB)]
                vbuf = [sb(f"vbuf{i}", [128, 4, 128], BF16, p2) for i in range(NKB)]
                kab = [sb(f"kab{i}", [128, (NHALO + 1) * TT], BF16, p2) for i in range(2)]
                vab = [sb(f"vab{i}", [128, (NHALO + 1) * 4, 2, 128], BF16, p2) for i in range(2)]
                Rt = [sb(f"Rt{i}", [128, TT], F32, p2) for i in range(2)]
                for i in range(2):
                    sc.op("dve", lambda i=i: nc.vector.memset(Rt[i][:], 0.0), writes=[f"Rt{i}"])
                    sc.op("pool", lambda i=i: nc.gpsimd.memset(vab[i][:, :, :, 64:128], 1.0), writes=[f"vab{i}"])
                LAG = 2
                NPT = 2 * (LAG + 2)
                pT = [sb(f"pT{i}", [128, TT], BF16, p2) for i in range(NPT)]
                acc = [sb(f"acc{i}", [128, TT], F32, p2) for i in range(4)]
                tf = [sb(f"tf{i}", [128, TT], F32, p2) for i in range(8)]
                sqo = sb("sqo", [128, TT], BF16, p2)
                ptc = [0]

                sc.dma("sp", "qaT", qaT[:], QA.rearrange("c p t -> p c t")[:, :, j * TT:(j + 1) * TT], writes=["qaT"])
                sc.dma("sp", "qdT", qdT[:], QD.rearrange("c p t -> p c t")[:, :, j * TT:(j + 1) * TT], writes=["qdT"])
                sc.dma("sp", "gaT", gaT[:], GA.rearrange("c p t -> p c t")[:, :, j * TT:(j + 1) * TT], writes=["gaT"])
                sc.dma("sp", "gdT", gdT[:], GD.rearrange("c p t -> p c t")[:, :, j * TT:(j + 1) * TT], writes=["gdT"])
                sc.dma("sp", "woa", woa[:], Wo0v_a, writes=["woa"])
                sc.dma("sp", "wod", wod[:], Wo0v_d, writes=["wod"])
                sc.dma("sp", "x1", x1[:], xav[:, :, j * TT:(j + 1) * TT], writes=["x1"])

                def attn_pipeline(steps, emit_front, emit_back):
                    pend = []
                    for st in steps:
                        pend.append(emit_front(st))
                        if len(pend) > LAG:
                            emit_back(pend.pop(0))
                    while pend:
                        emit_back(pend.pop(0))

                def recip_from_acc(accs, m, outs):
                    for a_, o_ in zip(accs, outs):
                        pi = ps_next(2, 8)
                        sc.op("pe", lambda: nc.tensor.matmul(psum[pi][0:m, :], lhsT=onesf[:, 0:m], rhs=acc[a_][:],
                                                             start=True, stop=True),
                              reads=[f"acc{a_}", "onesf"], writes=[f"ps{pi}"])
                        sc.op("act", lambda: nc.scalar.activation(out=tf[o_][0:m, :], in_=psum[pi][0:m, :],
                                                                  func=AF.Ln),
                              reads=[f"ps{pi}"], writes=[f"tf{o_}"])
                        sc.op("act", lambda: nc.scalar.activation(out=tf[o_][0:m, :], in_=tf[o_][0:m, :],
                                                                  func=AF.Exp, scale=-1.0),
                              reads=[f"tf{o_}"], writes=[f"tf{o_}"])

                nkb = (NHALO + 1) * 4
                for ci in range(4):
                    b2 = ci % 2
                    kk, vk = f"kab{b2}", f"vab{b2}"
                    sc.dma("sp", kk, kab[b2][:], KA[ci, :, j * nkb * 128:(j + 1) * nkb * 128], writes=[kk])
                    for hh in range(2):
                        hcol = (2 * ci + hh) * 64
                        sc.dma("sp", vk, vab[b2][:, :, hh, 0:64], VAv[:, j * nkb:(j + 1) * nkb, hcol:hcol + 64],
                               writes=[vk])

                    def front_a(kb):
                        ids = []
                        pis = []
                        for hh in range(2):
                            base = hh * 64
                            pi = ps_next(2, 8)
                            pis.append(pi)
                            sc.op("pe", lambda: nc.tensor.matmul(
                                psum[pi][:], lhsT=kab[b2][base:base + 64, kb * 128:(kb + 1) * 128],
                                rhs=qaT[base:base + 64, ci, :], start=True, stop=True),
                                reads=[kk, "qaT"], writes=[f"ps{pi}"])
                        bcol = (C_HB + 4 * j + kb // 4) if kb < NHALO * 4 else C_ZERO
                        m0 = M_DSW + (19 - kb) * 128
                        for hh in range(2):
                            pi = pis[hh]
                            pb_ = ptc[0] % NPT
                            ptc[0] += 1
                            ids.append(pb_)
                            sc.op("act", lambda: nc.scalar.activation(out=pT[pb_][:], in_=psum[pi][:], func=AF.Exp,
                                                                      scale=0.125, bias=ccol(bcol)),
                                  reads=[f"ps{pi}", "cst"], writes=[f"pT{pb_}"])
                            eng = "pool" if hh == 0 else "dve"
                            E_ = nc.gpsimd if hh == 0 else nc.vector
                            sc.op(eng, lambda: E_.tensor_tensor(out=pT[pb_][:], in0=pT[pb_][:],
                                                                in1=cmat[:, m0:m0 + 512], op=ALU.mult),
                                  reads=[f"pT{pb_}", "cmat"], writes=[f"pT{pb_}"])
                        return (kb, ids)

                    def back_a(item):
                        kb, ids = item
                        for hh in range(2):
                            pb_ = ids[hh]
                            sc.op("pe", lambda: nc.tensor.matmul(
                                psum[hh][:], lhsT=vab[b2][:, kb, hh, :], rhs=pT[pb_][:],
                                start=(kb == 0), stop=(kb == nkb - 1)),
                                reads=[vk, f"pT{pb_}"], writes=[f"ps{hh}"])

                    attn_pipeline(list(range(nkb)), front_a, back_a)
                    for hh in range(2):
                        h = 2 * ci + hh
                        sc.op("act", lambda: nc.scalar.activation(out=Rt[hh][64:128, :], in_=psum[hh][64:128, :],
                                                                  func=AF.Ln),
                              reads=[f"ps{hh}"], writes=[f"Rt{hh}"])
                        sc.op("act", lambda: nc.scalar.activation(out=Rt[hh][64:128, :], in_=Rt[hh][64:128, :],
                                                                  func=AF.Exp, scale=-1.0),
                              reads=[f"Rt{hh}"], writes=[f"Rt{hh}"])
                        pi = ps_next(2, 8)
                        sc.op("pe", lambda: nc.tensor.matmul(psum[pi][0:64, :], lhsT=cst[:, C_SHF:C_SHF + 64],
                                                             rhs=Rt[hh][:], start=True, stop=True),
                              reads=[f"Rt{hh}", "cst"], writes=[f"ps{pi}"])
                        sc.op("dve", lambda: nc.vector.tensor_tensor(out=tf[hh][0:64, :], in0=psum[pi][0:64, :],
                                                                     in1=gaT[:, h, :], op=ALU.mult),
                              reads=[f"ps{pi}", "gaT"], writes=[f"tf{hh}"])
                        sc.op("dve", lambda: nc.vector.tensor_tensor(out=yaT[:, h, :], in0=psum[hh][0:64, :],
                                                                     in1=tf[hh][0:64, :], op=ALU.mult),
                              reads=[f"ps{hh}", f"tf{hh}"], writes=["yaT"])

                ktiles = [(r, jj) for jj in range(j + 1) for r in range(NCORE)] + [(None, j)]
                npos = len(ktiles)
                own = npos - 1
                for h in range(4):
                    aset = (h % 2) * 2

                    def front_d(st):
                        pos, kb = st
                        kbi = (h * npos + pos) % NKB
                        kk, vk = f"kbuf{kbi}", f"vbuf{kbi}"
                        r_, jj_ = ktiles[pos]
                        if kb == 0:
                            sc.dma("sp", kk, kbuf[kbi][:], kd_tile(r_, jj_, h), reads=["KV2"], writes=[kk])
                            sc.dma("sp", vk, vbuf[kbi][:], vd_tile(r_, jj_, h), reads=["KV2"], writes=[vk])
                        diag = (pos == own)
                        bcol = (C_CB + r_) if (r_ is not None and jj_ == j) else C_ZERO
                        q0 = kb * 128 if diag else 0
                        pis = []
                        ids = []
                        for comp in range(2):
                            base = comp * 64
                            pi = ps_next(2, 8)
                            pis.append(pi)
                            sc.op("pe", lambda: nc.tensor.matmul(
                                psum[pi][:, q0:TT], lhsT=kbuf[kbi][base:base + 64, kb * 128:(kb + 1) * 128],
                                rhs=qdT[base:base + 64, h, q0:TT], start=True, stop=True),
                                reads=[kk, "qdT"], writes=[f"ps{pi}"])
                        for comp in range(2):
                            pi = pis[comp]
                            pb_ = ptc[0] % NPT
                            ptc[0] += 1
                            ids.append(pb_)
                            sc.op("act", lambda: nc.scalar.activation(
                                out=pT[pb_][:, q0:TT], in_=psum[pi][:, q0:TT], func=AF.Exp, scale=0.125,
                                bias=ccol(bcol)), reads=[f"ps{pi}", "cst"], writes=[f"pT{pb_}"])
                            if diag:
                                sc.op("pool", lambda: nc.gpsimd.tensor_tensor(
                                    out=pT[pb_][:, q0:q0 + 128], in0=pT[pb_][:, q0:q0 + 128], in1=tri,
                                    op=ALU.mult), reads=[f"pT{pb_}", "cmat"], writes=[f"pT{pb_}"])
                            a_ = aset + comp
                            eng = "dve" if comp == 0 else "pool"
                            E_ = nc.vector if comp == 0 else nc.gpsimd
                            if pos == 0 and kb == 0:
                                sc.op(eng, lambda: E_.tensor_copy(out=acc[a_][:], in_=pT[pb_][:]),
                                      reads=[f"pT{pb_}"], writes=[f"acc{a_}"])
                            else:
                                sc.op(eng, lambda: E_.tensor_tensor(out=acc[a_][:, q0:TT], in0=acc[a_][:, q0:TT],
                                                                    in1=pT[pb_][:, q0:TT], op=ALU.add),
                                      reads=[f"pT{pb_}", f"acc{a_}"], writes=[f"acc{a_}"])
                        return (pos, kb, q0, kbi, ids)

                    def back_d(item):
                        pos, kb, q0, kbi, ids = item
                        vk = f"vbuf{kbi}"
                        st = (pos == 0 and kb == 0)
                        last = (pos == npos - 1 and kb == 3)
                        for comp in range(2):
                            pb_ = ids[comp]
                            sc.op("pe", lambda: nc.tensor.matmul(
                                psum[comp][:, q0:TT], lhsT=vbuf[kbi][:, kb, :], rhs=pT[pb_][:, q0:TT],
                                start=st, stop=last), reads=[vk, f"pT{pb_}"], writes=[f"ps{comp}"])

                    attn_pipeline([(pos, kb) for pos in range(npos) for kb in range(4)], front_d, back_d)
                    recip_from_acc([aset, aset + 1], 128, [0, 1])
                    for comp in range(2):
                        sc.op("dve", lambda comp=comp: nc.vector.tensor_tensor(
                            out=tf[2 + comp][:], in0=psum[comp][:], in1=tf[comp][:], op=ALU.mult),
                            reads=[f"ps{comp}", f"tf{comp}"], writes=[f"tf{2 + comp}"])
                    sc.op("dve", lambda: nc.vector.scalar_tensor_tensor(
                        out=tf[4][:], in0=tf[3][:], scalar=lamc[:, 0:1], in1=tf[2][:], op0=ALU.mult, op1=ALU.add),
                        reads=["tf3", "tf2", "lamc"], writes=["tf4"])
                    sc.op("act", lambda: nc.scalar.activation(out=sqo[:], in_=tf[4][:], func=AF.Square),
                          reads=["tf4"], writes=["sqo"])
                    pi = ps_next(2, 8)
                    sc.op("pe", lambda: nc.tensor.matmul(psum[pi][:], lhsT=ones, rhs=sqo[:], start=True, stop=True),
                          reads=["sqo", "cmat"], writes=[f"ps{pi}"])
                    sc.op("act", lambda: nc.scalar.activation(out=tf[5][:], in_=psum[pi][:], func=AF.Ln,
                                                              scale=1.0 / 128, bias=ccol(C_EPSR)),
                          reads=[f"ps{pi}", "cst"], writes=["tf5"])
                    sc.op("act", lambda: nc.scalar.activation(out=tf[5][:], in_=tf[5][:], func=AF.Exp, scale=-0.5),
                          reads=["tf5"], writes=["tf5"])
                    sc.op("pool", lambda: nc.gpsimd.tensor_tensor(out=tf[5][:], in0=tf[5][:], in1=gdT[:, h, :],
                                                                  op=ALU.mult),
                          reads=["tf5", "gdT"], writes=["tf5"])
                    sc.op("dve", lambda: nc.vector.scalar_tensor_tensor(
                        out=ydT[:, h, :], in0=tf[4][:], scalar=lamc[:, 1:2], in1=tf[5][:],
                        op0=ALU.mult, op1=ALU.mult), reads=["tf4", "tf5", "lamc"], writes=["ydT"])

                for dc in range(8):
                    pi = ps_next(2, 8)
                    for h in range(8):
                        sc.op("pe", lambda h=h: nc.tensor.matmul(
                            psum[pi][:], lhsT=woa[:, h, dc * 128:(dc + 1) * 128], rhs=yaT[:, h, :],
                            start=(h == 0), stop=False), reads=["woa", "yaT"], writes=[f"ps{pi}"])
                    for h in range(4):
                        sc.op("pe", lambda h=h: nc.tensor.matmul(
                            psum[pi][:], lhsT=wod[:, h, dc * 128:(dc + 1) * 128], rhs=ydT[:, h, :],
                            start=False, stop=(h == 3)), reads=["wod", "ydT"], writes=[f"ps{pi}"])
                    sc.op("dve", lambda: nc.vector.tensor_tensor(out=x1[:, dc, :], in0=psum[pi][:], in1=x1[:, dc, :],
                                                                 op=ALU.add),
                          reads=[f"ps{pi}", "x1"], writes=["x1"])
                sc.barrier()
            if stop == "p2":
                return nc

            with ExitStack() as p3:
                sq = sb("sq1", [128, 8, TT], BF16, p3)
                h1 = sb("h1", [128, 8, TT], BF16, p3)
                rs = sb("rs1", [128, TT], F32, p3)
                ssm1 = sb("ssm1", [128, TT], BF16, p3)
                wst = [sb(f"w1st{i}", [128, 4096], BF16, p3) for i in range(2)]
                uT = sb("uT", [128, 16, TT], BF16, p3)
                gsT = sb("gsT", [128, 16, TT], BF16, p3)
                gv = sb("gv", [128, 4, 2048], BF16, p3)
                vtmp = sb("vtmp", [128, 2048], F32, p3)
                lng = sb("lng", [128, 2, 2048], F32, p3)
                wsf = sb("wsf", [128, 8, 128], BF16, p3)
                bsb = sb("bsb_sb", [128, 8, 128], F32, p3)
                stats = sb("stats", [128, 4, 6], F32, p3)
                mv = sb("mv", [128, 4], F32, p3)
                tf = [sb(f"tg{i}", [128, TT], F32, p3) for i in range(3)]
                wsb = [0]

                def load_w1(src_ap, shape3):
                    b = wsb[0] % 2
                    wsb[0] += 1
                    a, bb = shape3
                    view = wst[b][:, 0:a * bb].rearrange("p (a b) -> p a b", b=bb)
                    sc.dma("sp", f"w1st{b}", view, src_ap, writes=[f"w1st{b}"])
                    return view, f"w1st{b}"

                sc.dma("sp", "lng", lng[:, 0, :], lngb_d[0:1, :].partition_broadcast(128), writes=["lng"])
                sc.dma("sp", "lng", lng[:, 1, :], lngb_d[1:2, :].partition_broadcast(128), writes=["lng"])
                sc.dma("pool", "wsf", wsf[:].rearrange("p g t -> p (g t)"), wsT_d, writes=["wsf"])
                sc.dma("sp", "bsb", bsb[:].rearrange("p g t -> p (g t)"), bsb_d.partition_broadcast(128),
                       writes=["bsb"])
                for g in range(8):
                    sc.op("pool", lambda g=g: nc.gpsimd.tensor_tensor(out=wsf[:, g, :], in0=wsf[:, g, :], in1=tri,
                                                                      op=ALU.mult),
                          reads=["wsf", "cmat"], writes=["wsf"])

                rmsnorm_tile(x1, "x1", C_SGUN, sq, "sq1", h1, "h1", rs, "rs1", ssm1, "ssm1")
                for wc in range(4):
                    w, wk = load_w1(W1v[:, :, 2048 + wc * 512:2048 + (wc + 1) * 512], (8, 512))
                    for sub in range(4):
                        pi = ps_next()
                        for k in range(8):
                            sc.op("pe", lambda k=k: nc.tensor.matmul(
                                psum[pi][:], lhsT=h1[:, k, sub * 128:(sub + 1) * 128], rhs=w[:, k, :],
                                start=(k == 0), stop=(k == 7)), reads=["h1", wk], writes=[f"ps{pi}"])
                        sc.op("act", lambda: nc.scalar.activation(out=gv[:, sub, wc * 512:(wc + 1) * 512],
                                                                  in_=psum[pi][:], func=AF.Gelu),
                              reads=[f"ps{pi}"], writes=[f"gv{sub}"])
                for sub in range(4):
                    for q in range(4):
                        sc.op("dve", lambda q=q: nc.vector.bn_stats(out=stats[:, q, :],
                                                                    in_=gv[:, sub, q * 512:(q + 1) * 512]),
                              reads=[f"gv{sub}"], writes=["stats"])
                    sc.op("dve", lambda: nc.vector.bn_aggr(out=mv[:, 0:2], in_=stats[:].rearrange("p a b -> p (a b)")),
                          reads=["stats"], writes=["mv"])
                    sc.op("act", lambda: nc.scalar.activation(out=mv[:, 2:3], in_=mv[:, 1:2], func=AF.Ln,
                                                              bias=ccol(C_EPSL)),
                          reads=["mv", "cst"], writes=["mv"])
                    sc.op("act", lambda: nc.scalar.activation(out=mv[:, 3:4], in_=mv[:, 2:3], func=AF.Exp, scale=-0.5),
                          reads=["mv"], writes=["mv"])
                    sc.op("dve", lambda: nc.vector.tensor_scalar(
                        out=vtmp[:], in0=gv[:, sub, :], scalar1=mv[:, 0:1], scalar2=mv[:, 3:4],
                        op0=ALU.subtract, op1=ALU.mult), reads=[f"gv{sub}", "mv"], writes=["vtmp"])
                    sc.op("pool", lambda: nc.gpsimd.tensor_tensor(out=vtmp[:], in0=vtmp[:], in1=lng[:, 0, :],
                                                                  op=ALU.mult),
                          reads=["vtmp", "lng"], writes=["vtmp"])
                    sc.op("pool", lambda: nc.gpsimd.tensor_tensor(out=gv[:, sub, :], in0=vtmp[:], in1=lng[:, 1, :],
                                                                  op=ALU.add),
                          reads=["vtmp", "lng"], writes=[f"gv{sub}"])
                for wc in range(4):
                    w, wk = load_w1(W1v[:, :, wc * 512:(wc + 1) * 512], (8, 512))
                    for s4 in range(4):
                        pi = ps_next()
                        proj_fm(pi, h1, "h1", w, wk, s4 * 128)
                        c = wc * 4 + s4
                        sc.op("act", lambda c=c: nc.scalar.activation(out=uT[:, c, :], in_=psum[pi][:], func=AF.Gelu),
                              reads=[f"ps{pi}"], writes=["uT"])
                for wc in range(4):
                    w, wk = load_w1(W1v[:, :, 4096 + wc * 512:4096 + (wc + 1) * 512], (8, 512))
                    for s4 in range(4):
                        pi = ps_next()
                        proj_fm(pi, h1, "h1", w, wk, s4 * 128)
                        c = wc * 4 + s4
                        sc.op("act", lambda c=c: nc.scalar.activation(out=gsT[:, c, :], in_=psum[pi][:], func=AF.Silu),
                              reads=[f"ps{pi}"], writes=["gsT"])
                for c in range(16):
                    g = c // 2
                    pi = ps_next()
                    for sub in range(4):
                        sc.op("pe", lambda sub=sub: nc.tensor.matmul(
                            psum[pi][:, sub * 128:(sub + 1) * 128], lhsT=gv[:, sub, c * 128:(c + 1) * 128],
                            rhs=wsf[:, g, :], start=True, stop=True),
                            reads=[f"gv{sub}", "wsf"], writes=[f"ps{pi}"])
                    tb = c % 3
                    for sub in range(4):
                        sc.op("dve", lambda sub=sub: nc.vector.tensor_tensor(
                            out=tf[tb][:, sub * 128:(sub + 1) * 128], in0=psum[pi][:, sub * 128:(sub + 1) * 128],
                            in1=bsb[:, g, :], op=ALU.add), reads=[f"ps{pi}", "bsb"], writes=[f"tg{tb}"])
                    sc.op("pool", lambda: nc.gpsimd.tensor_tensor(out=gsT[:, c, :], in0=gsT[:, c, :], in1=uT[:, c, :],
                                                                  op=ALU.mult),
                          reads=["gsT", "uT"], writes=["gsT"])
                    sc.op("dve", lambda: nc.vector.tensor_tensor(out=uT[:, c, :], in0=tf[tb][:], in1=gsT[:, c, :],
                                                                 op=ALU.mult),
                          reads=[f"tg{tb}", "gsT"], writes=["uT"])
                for q4 in range(4):
                    w, wk = load_w1(Wo1v[:, :, q4 * 256:(q4 + 1) * 256], (16, 256))
                    for d2 in range(2):
                        dc = q4 * 2 + d2
                        pi = ps_next()
                        for c in range(16):
                            sc.op("pe", lambda c=c: nc.tensor.matmul(
                                psum[pi][:], lhsT=w[:, c, d2 * 128:(d2 + 1) * 128], rhs=uT[:, c, :],
                                start=(c == 0), stop=(c == 15)), reads=[wk, "uT"], writes=[f"ps{pi}"])
                        tb = dc % 3
                        sc.op("dve", lambda: nc.vector.tensor_tensor(out=tf[tb][:], in0=psum[pi][:], in1=x1[:, dc, :],
                                                                     op=ALU.add),
                              reads=[f"ps{pi}", "x1"], writes=[f"tg{tb}"])
                        sc.dma("sp", f"tg{tb}", out_d[dc * 128:(dc + 1) * 128, j * TT:(j + 1) * TT], tf[tb][:],
                               reads=[f"tg{tb}"], writes=["out"])
                sc.barrier()
            if stop == "p3":
                return nc
    return nc


def _tile_order(c):
    order = []
    for j in range(NSLOT):
        order += [8 * j + i for i in range(c + 1, 8)] + [8 * j + i for i in range(0, c)] + [8 * j + c]
    return order


def _rope_tables(positions):
    rot = 16
    inv = 1.0 / (500000.0 ** (np.arange(0, rot, 2, dtype=np.float32) / rot))
    ang = positions.astype(np.float32)[:, None] * inv[None, :].astype(np.float32)
    cos, sin = np.cos(ang).astype(np.float32), np.sin(ang).astype(np.float32)
    T = positions.shape[0]
    cf = np.ones((128, T), np.float32)
    sf = np.zeros((128, T), np.float32)
    for hb in (0, 64):
        cf[hb:hb + 8] = cos.T
        cf[hb + 8:hb + 16] = cos.T
        sf[hb:hb + 8] = sin.T
        sf[hb + 8:hb + 16] = sin.T
    return np.stack([cf, sf], 0)


def _const_mats():
    cm = np.zeros((128, NCM), np.float32)
    cm[:, M_ONES:M_ONES + 128] = 1.0
    cm[0:64, M_BONES:M_BONES + 64] = 1.0
    cm[64:128, M_BONES + 64:M_BONES + 128] = 1.0
    for m in range(128):
        dl = m % 64
        if dl < 8:
            cm[m + 8, M_ROT + m] = -1.0
        elif dl < 16:
            cm[m - 8, M_ROT + m] = 1.0
    kk = np.arange(128)[:, None]
    qq = np.arange(128)[None, :]
    cm[:, M_TRI:M_TRI + 128] = (kk <= qq).astype(np.float32)
    for idx in range(23):
        delta = idx - 3
        d = 128 * delta + qq - kk
        m = ((d >= 0) & (d <= 128)).astype(np.float32)
        m += ((d % 4 == 0) & (d >= 0) & (d <= 512)).astype(np.float32)
        m += ((d % 16 == 0) & (d >= 0) & (d <= 2048)).astype(np.float32)
        cm[:, M_DSW + idx * 128:M_DSW + (idx + 1) * 128] = m
    return cm.astype(ml_dtypes.bfloat16)


_NC_CACHE = {}


def _prep(x, att_norm, att_w_in, dsw_q_norm, dsw_k_norm, diff_q_norm, diff_k_norm,
           diff_lam_q1, diff_lam_k1, diff_lam_q2, diff_lam_k2, diff_subln, att_w_out,
           sgu_norm, sgu_w_in, sgu_ln_g, sgu_ln_b, sgu_w_s, sgu_b_s, sgu_w_out):
    f = np.float32
    xT = np.ascontiguousarray(np.asarray(x, f)[0].T)
    xT_tiles = xT.reshape(D, NT, TT)
    cm = _const_mats()
    w_in0 = np.ascontiguousarray(np.asarray(att_w_in, f)[0])
    w_out0 = np.ascontiguousarray(np.asarray(att_w_out, f)[0])
    w_in1 = np.ascontiguousarray(np.asarray(sgu_w_in, f)[0])
    w_out1 = np.ascontiguousarray(np.asarray(sgu_w_out, f)[0])
    lngb = np.ascontiguousarray(np.stack([np.asarray(sgu_ln_g, f)[0], np.asarray(sgu_ln_b, f)[0]], 0))
    wsT = np.ascontiguousarray(np.asarray(sgu_w_s, f)[0].transpose(2, 0, 1).reshape(128, 8 * 128))
    bsb = np.ascontiguousarray(np.asarray(sgu_b_s, f)[0].reshape(1, 8 * 128))
    in_maps = []
    orders = []
    for c in range(NCORE):
        order = _tile_order(c)
        orders.append(order)
        own_tiles = [8 * j + c for j in range(NSLOT)]
        x_all = np.ascontiguousarray(xT_tiles[:, own_tiles, :].reshape(D, NSLOT * TT))
        pos_all = (np.asarray(own_tiles)[:, None] * TT + np.arange(TT)[None, :]).reshape(-1)
        halo_tiles = []
        for j in range(NSLOT):
            halo_tiles += [8 * j + c - NHALO + m for m in range(NHALO)]
        x_halo = np.zeros((D, NSLOT * NHALO, TT), f)
        pos_halo = np.zeros((NSLOT * NHALO, TT), np.int64)
        for i, t in enumerate(halo_tiles):
            if t >= 0:
                x_halo[:, i, :] = xT_tiles[:, t, :]
                pos_halo[i] = t * TT + np.arange(TT)
        cst = np.zeros((128, NCST), f)
        cst[:, C_ATTN:C_ATTN + 8] = np.asarray(att_norm, f)[0].reshape(8, 128).T
        cst[:, C_SGUN:C_SGUN + 8] = np.asarray(sgu_norm, f)[0].reshape(8, 128).T
        cst[:, C_DSWQ] = np.tile(np.asarray(dsw_q_norm, f)[0], 2)
        cst[:, C_DSWK] = np.tile(np.asarray(dsw_k_norm, f)[0], 2)
        cst[:, C_DIFQ] = np.tile(np.asarray(diff_q_norm, f)[0], 2)
        cst[:, C_DIFK] = np.tile(np.asarray(diff_k_norm, f)[0], 2)
        cst[:, C_SUBLN] = np.asarray(diff_subln, f)[0]
        for i in range(8):
            cst[:, C_CB + i] = 0.0 if (i < c) else NEG
        for j in range(NSLOT):
            for m in range(NHALO):
                cst[:, C_HB + 4 * j + m] = 0.0 if (8 * j + c - NHALO + m) >= 0 else NEG
        cst[:, C_LQ1:C_LQ1 + 64] = np.asarray(diff_lam_q1, f)[0][None, :]
        cst[:, C_LK1:C_LK1 + 64] = np.asarray(diff_lam_k1, f)[0][None, :]
        cst[:, C_LQ2:C_LQ2 + 64] = np.asarray(diff_lam_q2, f)[0][None, :]
        cst[:, C_LK2:C_LK2 + 64] = np.asarray(diff_lam_k2, f)[0][None, :]
        cst[:, C_EPSR] = RMS_EPS
        cst[:, C_EPSL] = LN_EPS
        cst[:, C_ONE] = 1.0
        for m in range(64):
            cst[m + 64, C_SHF + m] = 1.0
        in_maps.append({
            "x_own": x_all,
            "x_halo": np.ascontiguousarray(x_halo.reshape(D, NSLOT * NHALO * TT)),
            "rope_own": _rope_tables(pos_all),
            "rope_halo": _rope_tables(pos_halo.reshape(-1)),
            "w_in0": w_in0, "w_out0": w_out0, "w_in1": w_in1, "w_out1": w_out1,
            "cst": cst, "cmat": cm, "lngb": lngb, "wsT": wsT, "bsb": bsb,
        })
    return in_maps


def kernel(**inputs):
    f = np.float32
    in_maps = _prep(**inputs)
    if "nc" not in _NC_CACHE:
        _NC_CACHE["nc"] = build_program()
    nc = _NC_CACHE["nc"]
    res = run_bass_kernel_spmd(nc, in_maps, core_ids=list(range(NCORE)))
    out = np.zeros((S, D), f)
    for c in range(NCORE):
        oT = np.asarray(res.results[c]["outT"], f)
        for j in range(NSLOT):
            t = 8 * j + c
            out[t * TT:(t + 1) * TT, :] = oT[:, j * TT:(j + 1) * TT].T
    return out[None]
```
